# Optimizing a Trainium2 kernel written in Bass

```python
import math
import jax
import jax.numpy as jnp
from jax import lax
import numpy as np

D_MODEL = 1024
BATCH = 32
SEQ = 256
DEPTH = 2
DEC_BATCH = 4
DEC_SEQ = 1024
PAST_LEN = 512

GRID_W = 64
MIX_GROUP = D_MODEL // 4
NA_DH = 64
NA_HEADS = MIX_GROUP // NA_DH
NA_WIN_R = 8
NA_WIN_C = 16
NA_BAND = 2 * NA_WIN_C
S5_CH = MIX_GROUP
S5_GROUP = 16
S5_GROUPS = S5_CH // S5_GROUP
S5_N = 64
GQ_DH = 64
GQ_HEADS = MIX_GROUP // GQ_DH
GQ_KV = GQ_HEADS // 2
MLA_V = 64
MLA_HEADS = MIX_GROUP // MLA_V
MLA_NOPE = 64
MLA_ROPE = 32
MLA_QK = MLA_NOPE + MLA_ROPE
MLA_QLORA = (3 * D_MODEL) // 16
MLA_KVLORA = D_MODEL // 8
NA_IN = 3 * NA_HEADS * NA_DH
S5_IN = S5_CH
GQ_IN = (GQ_HEADS + 2 * GQ_KV) * GQ_DH
MLA_IN = MLA_QLORA + MLA_KVLORA + MLA_ROPE
D_IN = NA_IN + S5_IN + GQ_IN + MLA_IN
D_CAT = NA_HEADS * NA_DH + S5_CH + GQ_HEADS * GQ_DH + MLA_HEADS * MLA_V
D_FF = 128 * ((8 * D_MODEL // 3 + 127) // 128)
CONV_W = 3
Q_BLOCK = 128
ROPE_BASE = 10000.0
EPS = 1e-6
NEG = -1e30
DT_MIN = 1e-3
DT_MAX = 1e-1

kernel_name = 'hybrid_diffusion_prefix_trunk_step'


def rmsnorm(x, g):
    xf = x.astype(jnp.float32)
    y = xf * lax.rsqrt(jnp.mean(xf * xf, axis=-1, keepdims=True) + EPS)
    return (y * g.astype(jnp.float32)).astype(x.dtype)


def to_heads(x, n, dh):
    b, L, _ = x.shape
    return x.reshape(b, L, n, dh).transpose(0, 2, 1, 3)


def from_heads(x):
    b, h, L, dh = x.shape
    return x.transpose(0, 2, 1, 3).reshape(b, L, h * dh)


def rope_angles_1d(pos, dim):
    half = dim // 2
    inv = ROPE_BASE ** (-jnp.arange(half, dtype=jnp.float32) / half)
    ang = pos.astype(jnp.float32)[:, None] * inv[None, :]
    return jnp.concatenate([ang, ang], axis=-1)


def rope_2d_tables(t, dim):
    pos = jnp.arange(t)
    ang = jnp.concatenate([rope_angles_1d(pos // GRID_W, dim // 2),
                           rope_angles_1d(pos % GRID_W, dim // 2)], axis=-1)
    return jnp.cos(ang), jnp.sin(ang)


def rotate_half(x):
    x1, x2 = jnp.split(x, 2, axis=-1)
    return jnp.concatenate([-x2, x1], axis=-1)


def apply_rope_2d(x, cos, sin):
    h = x.shape[-1] // 2
    rot = jnp.concatenate([rotate_half(x[..., :h]), rotate_half(x[..., h:])], axis=-1)
    return (x * cos + rot * sin).astype(x.dtype)


def rope_tail(x, cos, sin, r):
    return jnp.concatenate([x[..., :-r], apply_rope_2d(x[..., -r:], cos, sin)], axis=-1)


def attend(q, k, v, scale):
    b, hq, t, dk = q.shape
    hk, dv = k.shape[1], v.shape[-1]
    rep = hq // hk
    qb = Q_BLOCK if t % Q_BLOCK == 0 else t
    qg = jnp.moveaxis(q.reshape(b, hk, rep, t // qb, qb, dk), 3, 0)

    def block(qi):
        s = jnp.einsum('bgrqd,bgkd->bgrqk', qi, k).astype(jnp.float32) * scale
        pr = jax.nn.softmax(s, axis=-1).astype(v.dtype)
        return jnp.einsum('bgrqk,bgkd->bgrqd', pr, v)

    o = lax.map(block, qg)
    return jnp.moveaxis(o, 0, 3).reshape(b, hq, t, dv)


def na_tables(rows, rpb):
    wr = min(NA_WIN_R, rows)
    ncb = GRID_W // NA_WIN_C
    r = np.arange(rows)
    row_idx = np.clip(r - wr // 2, 0, rows - wr)[:, None] + np.arange(wr)[None, :]
    band_start = np.clip(np.arange(ncb) * NA_WIN_C - NA_WIN_C // 2, 0, GRID_W - NA_BAND)
    col_idx = band_start[:, None] + np.arange(NA_BAND)[None, :]
    qcol = np.arange(GRID_W).reshape(ncb, NA_WIN_C)
    col_start = np.clip(qcol - NA_WIN_C // 2, 0, GRID_W - NA_WIN_C)
    kcol = col_idx[:, None, :]
    valid = (kcol >= col_start[..., None]) & (kcol < col_start[..., None] + NA_WIN_C)
    d_r = row_idx - r[:, None] + NA_WIN_R - 1
    d_c = np.clip(kcol - qcol[..., None], 1 - NA_WIN_C, NA_WIN_C - 1) + NA_WIN_C - 1
    bias = rpb[:, d_r[:, None, None, :, None], d_c[None, :, :, None, :]].astype(jnp.float32)
    bias = jnp.where(jnp.asarray(valid)[None, None, :, :, None, :], bias, NEG)
    return row_idx, col_idx, bias.reshape(rpb.shape[0], rows, ncb, NA_WIN_C, wr * NA_BAND)


def na_latent(q, k, v, k_ctx, v_ctx, rpb):
    b, h, t, dh = q.shape
    rows = t // GRID_W
    wr = min(NA_WIN_R, rows)
    ncb = GRID_W // NA_WIN_C
    row_idx, col_idx, bias = na_tables(rows, rpb)

    def gather(x):
        xg = jnp.take(x.reshape(b, h, rows, GRID_W, dh), jnp.asarray(row_idx), axis=2)
        xg = jnp.take(xg, jnp.asarray(col_idx), axis=4)
        return jnp.swapaxes(xg, 3, 4).reshape(b, h, rows, ncb, wr * NA_BAND, dh)

    kg, vg = gather(k), gather(v)
    qg = q.reshape(b, h, rows, ncb, NA_WIN_C, dh)
    scale = NA_DH ** -0.5
    s_loc = jnp.einsum('bhrnqd,bhrnkd->bhrnqk', qg, kg).astype(jnp.float32) * scale + bias[None]
    s_ctx = jnp.einsum('bhrnqd,bhpd->bhrnqp', qg, k_ctx).astype(jnp.float32) * scale
    pr = jax.nn.softmax(jnp.concatenate([s_loc, s_ctx], axis=-1), axis=-1).astype(v.dtype)
    nl = wr * NA_BAND
    o = (jnp.einsum('bhrnqk,bhrnkd->bhrnqd', pr[..., :nl], vg)
         + jnp.einsum('bhrnqp,bhpd->bhrnqd', pr[..., nl:], v_ctx))
    return o.reshape(b, h, t, dh)


def s5_discretize(lam_re, lam_im, log_dt, b_re, b_im):
    dt = jnp.exp(log_dt.astype(jnp.float32))[:, None]
    lr, li = lam_re.astype(jnp.float32), lam_im.astype(jnp.float32)
    mag = jnp.exp(lr * dt)
    a_re, a_im = mag * jnp.cos(li * dt), mag * jnp.sin(li * dt)
    den = lr * lr + li * li
    f_re = ((a_re - 1.0) * lr + a_im * li) / den
    f_im = (a_im * lr - (a_re - 1.0) * li) / den
    br, bi = b_re.astype(jnp.float32), b_im.astype(jnp.float32)
    bb_re = f_re[..., None] * br - f_im[..., None] * bi
    bb_im = f_re[..., None] * bi + f_im[..., None] * br
    return a_re, a_im, bb_re, bb_im


def complex_scan_op(e1, e2):
    a1r, a1i, b1r, b1i = e1
    a2r, a2i, b2r, b2i = e2
    return (a1r * a2r - a1i * a2i, a1r * a2i + a1i * a2r,
            a2r * b1r - a2i * b1i + b2r, a2r * b1i + a2i * b1r + b2i)


def s5_direction(u, a_re, a_im, bb_re, bb_im, c_re, c_im, h0_re, h0_im, reverse):
    bu_re = jnp.einsum('gnc,blgc->blgn', bb_re, u)
    bu_im = jnp.einsum('gnc,blgc->blgn', bb_im, u)
    first = -1 if reverse else 0
    bu_re = bu_re.at[:, first].add(a_re * h0_re - a_im * h0_im)
    bu_im = bu_im.at[:, first].add(a_re * h0_im + a_im * h0_re)
    ar = jnp.broadcast_to(a_re, bu_re.shape)
    ai = jnp.broadcast_to(a_im, bu_re.shape)
    _, _, h_re, h_im = lax.associative_scan(complex_scan_op, (ar, ai, bu_re, bu_im), axis=1, reverse=reverse)
    y = (jnp.einsum('gcn,blgn->blgc', c_re.astype(jnp.float32), h_re)
         - jnp.einsum('gcn,blgn->blgc', c_im.astype(jnp.float32), h_im))
    last = 0 if reverse else -1
    return y, h_re[:, last], h_im[:, last]


def s5_mixer(u_flat, p, h0):
    b, L, _ = u_flat.shape
    u = u_flat.astype(jnp.float32).reshape(b, L, S5_GROUPS, S5_GROUP)
    h0 = h0.astype(jnp.float32)
    y = p['s5_d'].astype(jnp.float32).reshape(S5_GROUPS, S5_GROUP) * u
    finals = []
    for d, rev in enumerate((False, True)):
        a_re, a_im, bb_re, bb_im = s5_discretize(p['s5_lam_re'][d], p['s5_lam_im'][d], p['s5_log_dt'][d],
                                                 p['s5_b_re'][d], p['s5_b_im'][d])
        y_d, f_re, f_im = s5_direction(u, a_re, a_im, bb_re, bb_im, p['s5_c_re'][d], p['s5_c_im'][d],
                                       h0[:, d, 0], h0[:, d, 1], rev)
        y = y + y_d
        finals.append(jnp.stack([f_re, f_im], axis=1))
    state = jnp.stack(finals, axis=1)
    y = jax.nn.gelu(y.reshape(b, L, S5_CH))
    y = y * jax.nn.sigmoid(y @ p['s5_w_glu'].astype(jnp.float32) + p['s5_b_glu'].astype(jnp.float32))
    return y.astype(u_flat.dtype), state.astype(u_flat.dtype)


def split_proj(z):
    return jnp.split(z, [NA_IN, NA_IN + S5_IN, NA_IN + S5_IN + GQ_IN], axis=-1)


def na_qkv(z, p):
    q, k, v = jnp.split(z, 3, axis=-1)
    return (rmsnorm(to_heads(q, NA_HEADS, NA_DH), p['na_qn']),
            rmsnorm(to_heads(k, NA_HEADS, NA_DH), p['na_kn']),
            to_heads(v, NA_HEADS, NA_DH))


def gq_qkv(z, p):
    q, k, v = jnp.split(z, [GQ_HEADS * GQ_DH, (GQ_HEADS + GQ_KV) * GQ_DH], axis=-1)
    return (rmsnorm(to_heads(q, GQ_HEADS, GQ_DH), p['gq_qn']),
            rmsnorm(to_heads(k, GQ_KV, GQ_DH), p['gq_kn']),
            to_heads(v, GQ_KV, GQ_DH))


def mla_latents(z, p):
    cq, ckv, krope = jnp.split(z, [MLA_QLORA, MLA_QLORA + MLA_KVLORA], axis=-1)
    q = rmsnorm(to_heads(rmsnorm(cq, p['mla_qa_g']) @ p['mla_w_uq'], MLA_HEADS, MLA_QK), p['mla_qn'])
    return q, rmsnorm(ckv, p['mla_kva_g']), krope


def mla_kv(ckv, krope, p):
    b, L, _ = ckv.shape
    kv = to_heads(ckv @ p['mla_w_ukv'], MLA_HEADS, MLA_NOPE + MLA_V)
    kr = jnp.broadcast_to(krope[:, None], (b, MLA_HEADS, L, MLA_ROPE)).astype(kv.dtype)
    k = rmsnorm(jnp.concatenate([kv[..., :MLA_NOPE], kr], axis=-1), p['mla_kn'])
    return k, kv[..., MLA_NOPE:]


def mixers_context(h, p):
    b = h.shape[0]
    na_z, s5_u, gq_z, mla_z = split_proj(h @ p['w_in'])
    nq, nk, nv = na_qkv(na_z, p)
    o_na = attend(nq, nk, nv, NA_DH ** -0.5)
    o_s5, s5_state = s5_mixer(s5_u, p, jnp.zeros((b, 2, 2, S5_GROUPS, S5_N), jnp.float32))
    gq, gk, gv = gq_qkv(gq_z, p)
    o_gq = attend(gq, gk, gv, GQ_DH ** -0.5)
    mq, ckv, krope = mla_latents(mla_z, p)
    mk, mv = mla_kv(ckv, krope, p)
    o_mla = attend(mq, mk, mv, MLA_QK ** -0.5)
    mixed = jnp.concatenate([from_heads(o_na), o_s5, from_heads(o_gq), from_heads(o_mla)], axis=-1)
    return mixed @ p['w_out'], (nk, nv, s5_state, gk, gv, ckv, krope)


def mixers_latent(h, ctx, p):
    na_kc, na_vc, s5_h0, gq_kc, gq_vc, ckv_c, krope_c = ctx
    t = h.shape[1]
    na_z, s5_u, gq_z, mla_z = split_proj(h @ p['w_in'])
    nq, nk, nv = na_qkv(na_z, p)
    o_na = na_latent(nq, nk, nv, na_kc, na_vc, p['na_rpb'])
    o_s5, _ = s5_mixer(s5_u, p, s5_h0)
    cos_g, sin_g = rope_2d_tables(t, GQ_DH)
    gq, gk, gv = gq_qkv(gq_z, p)
    gq = apply_rope_2d(gq, cos_g, sin_g)
    gk = apply_rope_2d(gk, cos_g, sin_g)
    o_gq = attend(gq, jnp.concatenate([gk, gq_kc], axis=2), jnp.concatenate([gv, gq_vc], axis=2), GQ_DH ** -0.5)
    cos_m, sin_m = rope_2d_tables(t, MLA_ROPE)
    mq, ckv, krope = mla_latents(mla_z, p)
    mk, mv = mla_kv(ckv, krope, p)
    mkc, mvc = mla_kv(ckv_c, krope_c, p)
    mq = rope_tail(mq, cos_m, sin_m, MLA_ROPE)
    mk = rope_tail(mk, cos_m, sin_m, MLA_ROPE)
    o_mla = attend(mq, jnp.concatenate([mk, mkc], axis=2), jnp.concatenate([mv, mvc], axis=2), MLA_QK ** -0.5)
    mixed = jnp.concatenate([from_heads(o_na), o_s5, from_heads(o_gq), from_heads(o_mla)], axis=-1)
    return mixed @ p['w_out']


def conv_ffn(h, p):
    u = h @ p['ffn_w_up']
    ch = u.shape[-1]
    rhs = p['ffn_conv_w'][:, None, :].astype(u.dtype)
    u = lax.conv_general_dilated(u, rhs, window_strides=(1,), padding=[(CONV_W // 2, CONV_W // 2)],
                                 dimension_numbers=('NWC', 'WIO', 'NWC'), feature_group_count=ch) + p['ffn_conv_b']
    gate, up = jnp.split(u, 2, axis=-1)
    return (jax.nn.silu(gate) * up) @ p['ffn_w_down']


def ada_mods(cvec, p):
    m = jax.nn.silu(cvec) @ p['ada_w'] + p['ada_b']
    return jnp.split(m[:, None, :], 6, axis=-1)


def trunk_layer(x, cvec, p, mixer_fn):
    sh1, sc1, g1, sh2, sc2, g2 = ada_mods(cvec, p)
    mixed, extra = mixer_fn(rmsnorm(x, p['norm1_g']) * (1 + sc1) + sh1)
    x = x + g1 * mixed
    x = x + g2 * conv_ffn(rmsnorm(x, p['norm2_g']) * (1 + sc2) + sh2, p)
    return x, extra


def setup_inputs(seed: int = 0) -> dict:
    key = jax.random.key(seed)
    ks = iter(jax.random.split(key, 64))

    def nrm(shape, scale):
        return jax.random.normal(next(ks), shape, jnp.float32) * scale

    def gain(shape):
        return 1.0 + nrm(shape, 0.01)

    L, Bd, P = DEPTH, DEC_BATCH, PAST_LEN
    n_idx = jnp.arange(S5_N, dtype=jnp.float32)
    inp = {}
    inp['x_prompt'] = nrm((BATCH, SEQ, D_MODEL), 1.0)
    inp['x_sample'] = nrm((Bd, DEC_SEQ, D_MODEL), 1.0)
    inp['cache_na_k'] = nrm((Bd, L, NA_HEADS, P, NA_DH), 1.0)
    inp['cache_na_v'] = nrm((Bd, L, NA_HEADS, P, NA_DH), 1.0)
    inp['state_s5'] = nrm((Bd, L, 2, 2, S5_GROUPS, S5_N), 0.1)
    inp['cache_gqa_k'] = nrm((Bd, L, GQ_KV, P, GQ_DH), 1.0)
    inp['cache_gqa_v'] = nrm((Bd, L, GQ_KV, P, GQ_DH), 1.0)
    inp['cache_mla_ckv'] = nrm((Bd, L, P, MLA_KVLORA), 1.0)
    inp['cache_mla_krope'] = nrm((Bd, L, P, MLA_ROPE), 1.0)
    inp['c'] = nrm((Bd, D_MODEL), 1.0)
    inp['c_ctx'] = nrm((D_MODEL,), 1.0)
    inp['norm1_g'] = gain((L, D_MODEL))
    inp['norm2_g'] = gain((L, D_MODEL))
    inp['ada_w'] = nrm((L, D_MODEL, 6 * D_MODEL), 0.5 * D_MODEL ** -0.5)
    inp['ada_b'] = nrm((L, 6 * D_MODEL), 0.01)
    inp['w_in'] = nrm((L, D_MODEL, D_IN), D_MODEL ** -0.5)
    inp['na_qn'] = gain((L, NA_DH))
    inp['na_kn'] = gain((L, NA_DH))
    inp['na_rpb'] = nrm((L, NA_HEADS, 2 * NA_WIN_R - 1, 2 * NA_WIN_C - 1), 0.1)
    inp['s5_lam_re'] = -0.5 + nrm((L, 2, S5_GROUPS, S5_N), 0.01)
    inp['s5_lam_im'] = jnp.pi * n_idx + nrm((L, 2, S5_GROUPS, S5_N), 0.01)
    inp['s5_log_dt'] = jax.random.uniform(next(ks), (L, 2, S5_GROUPS), jnp.float32,
                                          math.log(DT_MIN), math.log(DT_MAX))
    inp['s5_b_re'] = nrm((L, 2, S5_GROUPS, S5_N, S5_GROUP), (2 * S5_GROUP) ** -0.5)
    inp['s5_b_im'] = nrm((L, 2, S5_GROUPS, S5_N, S5_GROUP), (2 * S5_GROUP) ** -0.5)
    inp['s5_c_re'] = nrm((L, 2, S5_GROUPS, S5_GROUP, S5_N), S5_N ** -0.5)
    inp['s5_c_im'] = nrm((L, 2, S5_GROUPS, S5_GROUP, S5_N), S5_N ** -0.5)
    inp['s5_d'] = nrm((L, S5_CH), 1.0)
    inp['s5_w_glu'] = nrm((L, S5_CH, S5_CH), S5_CH ** -0.5)
    inp['s5_b_glu'] = nrm((L, S5_CH), 0.01)
    inp['gq_qn'] = gain((L, GQ_DH))
    inp['gq_kn'] = gain((L, GQ_DH))
    inp['mla_qa_g'] = gain((L, MLA_QLORA))
    inp['mla_kva_g'] = gain((L, MLA_KVLORA))
    inp['mla_w_uq'] = nrm((L, MLA_QLORA, MLA_HEADS * MLA_QK), MLA_QLORA ** -0.5)
    inp['mla_w_ukv'] = nrm((L, MLA_KVLORA, MLA_HEADS * (MLA_NOPE + MLA_V)), MLA_KVLORA ** -0.5)
    inp['mla_qn'] = gain((L, MLA_QK))
    inp['mla_kn'] = gain((L, MLA_QK))
    inp['w_out'] = nrm((L, D_CAT, D_MODEL), D_CAT ** -0.5)
    inp['ffn_w_up'] = nrm((L, D_MODEL, 2 * D_FF), D_MODEL ** -0.5)
    inp['ffn_conv_w'] = nrm((L, CONV_W, 2 * D_FF), CONV_W ** -0.5)
    inp['ffn_conv_b'] = nrm((L, 2 * D_FF), 0.01)
    inp['ffn_w_down'] = nrm((L, D_FF, D_MODEL), D_FF ** -0.5)
    return inp


def reference(x_prompt, x_sample, cache_na_k, cache_na_v, state_s5, cache_gqa_k, cache_gqa_v,
              cache_mla_ckv, cache_mla_krope, c, c_ctx, norm1_g, norm2_g, ada_w, ada_b, w_in,
              na_qn, na_kn, na_rpb, s5_lam_re, s5_lam_im, s5_log_dt, s5_b_re, s5_b_im, s5_c_re, s5_c_im,
              s5_d, s5_w_glu, s5_b_glu, gq_qn, gq_kn, mla_qa_g, mla_kva_g, mla_w_uq, mla_w_ukv,
              mla_qn, mla_kn, w_out, ffn_w_up, ffn_conv_w, ffn_conv_b, ffn_w_down):
    stacked = dict(norm1_g=norm1_g, norm2_g=norm2_g, ada_w=ada_w, ada_b=ada_b, w_in=w_in,
                   na_qn=na_qn, na_kn=na_kn, na_rpb=na_rpb, s5_lam_re=s5_lam_re, s5_lam_im=s5_lam_im,
                   s5_log_dt=s5_log_dt, s5_b_re=s5_b_re, s5_b_im=s5_b_im, s5_c_re=s5_c_re, s5_c_im=s5_c_im,
                   s5_d=s5_d, s5_w_glu=s5_w_glu, s5_b_glu=s5_b_glu, gq_qn=gq_qn, gq_kn=gq_kn,
                   mla_qa_g=mla_qa_g, mla_kva_g=mla_kva_g, mla_w_uq=mla_w_uq, mla_w_ukv=mla_w_ukv,
                   mla_qn=mla_qn, mla_kn=mla_kn, w_out=w_out, ffn_w_up=ffn_w_up,
                   ffn_conv_w=ffn_conv_w, ffn_conv_b=ffn_conv_b, ffn_w_down=ffn_w_down)

    y_prompt = x_prompt
    states = []
    for l in range(DEPTH):
        p = {name: w[l] for name, w in stacked.items()}
        y_prompt, st = trunk_layer(y_prompt, c_ctx[None, :], p, lambda hn: mixers_context(hn, p))
        states.append(st)
    new_na_k = jnp.stack([s[0] for s in states], axis=1)
    new_na_v = jnp.stack([s[1] for s in states], axis=1)
    new_s5 = jnp.stack([s[2] for s in states], axis=1)
    new_gqa_k = jnp.stack([s[3] for s in states], axis=1)
    new_gqa_v = jnp.stack([s[4] for s in states], axis=1)
    new_mla_ckv = jnp.stack([s[5] for s in states], axis=1)
    new_mla_krope = jnp.stack([s[6] for s in states], axis=1)

    y_sample = x_sample
    for l in range(DEPTH):
        p = {name: w[l] for name, w in stacked.items()}
        ctx = (cache_na_k[:, l], cache_na_v[:, l], state_s5[:, l], cache_gqa_k[:, l], cache_gqa_v[:, l],
               cache_mla_ckv[:, l], cache_mla_krope[:, l])
        y_sample, _ = trunk_layer(y_sample, c, p, lambda hn: (mixers_latent(hn, ctx, p), None))

    return (y_prompt, y_sample, new_na_k, new_na_v, new_s5, new_gqa_k, new_gqa_v, new_mla_ckv, new_mla_krope)
```

```python
import math
from contextlib import ExitStack
import numpy as np
import ml_dtypes
import concourse.bass as bass
import concourse.mybir as mybir
from concourse.bass_utils import run_bass_kernel_spmd

F32 = mybir.dt.float32
BF16 = mybir.dt.bfloat16
AF = mybir.ActivationFunctionType
ALU = mybir.AluOpType
AX = mybir.AxisListType

SAME_ENGINE_SYNC = True


class T:
    __slots__ = ("ap", "name", "writer", "readers")

    def __init__(self, ap=None, name=""):
        self.ap = ap
        self.name = name
        self.writer = None
        self.readers = []


class Ev:
    __slots__ = ("sem", "val", "clock", "eng")

    def __init__(self, sem, val, clock, eng):
        self.sem = sem
        self.val = val
        self.clock = clock
        self.eng = eng


class EngState:
    def __init__(self, name, handle, sem):
        self.name = name
        self.h = handle
        self.sem = sem
        self.count = 0
        self.seen = {}
        self.ops = []


class K:
    def __init__(self, nc, n_dma_sems=64):
        self.nc = nc
        self.st = ExitStack()
        self.E = {}
        for name, h in (("pe", nc.tensor), ("act", nc.scalar), ("dve", nc.vector),
                        ("pool", nc.gpsimd), ("sp", nc.sync)):
            sem = self.st.enter_context(nc.semaphore("sem_" + name))
            self.E[name] = EngState(name, h, sem)
        self.dma_sems = []
        for i in range(n_dma_sems):
            s = self.st.enter_context(nc.semaphore("dsem%d" % i))
            self.dma_sems.append([s, 0, None])
        self.dma_rr = 0
        self.n_ops = 0
        self.final_events = []

    def sbuf(self, name, shape, dtype):
        return self.st.enter_context(self.nc.sbuf_tensor("sb_" + name, list(shape), dtype))

    def psum(self, name, shape, dtype=F32):
        return self.st.enter_context(self.nc.psum_tensor(name, list(shape), dtype))

    def _collect(self, es, reads, writes, skip_same):
        need = []
        for t in reads:
            if t.writer is not None:
                need.append(t.writer)
        for t in writes:
            if t.writer is not None:
                need.append(t.writer)
            need.extend(t.readers)
        waits = {}
        for ev in need:
            if skip_same and ev.eng is es:
                continue
            kk = id(ev.sem)
            if es.seen.get(kk, 0) >= ev.val:
                continue
            if kk not in waits or waits[kk][1] < ev.val:
                waits[kk] = (ev.sem, ev.val)
        for ev in need:
            for kk, v in ev.clock.items():
                if es.seen.get(kk, 0) < v:
                    es.seen[kk] = v
        return list(waits.values())

    def run_lanes(self, lanes):
        self.lane = None
        idx = [0] * len(lanes)
        left = sum(len(l_) for l_ in lanes)
        while left:
            for li, l_ in enumerate(lanes):
                if idx[li] < len(l_):
                    kind, a, kw = l_[idx[li]]
                    idx[li] += 1
                    left -= 1
                    if kind == "op":
                        self.op(*a)
                    else:
                        self.dma(*a, **kw)

    def op(self, eng, fn, reads=(), writes=()):
        if getattr(self, "lane", None) is not None:
            self.lane.append(("op", (eng, fn, list(reads), list(writes)), {}))
            return None
        es = self.E[eng]
        waits = self._collect(es, reads, writes, (eng == "pe") or (not SAME_ENGINE_SYNC))
        es.count += 1
        clock = dict(es.seen)
        clock[id(es.sem)] = es.count
        ev = Ev(es.sem, es.count, clock, es)
        es.ops.append((waits, fn, (es.sem, 1)))
        for t in reads:
            t.readers.append(ev)
        for t in writes:
            t.writer = ev
            t.readers = []
        self.n_ops += 1
        return ev

    def dma(self, eng, out, in_, reads=(), writes=(), final=False, **kw):
        if getattr(self, "lane", None) is not None:
            kw2 = dict(kw)
            kw2["final"] = final
            self.lane.append(("dma", (eng, out, in_, list(reads), list(writes)), kw2))
            return None
        es = self.E[eng]
        slot = self.dma_sems[self.dma_rr]
        self.dma_rr = (self.dma_rr + 1) % len(self.dma_sems)
        sem, cum, last = slot
        waits = self._collect(es, reads, writes, False)
        if last is not None and es.seen.get(id(sem), 0) < last.val:
            waits.append((sem, last.val))
            es.seen[id(sem)] = last.val
        cum += 16
        clock = dict(es.seen)
        clock[id(sem)] = cum
        ev = Ev(sem, cum, clock, None)
        slot[1] = cum
        slot[2] = ev

        def fn(h, out=out, in_=in_, kw=kw):
            return h.dma_start(out=out, in_=in_, **kw)
        es.ops.append((waits, fn, (sem, 16)))
        for t in reads:
            t.readers.append(ev)
        for t in writes:
            t.writer = ev
            t.readers = []
        if final:
            self.final_events.append(ev)
        self.n_ops += 1
        return ev

    def fence(self, old_tiles, new_tiles):
        evs = []
        for t in old_tiles:
            if t.writer is not None:
                evs.append(t.writer)
            evs.extend(t.readers)
        for t in new_tiles:
            t.readers = list(t.readers) + evs

    def mm(self, out, lhsT, rhs, start, stop, reads, writes, **kw):
        return self.op("pe", lambda h: h.matmul(out, lhsT, rhs, start=start, stop=stop, **kw), reads, writes)

    def tr(self, out, in_, ident, reads, writes):
        return self.op("pe", lambda h: h.transpose(out, in_, ident), reads, writes)

    def act(self, out, in_, func, reads, writes, **kw):
        return self.op("act", lambda h: h.activation(out=out, in_=in_, func=func, **kw), reads, writes)

    def tt(self, eng, out, in0, in1, op, reads, writes):
        return self.op(eng, lambda h: h.tensor_tensor(out=out, in0=in0, in1=in1, op=op), reads, writes)

    def ts(self, eng, out, in0, s1, s2, op0, op1, reads, writes):
        return self.op(eng, lambda h: h.tensor_scalar(out=out, in0=in0, scalar1=s1, scalar2=s2, op0=op0, op1=op1), reads, writes)

    def stt(self, out, in0, scalar, in1, op0, op1, reads, writes):
        return self.op("dve", lambda h: h.scalar_tensor_tensor(out=out, in0=in0, scalar=scalar, in1=in1, op0=op0, op1=op1), reads, writes)

    def copy(self, eng, out, in_, reads, writes):
        if eng == "act":
            return self.act(out, in_, AF.Copy, reads, writes)
        return self.op(eng, lambda h: h.tensor_copy(out=out, in_=in_), reads, writes)

    def memset(self, eng, ap, val, writes):
        return self.op(eng, lambda h: h.memset(ap, val), (), writes)

    def emit(self):
        nc = self.nc
        fin = []
        for ev in self.final_events:
            fin.append((ev.sem, ev.val))
        for name, es in self.E.items():
            if name != "sp" and es.count > 0:
                fin.append((es.sem, es.count))
        for slot in self.dma_sems:
            if slot[2] is not None:
                fin.append((slot[0], slot[1]))
        with nc.Block() as block:
            def replay(es, h, extra=()):
                for waits, fn, inc in es.ops:
                    for (s, v) in waits:
                        h.wait_ge(s, v)
                    ins = fn(h)
                    ins.then_inc(inc[0], inc[1])
                for (s, v) in extra:
                    h.wait_ge(s, v)

            @block.tensor
            def _(h):
                replay(self.E["pe"], h)

            @block.scalar
            def _(h):
                replay(self.E["act"], h)

            @block.vector
            def _(h):
                replay(self.E["dve"], h)

            @block.gpsimd
            def _(h):
                replay(self.E["pool"], h)

            @block.sync
            def _(h):
                replay(self.E["sp"], h, extra=fin)

    def close(self):
        self.st.close()
D = 1024
NCORES = 8
DEPTH = 2
NT = 1024
P_LEN = 512
D_IN = 1888
D_FF = 2816
NFT = D_FF // 128
EPS = 1e-6
TWO_PI = 2.0 * math.pi
G_OFF = dict(na_q=0, na_k=64, gq_q=128, gq_k=192, qa=256, kva=448, mq=576, mk=672)
NG = 768


def _pc(v, c):
    return np.ascontiguousarray(np.asarray(v, np.float32).reshape(c, 128).T)


def _rope_tables(t, dim):
    def ang1(pos, d):
        half = d // 2
        inv = (np.float32(10000.0) ** (-np.arange(half, dtype=np.float32) / np.float32(half))).astype(np.float32)
        a = pos.astype(np.float32)[:, None] * inv[None, :]
        return np.concatenate([a, a], axis=-1)
    pos = np.arange(t)
    ang = np.concatenate([ang1(pos // 64, dim // 2), ang1(pos % 64, dim // 2)], axis=-1).astype(np.float32)
    cos = np.cos(ang).astype(np.float32)
    sin = np.sin(ang).astype(np.float32)
    q = dim // 4
    sgn = np.tile(np.concatenate([-np.ones(q), np.ones(q)]), 2).astype(np.float32)
    sinS = sin * sgn[None, :]
    def tm(a):
        return np.ascontiguousarray(a.reshape(8, 128, dim).transpose(1, 0, 2))
    return tm(cos), tm(sinS)


def _constants():
    c = {}
    c["ident"] = np.eye(128, dtype=np.float32)
    sel = np.zeros((128, 2, 8, 128), np.float32)
    for p in range(128):
        for par in range(2):
            r = p % 32 - 16 * par
            if 0 <= r < 16:
                for j in range(8):
                    sel[p, par, j, 16 * j + r] = 1.0
    c["sel32"] = sel
    selT = np.zeros((128, 2, 8, 32), np.float32)
    for jj in range(8):
        for co in range(16):
            for par in range(2):
                selT[jj * 16 + co, par, jj, 16 * par + co] = 1.0
    c["selT32"] = selT
    s_idx = np.arange(128) // 16
    mL = (s_idx[:, None] <= s_idx[None, :]).astype(np.float32)
    mU = (s_idx[:, None] >= s_idx[None, :]).astype(np.float32)
    c["masks"] = np.ascontiguousarray(np.stack([mL, mU], axis=1))
    qc = 63 - (np.arange(128) % 64)
    cs = np.clip(qc - 8, 0, 48)
    kc = np.arange(64)
    valid = ((kc[None, :] >= cs[:, None]) & (kc[None, :] < cs[:, None] + 16)).astype(np.float32)
    c["na_vmask"] = np.ascontiguousarray(np.stack([valid, (1.0 - valid) * np.float32(-30000.0)], axis=1))
    aid = np.zeros((128, 128), np.float32)
    for p_ in range(128):
        aid[p_, (p_ // 64) * 64 + 63 - (p_ % 64)] = 1.0
    c["antiid"] = aid
    c["pvals"] = np.ascontiguousarray(np.broadcast_to(np.arange(-8, 9, dtype=np.float32)[None, :], (128, 17)))
    c["posv"] = np.ascontiguousarray(np.broadcast_to(np.arange(0, 129, dtype=np.float32)[None, :], (128, 129)))
    cg, sg = _rope_tables(1024, 64)
    cm, sm = _rope_tables(1024, 32)
    c["rope_g"] = np.ascontiguousarray(np.stack([cg, sg], axis=1))
    c["rope_m"] = np.ascontiguousarray(np.stack([cm, sm], axis=1))
    return c


def _common_inputs(inp):
    f = lambda a: np.ascontiguousarray(np.asarray(a, np.float32))
    c = _constants()
    c["ada_w"] = f(inp["ada_w"])
    c["ada_b"] = np.stack([_pc(inp["ada_b"][l], 48) for l in range(DEPTH)])
    c["normg"] = np.stack([np.stack([_pc(inp["norm1_g"][l], 8), _pc(inp["norm2_g"][l], 8)], axis=1) for l in range(DEPTH)])
    c["w_in"] = f(inp["w_in"])
    c["w_out"] = f(inp["w_out"])
    c["w_up"] = f(inp["ffn_w_up"])
    c["w_down"] = f(inp["ffn_w_down"])
    cw = np.asarray(inp["ffn_conv_w"], np.float32)
    c["convw"] = np.ascontiguousarray(cw.reshape(DEPTH, 3, 44, 128).transpose(0, 3, 2, 1))
    c["convb"] = np.ascontiguousarray(np.asarray(inp["ffn_conv_b"], np.float32).reshape(DEPTH, 44, 128).transpose(0, 2, 1))
    c["gains"] = np.ascontiguousarray(np.concatenate(
        [np.asarray(inp[n], np.float32) for n in ("na_qn", "na_kn", "gq_qn", "gq_kn", "mla_qa_g", "mla_kva_g", "mla_qn", "mla_kn")],
        axis=1))
    c["w_uq"] = f(inp["mla_w_uq"])
    c["w_ukv"] = f(inp["mla_w_ukv"])
    c["w_glu"] = f(inp["s5_w_glu"])
    c["bglu"] = np.stack([_pc(inp["s5_b_glu"][l], 2) for l in range(DEPTH)])
    rp = np.zeros((DEPTH, 4, 15, 159), np.float32)
    rp[..., 64:95] = np.asarray(inp["na_rpb"], np.float32)
    c["rpb"] = rp

    def gn(a):
        a = np.asarray(a, np.float32).reshape(DEPTH, 2, 8, 2, 64)
        return np.ascontiguousarray(a.transpose(0, 3, 4, 1, 2).reshape(DEPTH, 128, 16))
    c["s5_lam"] = np.ascontiguousarray(np.stack([gn(inp["s5_lam_re"]), gn(inp["s5_lam_im"])], axis=2))
    ldt = np.asarray(inp["s5_log_dt"], np.float32)
    c["s5_logdt"] = gn(np.broadcast_to(ldt[..., None], (DEPTH, 2, 16, 64)))

    def gnc(a):
        a = np.asarray(a, np.float32).reshape(DEPTH, 2, 8, 2, 64, 16)
        return np.ascontiguousarray(a.transpose(0, 3, 4, 1, 2, 5).reshape(DEPTH, 128, 16, 16))
    c["s5_B"] = np.ascontiguousarray(np.stack([gnc(inp["s5_b_re"]), gnc(inp["s5_b_im"])], axis=2))
    ct = lambda a: np.asarray(a, np.float32).transpose(0, 1, 2, 4, 3)
    c["s5_C"] = np.ascontiguousarray(np.stack([gnc(ct(inp["s5_c_re"])), gnc(ct(inp["s5_c_im"]))], axis=2))
    dsk = np.asarray(inp["s5_d"], np.float32).reshape(DEPTH, 16, 16)
    c["s5_dsk"] = np.ascontiguousarray(np.broadcast_to(dsk.transpose(0, 2, 1)[:, None, :, :], (DEPTH, 8, 16, 16)).reshape(DEPTH, 128, 16))
    return c


def _core_inputs(inp, core):
    def xT(x):
        return np.ascontiguousarray(np.asarray(x, np.float32).T.reshape(8, 128, NT).transpose(1, 0, 2))
    b = core // 2
    m = {}
    m["xT_p"] = xT(np.asarray(inp["x_prompt"])[4 * core:4 * core + 4].reshape(NT, D))
    m["xT_s"] = xT(np.asarray(inp["x_sample"])[b])
    m["cvec"] = np.ascontiguousarray(np.stack([_pc(inp["c_ctx"], 8), _pc(np.asarray(inp["c"])[b], 8)], axis=2))
    m["c_na_kT"] = np.ascontiguousarray(np.asarray(inp["cache_na_k"], np.float32)[b].transpose(0, 1, 3, 2))
    m["c_na_v"] = np.ascontiguousarray(np.asarray(inp["cache_na_v"], np.float32)[b])
    m["c_gq_kT"] = np.ascontiguousarray(np.asarray(inp["cache_gqa_k"], np.float32)[b].transpose(0, 1, 3, 2))
    m["c_gq_v"] = np.ascontiguousarray(np.asarray(inp["cache_gqa_v"], np.float32)[b])
    m["c_ckv"] = np.ascontiguousarray(np.asarray(inp["cache_mla_ckv"], np.float32)[b])
    m["c_krope"] = np.ascontiguousarray(np.asarray(inp["cache_mla_krope"], np.float32)[b])
    st = np.asarray(inp["state_s5"], np.float32)[b].reshape(DEPTH, 2, 2, 8, 2, 64)
    m["s5_h0"] = np.ascontiguousarray(st.transpose(0, 4, 5, 1, 2, 3).reshape(DEPTH, 128, 2, 2, 8))
    return m
ARENA_BYTES = 104 * 1024


class Arena:
    def __init__(self, k, nbytes):
        self.k = k
        self.t = k.sbuf("arena", [128, nbytes // 2], BF16)
        self.n = nbytes // 2
        self.off = 0
        self.cur = []
        self.pending = []

    def reset(self):
        evs = []
        for t in self.cur:
            if t.writer is not None:
                evs.append(t.writer)
            evs.extend(t.readers)
        self.pending = self._dedupe(evs)
        self.cur = []
        self.off = 0

    @staticmethod
    def _dedupe(evs):
        best = {}
        for ev in evs:
            kk = id(ev.sem)
            if kk not in best or best[kk].val < ev.val:
                best[kk] = ev
        return list(best.values())

    def mark(self):
        return (self.off, len(self.cur))

    def reset_to(self, mark):
        evs = []
        for t in self.cur[mark[1]:]:
            if t.writer is not None:
                evs.append(t.writer)
            evs.extend(t.readers)
        self.pending = self._dedupe(list(self.pending) + evs)
        self.cur = self.cur[:mark[1]]
        self.off = mark[0]

    def alloc(self, free_shape, dtype, name=""):
        n = 1
        for s in free_shape:
            n *= s
        nb16 = n * (2 if dtype == F32 else 1)
        self.off = (self.off + 15) // 16 * 16
        assert self.off + nb16 <= self.n, ("arena overflow", name, self.off, nb16, self.n)
        ap = self.t[:, self.off:self.off + nb16]
        self.off += nb16
        if dtype == F32:
            ap = ap.bitcast(F32)
        if len(free_shape) == 2:
            ap = ap.rearrange("p (a b) -> p a b", a=free_shape[0])
        elif len(free_shape) == 3:
            ap = ap.rearrange("p (a b c) -> p a b c", a=free_shape[0], b=free_shape[1])
        elif len(free_shape) == 4:
            ap = ap.rearrange("p (a b c d) -> p a b c d", a=free_shape[0], b=free_shape[1], c=free_shape[2])
        t = T(ap, name)
        t.readers = list(self.pending)
        self.cur.append(t)
        return ap, t


class Prog:
    def __init__(self, shapes, groups=("ctx", "lat"), nlayers=2, dbg=False):
        self.groups = groups
        self.nlayers = nlayers
        nc = bass.Bass("TRN2", target_bir_lowering=False)
        self.nc = nc
        self.dr = {n: nc.dram_tensor(n, list(s), F32, kind="ExternalInput").ap() for n, s in shapes.items()}
        oshapes = dict(yT_p=[128, 8, NT], yT_s=[128, 8, NT], o_na_k=[4, 2, 4, 256, 64], o_na_v=[4, 2, 4, 256, 64],
                       o_s5=[4, 2, 2, 2, 16, 64], o_gq_k=[4, 2, 2, 256, 64], o_gq_v=[4, 2, 2, 256, 64],
                       o_ckv=[4, 2, 256, 128], o_krope=[4, 2, 256, 32])
        self.do = {n: nc.dram_tensor(n, s, F32, kind="ExternalOutput").ap() for n, s in oshapes.items()}
        k = self.k = K(nc)
        self.ps = [k.psum("ps%d" % i, [128, 512]) for i in range(8)]
        self.pt = [T(self.ps[i][:], "ps%d" % i) for i in range(8)]
        sb = k.sbuf
        self.xT = sb("xT", [128, 8, NT], F32)
        self.xT_t = [[T() for _ in range(2)] for _ in range(8)]
        self.hT = sb("hT", [128, 8, NT], BF16)
        self.hT_t = [[T() for _ in range(2)] for _ in range(8)]
        self.mixT = sb("mixT", [128, 8, NT], BF16)
        self.mix_t = [[T() for _ in range(2)] for _ in range(8)]
        self.wout = sb("wout", [128, 8, D], BF16)
        self.wout_t = T()
        self.arena = Arena(k, ARENA_BYTES)
        self.load_consts()
        self.ada_phase()
        for grp in groups:
            self.load_x(grp)
            for l in range(nlayers):
                self.layer(grp, l)
            self.store_x(grp)
        k.emit()
        k.close()

    def load_consts(self):
        k, dr = self.k, self.dr
        sb = k.sbuf
        self.ident = sb("ident", [128, 128], F32); self.ident_t = T()
        k.dma("sp", self.ident[:], dr["ident"], writes=[self.ident_t])
        self.ident16 = sb("ident16", [128, 128], BF16); self.ident16_t = T()
        k.copy("dve", self.ident16[:], self.ident[:], [self.ident_t], [self.ident16_t])
        self.ones16 = sb("ones16", [128, 128], BF16); self.ones_t = T()
        k.memset("dve", self.ones16[:], 1.0, [self.ones_t])
        self.sel = sb("sel32", [128, 2, 8, 128], BF16); self.sel_t = T()
        k.dma("pool", self.sel[:], dr["sel32"], writes=[self.sel_t])
        self.selT = sb("selT32", [128, 2, 8, 32], BF16); self.selT_t = T()
        k.dma("pool", self.selT[:], dr["selT32"], writes=[self.selT_t])
        self.masks = sb("masks", [128, 2, 128], F32); self.masks_t = T()
        k.dma("sp", self.masks[:], dr["masks"], writes=[self.masks_t])
        self.vmask = sb("na_vmask", [128, 2, 64], F32); self.vmask_t = T()
        k.dma("sp", self.vmask[:], dr["na_vmask"], writes=[self.vmask_t])
        self.anti16 = sb("anti16", [128, 128], BF16); self.anti16_t = T()
        k.dma("pool", self.anti16[:], dr["antiid"], writes=[self.anti16_t])
        self.pvals = sb("pvals", [128, 17], F32); self.pvals_t = T()
        k.dma("sp", self.pvals[:], dr["pvals"], writes=[self.pvals_t])
        self.posv = sb("posv", [128, 129], F32); self.posv_t = T()
        k.dma("sp", self.posv[:], dr["posv"], writes=[self.posv_t])
        self.rope_g = sb("rope_g", [128, 2, 8, 64], F32); self.rope_g_t = T()
        k.dma("sp", self.rope_g[:], dr["rope_g"], writes=[self.rope_g_t])
        self.rope_m = sb("rope_m", [128, 2, 8, 32], F32); self.rope_m_t = T()
        k.dma("sp", self.rope_m[:], dr["rope_m"], writes=[self.rope_m_t])
        self.normg = sb("normg", [128, DEPTH, 2, 8], F32); self.normg_t = T()
        self.convw = sb("convw", [128, DEPTH, 44, 3], F32); self.convw_t = T()
        self.convb = sb("convb", [128, DEPTH, 44], F32); self.convb_t = T()
        self.bglu = sb("bglu", [128, DEPTH, 2], F32); self.bglu_t = T()
        self.adab = sb("adab", [128, DEPTH, 48], F32); self.adab_t = T()
        for l in range(DEPTH):
            k.dma("sp", self.normg[:, l], dr["normg"][l], writes=[self.normg_t])
            k.dma("sp", self.convw[:, l], dr["convw"][l], writes=[self.convw_t])
            k.dma("sp", self.convb[:, l], dr["convb"][l], writes=[self.convb_t])
            k.dma("sp", self.bglu[:, l], dr["bglu"][l], writes=[self.bglu_t])
            k.dma("sp", self.adab[:, l], dr["ada_b"][l], writes=[self.adab_t])
        self.eps = sb("eps_c", [128, 1], F32); self.eps_t = T()
        k.memset("dve", self.eps[:], EPS, [self.eps_t])
        self.gains = sb("gains", [128, NG], F32); self.gains_t = T()
        self.mods = sb("mods", [128, DEPTH, 2, 6, 8], F32); self.mods_t = T()

    def ada_phase(self):
        k, dr, ar = self.k, self.dr, self.arena
        ar.reset()
        cv, cv_t = ar.alloc([8, 2], F32, "cv")
        sc, sc_t = ar.alloc([8, 2], F32, "silu_c")
        st = [ar.alloc([8, 512], F32, "ada_st%d" % i) for i in range(2)]
        raw, raw_t = ar.alloc([48, 2], F32, "mods_raw")
        tmp, tmp_t = ar.alloc([2, 8], F32, "mods_tmp")
        k.dma("sp", cv, dr["cvec"], writes=[cv_t])
        k.act(sc, cv, AF.Silu, [cv_t], [sc_t])
        pb, pb_t = self.ps[0], self.pt[0]
        for l in range(self.nlayers):
            for jg in range(12):
                sap, s_t = st[jg % 2]
                k.dma("sp", sap, dr["ada_w"][l][:, jg * 512:(jg + 1) * 512].rearrange("(c p) n -> p c n", p=128), writes=[s_t])
                for jj in range(4):
                    j = jg * 4 + jj
                    for c in range(8):
                        k.mm(pb[:, j * 2:j * 2 + 2], sap[:, c, jj * 128:(jj + 1) * 128], sc[:, c, :], c == 0, c == 7,
                             [s_t, sc_t], [pb_t])
            k.tt("dve", raw, pb[:, 0:96].rearrange("p (j v) -> p j v", v=2),
                 self.adab[:, l, :].unsqueeze(2).broadcast_to([128, 48, 2]), ALU.add, [pb_t, self.adab_t], [raw_t])
            md = self.mods
            for cvi in range(2):
                r6 = raw[:, :, cvi].rearrange("p (m c) -> p m c", c=8)
                k.copy("dve", md[:, l, cvi, 0:6:3, :], r6[:, 0:6:3, :], [raw_t], [self.mods_t])
                k.copy("dve", md[:, l, cvi, 2:6:3, :], r6[:, 2:6:3, :], [raw_t], [self.mods_t])
                k.ts("dve", tmp, r6[:, 1:6:3, :], 1.0, None, ALU.add, ALU.bypass, [raw_t], [tmp_t])
                k.tt("dve", md[:, l, cvi, 1:6:3, :], tmp, self.normg[:, l, :, :], ALU.mult, [tmp_t, self.normg_t], [self.mods_t])

    def load_x(self, grp):
        k = self.k
        src = self.dr["xT_p" if grp == "ctx" else "xT_s"]
        for c in range(8):
            k.dma("sp", self.xT[:, c, :], src[:, c, :], writes=[self.xT_t[c][0], self.xT_t[c][1]])

    def store_x(self, grp):
        k = self.k
        dst = self.do["yT_p" if grp == "ctx" else "yT_s"]
        for c in range(8):
            k.dma("sp", dst[:, c, :], self.xT[:, c, :], reads=[self.xT_t[c][0], self.xT_t[c][1]], final=True)

    def norm_mod(self, l, cvi, which):
        k, ar = self.k, self.arena
        sq, sq_t = ar.alloc([8, 512], BF16, "nm_sq")
        rs, rs_t = ar.alloc([512], F32, "nm_rs")
        rstd, rstd_t = ar.alloc([512], F32, "nm_rstd")
        tmps = [ar.alloc([512], F32, "nm_tmp%d" % i) for i in range(2)]
        sh_slot, gm_slot = (0, 1) if which == 0 else (3, 4)
        pb, pb_t = self.ps[7], self.pt[7]
        for b in range(2):
            bs = slice(b * 512, (b + 1) * 512)
            xts = [self.xT_t[c][b] for c in range(8)]
            k.act(sq, self.xT[:, :, bs], AF.Square, xts, [sq_t])
            for c in range(8):
                k.mm(pb[:, :], self.ones16[:], sq[:, c, :], c == 0, c == 7, [sq_t, self.ones_t], [pb_t])
            k.act(rs, pb[:, :], AF.Sqrt, [pb_t, self.eps_t], [rs_t], scale=1.0 / D, bias=self.eps[:])
            k.op("dve", lambda h, o=rstd, i=rs: h.reciprocal(out=o, in_=i), [rs_t], [rstd_t])
            for c in range(8):
                tp, tp_t = tmps[c % 2]
                k.stt(tp, self.xT[:, c, bs], self.mods[:, l, cvi, gm_slot, c:c + 1], rstd, ALU.mult, ALU.mult,
                      [self.xT_t[c][b], self.mods_t, rstd_t], [tp_t])
                k.act(self.hT[:, c, bs], tp, AF.Identity, [tp_t, self.mods_t], [self.hT_t[c][b]],
                      bias=self.mods[:, l, cvi, sh_slot, c:c + 1], scale=1.0)

    def rms_heads(self, src, src_ts, nh, dh, g_bc, g_t, dst, dst_ts, scr, mul_eng="pool", dst2=None):
        k = self.k
        n = nh * dh
        sq, sq_t = scr["sq"]
        ss, ss_t = scr["ss"]
        rstd, rstd_t = scr["rstd"]
        tmp, tmp_t = scr["tmp"]
        sq3 = sq[:, 0:n].rearrange("p (h d) -> p h d", h=nh)
        tmp3 = tmp[:, 0:n].rearrange("p (h d) -> p h d", h=nh)
        k.act(sq3, src, AF.Square, src_ts, [sq_t])
        k.op("dve", lambda h: h.tensor_reduce(out=ss[:, 0:nh], in_=sq3, axis=AX.X, op=ALU.add), [sq_t], [ss_t])
        k.act(ss[:, 0:nh], ss[:, 0:nh], AF.Sqrt, [ss_t, self.eps_t], [ss_t], scale=1.0 / dh, bias=self.eps[:])
        k.op("dve", lambda h: h.reciprocal(out=rstd[:, 0:nh], in_=ss[:, 0:nh]), [ss_t], [rstd_t])
        k.tt("dve", tmp3, src, rstd[:, 0:nh].unsqueeze(2).broadcast_to([128, nh, dh]), ALU.mult, src_ts + [rstd_t], [tmp_t])
        k.tt(mul_eng, dst, tmp3, g_bc, ALU.mult, [tmp_t, g_t], dst_ts)
        if dst2 is not None:
            d2, d2_ts, hs = dst2
            k.tt("dve", d2, tmp3[:, hs, :], g_bc[:, hs, :], ALU.mult, [tmp_t, g_t], d2_ts)

    def rope(self, x, x_ts, nh, dr, tab, tab_t, tt, out, out_ts, scr):
        k = self.k
        q = dr // 4
        t1, t1_t = scr["r1"]
        t2, t2_t = scr["r2"]
        n = nh * dr
        t13 = t1[:, 0:n].rearrange("p (h d) -> p h d", h=nh)
        t23 = t2[:, 0:n].rearrange("p (h d) -> p h d", h=nh)
        cos = tab[:, 0, tt, :]
        sin = tab[:, 1, tt, :]
        k.tt("dve", t13, x, cos.unsqueeze(1).broadcast_to([128, nh, dr]), ALU.mult, x_ts + [tab_t], [t1_t])
        for b in range(2):
            xs = x[:, :, b * 2 * q:(b + 1) * 2 * q].rearrange("p h (f q) -> p h f q", f=2)[:, :, ::-1, :]
            sb_ = sin[:, b * 2 * q:(b + 1) * 2 * q].rearrange("p (f q) -> p f q", f=2).unsqueeze(1).broadcast_to([128, nh, 2, q])
            ob = t23[:, :, b * 2 * q:(b + 1) * 2 * q].rearrange("p h (f q) -> p h f q", f=2)
            k.tt("pool", ob, xs, sb_, ALU.mult, x_ts + [tab_t], [t2_t])
        k.tt("dve", out, t13, t23, ALU.add, [t1_t, t2_t], out_ts)

    def tok_scratch(self, rope=True, n=512):
        ar = self.arena
        s = {}
        s["sq"] = ar.alloc([n], F32, "sq")
        s["ss"] = ar.alloc([8], F32, "ss")
        s["rstd"] = ar.alloc([8], F32, "rstd")
        s["tmp"] = ar.alloc([n], F32, "tmp")
        if rope:
            s["r1"] = ar.alloc([n], F32, "r1")
            s["r2"] = ar.alloc([n], F32, "r2")
        return s

    def load_win(self, l, c0, c1, name):
        k, ar = self.k, self.arena
        w, w_t = ar.alloc([8, c1 - c0], BF16, name)
        src = self.dr["w_in"][l][:, c0:c1].rearrange("(c p) n -> p c n", p=128)
        for c in range(0, 8, 2):
            k.dma("pool", w[:, c:c + 2, :], src[:, c:c + 2, :], writes=[w_t])
        return w, w_t

    def gains_bc(self, l):
        k = self.k
        k.dma("sp", self.gains[:], self.dr["gains"][l:l + 1, :].broadcast_to([128, NG]), writes=[self.gains_t])

    def attention(self, units, Pbufs, rD):
        k = self.k
        sbanks = [0, 1, 2, 3]
        sb_i = 0
        pb_i = 0
        for ui, u in enumerate(units):
            nq = u["nq"]
            per = max(1, 512 // nq)
            chunks = u["chunks"]
            groups = [chunks[i:i + per] for i in range(0, len(chunks), per)]
            ob, db = (4, 5) if ui % 2 == 0 else (6, 7)
            psO, psD = self.ps[ob], self.ps[db]
            r0 = u["rows"]
            rsl = slice(r0, r0 + 64)
            nch = len(chunks)
            done = [0]

            def do_s(grp_chunks):
                nonlocal sb_i, pb_i
                bank = sbanks[sb_i % 4]
                sb_i += 1
                pS, pS_t = self.ps[bank], self.pt[bank]
                for j, (KT, V, k_ts, bias) in enumerate(grp_chunks):
                    k.mm(pS[:, j * nq:(j + 1) * nq], KT, u["QT"], True, bias is None, u["q_ts"] + k_ts, [pS_t])
                    if bias is not None:
                        k.mm(pS[:, j * nq:(j + 1) * nq], bias[0], bias[1], False, True, bias[2], [pS_t])
                Pb, Pb_t = Pbufs[pb_i % len(Pbufs)]
                pb_i += 1
                cols = len(grp_chunks) * nq
                k.act(Pb[:, 0:cols], pS[:, 0:cols], AF.Exp, [pS_t], [Pb_t])
                return Pb, Pb_t

            def do_pv(grp_chunks, Pb, Pb_t):
                for j, (KT, V, k_ts, bias) in enumerate(grp_chunks):
                    first = done[0] == 0
                    last = done[0] == nch - 1
                    k.mm(psO[rsl, 0:nq], V, Pb[:, j * nq:(j + 1) * nq], first, last, [Pb_t] + k_ts, [self.pt[ob]])
                    k.mm(psD[rsl, 0:nq], self.ones16[:, 0:64], Pb[:, j * nq:(j + 1) * nq], first, last, [Pb_t, self.ones_t], [self.pt[db]])
                    done[0] += 1

            pend = []
            for g in groups:
                pend.append((g,) + do_s(g))
                if len(pend) > 2:
                    do_pv(*pend.pop(0))
            while pend:
                do_pv(*pend.pop(0))
            rd, rd_t = rD
            k.op("dve", lambda h, o=rd[rsl, 0:nq], i=psD[rsl, 0:nq]: h.reciprocal(out=o, in_=i), [self.pt[db]], [rd_t])
            k.tt("dve", u["out"], psO[rsl, 0:nq], rd[rsl, 0:nq], ALU.mult, [self.pt[ob], rd_t], u["out_ts"])

    def mixer_na(self, grp, l):
        k, ar, dr = self.k, self.arena, self.dr
        lat = grp == "lat"
        ar.reset_to(self.mark_all)
        win, win_t = self.wins["na"]
        scrs = [self.tok_scratch(rope=False), self.tok_scratch(rope=False)]
        gbc, gbc_t = ar.alloc([8, 64], F32, "g_na")
        QKT, QKT_t = ar.alloc([4, NT], BF16, "QKT_na")
        Vn, Vn_t = ar.alloc([8, 256], BF16, "V_na")
        nrm = [ar.alloc([8, 64], F32, "na_nrm%d" % i) for i in range(2)]
        qk16 = [ar.alloc([512], BF16, "na_qk16%d" % i) for i in range(2)]
        v32 = [ar.alloc([256], F32, "na_v32%d" % i) for i in range(2)]
        Pbufs = [ar.alloc([512], BF16, "na_P%d" % i) for i in range(4)]
        rD = ar.alloc([512], F32, "na_rD")
        scale = 64 ** -0.5
        k.ts("dve", gbc[:, 0:4, :], self.gains[:, G_OFF["na_q"]:G_OFF["na_q"] + 64].unsqueeze(1).broadcast_to([128, 4, 64]),
             scale, None, ALU.mult, ALU.bypass, [self.gains_t], [gbc_t])
        k.copy("dve", gbc[:, 4:8, :], self.gains[:, G_OFF["na_k"]:G_OFF["na_k"] + 64].unsqueeze(1).broadcast_to([128, 4, 64]),
               [self.gains_t], [gbc_t])
        def prep(tt):
            scr = scrs[tt % 2]
            ts_ = slice(tt * 128, (tt + 1) * 128)
            b = tt // 4
            za, zb = (0, 1) if tt % 2 == 0 else (2, 3)
            trb = 4 + tt % 2
            hts = [self.hT_t[c][b] for c in range(8)]
            for c in range(8):
                k.mm(self.ps[za][:, :], self.hT[:, c, ts_], win[:, c, 0:512], c == 0, c == 7, [hts[c], win_t], [self.pt[za]])
            for c in range(8):
                k.mm(self.ps[zb][:, 0:256], self.hT[:, c, ts_], win[:, c, 512:768], c == 0, c == 7, [hts[c], win_t], [self.pt[zb]])
            nr, nr_t = nrm[tt % 2]
            q16, q16_t = qk16[tt % 2]
            vv, vv_t = v32[tt % 2]
            self.rms_heads(self.ps[za][:, :].rearrange("p (h d) -> p h d", h=8), [self.pt[za]], 8, 64, gbc, gbc_t,
                           q16.rearrange("p (h d) -> p h d", h=8), [q16_t], scr, mul_eng="dve",
                           dst2=None if lat else (nr[:, 4:8, :], [nr_t], slice(4, 8)))
            if not lat:
                k.copy("act", vv, self.ps[zb][:, 0:256], [self.pt[zb]], [vv_t])
            k.copy("act", Vn[:, tt, :], self.ps[zb][:, 0:256], [self.pt[zb]], [Vn_t])
            if not lat:
                s, t0 = tt // 2, (tt % 2) * 128
                k.dma("sp", self.do["o_na_k"][s, l, :, t0:t0 + 128, :].rearrange("h t d -> t h d"), nr[:, 4:8, :], reads=[nr_t], final=True)
                k.dma("sp", self.do["o_na_v"][s, l, :, t0:t0 + 128, :].rearrange("h t d -> t h d"),
                      vv.rearrange("p (h d) -> p h d", h=4), reads=[vv_t], final=True)
            pT = self.ps[trb][:].bitcast(BF16)
            for i in range(4):
                k.tr(pT[:, i * 128:(i + 1) * 128], q16[:, i * 128:(i + 1) * 128], self.ident16[:], [q16_t, self.ident16_t], [self.pt[trb]])
            k.copy("dve", QKT[:, :, ts_], pT[:, 0:512].rearrange("p (i t) -> p i t", i=4), [self.pt[trb]], [QKT_t])
        for tp_ in range(4):
            lanes = []
            for tt in (2 * tp_, 2 * tp_ + 1):
                k.lane = []
                lanes.append(k.lane)
                prep(tt)
            k.run_lanes(lanes)
        if not lat:
            units = []
            for s in range(4):
                for h in range(4):
                    r0 = (h % 2) * 64
                    rs_ = slice(r0, r0 + 64)
                    qs = slice(s * 256, (s + 1) * 256)
                    chunks = []
                    for j in range(2):
                        ks = slice(s * 256 + j * 128, s * 256 + (j + 1) * 128)
                        chunks.append((QKT[rs_, 2 + h // 2, ks], Vn[:, s * 2 + j, h * 64:(h + 1) * 64], [QKT_t, Vn_t], None))
                    units.append(dict(QT=QKT[rs_, h // 2, qs], q_ts=[QKT_t], nq=256, chunks=chunks, rows=r0,
                                      out=self.mixT[rs_, h // 2, qs], out_ts=[self.mix_t[h // 2][s // 2]]))
            self.attention(units, Pbufs, rD)
        else:
            self.na_latent(l, QKT, QKT_t, Vn, Vn_t, Pbufs, rD)

    def mixer_gq(self, grp, l):
        k, ar, dr = self.k, self.arena, self.dr
        lat = grp == "lat"
        ar.reset_to(self.mark_all)
        win, win_t = self.wins["gq"]
        scrs = [self.tok_scratch(rope=lat), self.tok_scratch(rope=lat)]
        gbc, gbc_t = ar.alloc([6, 64], F32, "g_gq")
        nkeys = NT + (P_LEN if lat else 0)
        QT, QT_t = ar.alloc([2, NT], BF16, "QT_gq")
        KT, KT_t = ar.alloc([nkeys], BF16, "KT_gq")
        Vg, Vg_t = ar.alloc([nkeys // 128, 128], BF16, "V_gq")
        nrm = [ar.alloc([6, 64], F32, "gq_nrm%d" % i) for i in range(2)]
        rp = [ar.alloc([6, 64], F32, "gq_rp%d" % i) for i in range(2)]
        qk16 = [ar.alloc([384], BF16, "gq_qk16%d" % i) for i in range(2)]
        v32 = [ar.alloc([128], F32, "gq_v32%d" % i) for i in range(2)]
        Pbufs = [ar.alloc([512], BF16, "gq_P%d" % i) for i in range(4)]
        rD = ar.alloc([512], F32, "gq_rD")
        scale = 64 ** -0.5
        k.ts("dve", gbc[:, 0:4, :], self.gains[:, G_OFF["gq_q"]:G_OFF["gq_q"] + 64].unsqueeze(1).broadcast_to([128, 4, 64]),
             scale, None, ALU.mult, ALU.bypass, [self.gains_t], [gbc_t])
        k.copy("dve", gbc[:, 4:6, :], self.gains[:, G_OFF["gq_k"]:G_OFF["gq_k"] + 64].unsqueeze(1).broadcast_to([128, 2, 64]),
               [self.gains_t], [gbc_t])
        if lat:
            st32, st32_t = ar.alloc([512], F32, "gq_stage")
            for kv in range(2):
                k.dma("sp", st32[kv * 64:(kv + 1) * 64, :], dr["c_gq_kT"][l, kv], writes=[st32_t])
            k.copy("pool", KT[:, NT:NT + P_LEN], st32, [st32_t], [KT_t])
            for j in range(4):
                k.dma("pool", Vg[:, 8 + j, :].rearrange("p (h d) -> p h d", h=2),
                      dr["c_gq_v"][l, :, j * 128:(j + 1) * 128, :].rearrange("h t d -> t h d"), writes=[Vg_t])
        def prep(tt):
            scr = scrs[tt % 2]
            ts_ = slice(tt * 128, (tt + 1) * 128)
            b = tt // 4
            za = 0 if tt % 2 == 0 else 2
            trb = 4 + tt % 2
            hts = [self.hT_t[c][b] for c in range(8)]
            for c in range(8):
                k.mm(self.ps[za][:, :], self.hT[:, c, ts_], win[:, c, :], c == 0, c == 7, [hts[c], win_t], [self.pt[za]])
            nr, nr_t = nrm[tt % 2]
            q16, q16_t = qk16[tt % 2]
            vv, vv_t = v32[tt % 2]
            self.rms_heads(self.ps[za][:, 0:384].rearrange("p (h d) -> p h d", h=6), [self.pt[za]], 6, 64, gbc, gbc_t, nr, [nr_t], scr,
                           mul_eng="dve")
            k.copy("act", vv, self.ps[za][:, 384:512], [self.pt[za]], [vv_t])
            k.copy("pool", Vg[:, tt, :], vv, [vv_t], [Vg_t])
            if not lat:
                s, t0 = tt // 2, (tt % 2) * 128
                k.dma("sp", self.do["o_gq_k"][s, l, :, t0:t0 + 128, :].rearrange("h t d -> t h d"), nr[:, 4:6, :], reads=[nr_t], final=True)
                k.dma("sp", self.do["o_gq_v"][s, l, :, t0:t0 + 128, :].rearrange("h t d -> t h d"),
                      vv.rearrange("p (h d) -> p h d", h=2), reads=[vv_t], final=True)
                src, src_t = nr, nr_t
            else:
                src, src_t = rp[tt % 2]
                self.rope(nr, [nr_t], 6, 64, self.rope_g, self.rope_g_t, tt, src, [src_t], scr)
            q16v = q16.rearrange("p (h d) -> p h d", h=6)
            k.copy("pool", q16v[:, 0:4, :].rearrange("p (b a) d -> p a b d", a=2), src[:, 0:4, :].rearrange("p (a b) d -> p a b d", a=2),
                   [src_t], [q16_t])
            k.copy("pool", q16v[:, 4:6, :], src[:, 4:6, :], [src_t], [q16_t])
            pT = self.ps[trb][:].bitcast(BF16)
            for i in range(3):
                k.tr(pT[:, i * 128:(i + 1) * 128], q16[:, i * 128:(i + 1) * 128], self.ident16[:], [q16_t, self.ident16_t], [self.pt[trb]])
            k.copy("dve", QT[:, :, ts_], pT[:, 0:256].rearrange("p (i t) -> p i t", i=2), [self.pt[trb]], [QT_t])
            k.copy("dve", KT[:, ts_], pT[:, 256:384], [self.pt[trb]], [KT_t])
        for tp_ in range(4):
            lanes = []
            for tt in (2 * tp_, 2 * tp_ + 1):
                k.lane = []
                lanes.append(k.lane)
                prep(tt)
            k.run_lanes(lanes)
        units = []
        if not lat:
            for s in range(4):
                for h in range(4):
                    kv = h // 2
                    rb = slice(kv * 64, kv * 64 + 64)
                    qs = slice(s * 256, (s + 1) * 256)
                    chunks = []
                    for j in range(2):
                        ks = slice(s * 256 + j * 128, s * 256 + (j + 1) * 128)
                        chunks.append((KT[rb, ks], Vg[:, s * 2 + j, kv * 64:(kv + 1) * 64], [KT_t, Vg_t], None))
                    r0 = (h % 2) * 64
                    units.append(dict(QT=QT[rb, h % 2, qs], q_ts=[QT_t], nq=256, chunks=chunks, rows=r0,
                                      out=self.mixT[r0:r0 + 64, 4 + h // 2, qs], out_ts=[self.mix_t[4 + h // 2][s // 2]]))
        else:
            for h in range(4):
                kv = h // 2
                rb = slice(kv * 64, kv * 64 + 64)
                for qb in range(2):
                    qs = slice(qb * 512, (qb + 1) * 512)
                    chunks = []
                    for j in range(12):
                        chunks.append((KT[rb, j * 128:(j + 1) * 128], Vg[:, j, kv * 64:(kv + 1) * 64], [KT_t, Vg_t], None))
                    r0 = (h % 2) * 64
                    units.append(dict(QT=QT[rb, h % 2, qs], q_ts=[QT_t], nq=512, chunks=chunks, rows=r0,
                                      out=self.mixT[r0:r0 + 64, 4 + h // 2, qs], out_ts=[self.mix_t[4 + h // 2][qb]]))
        self.attention(units, Pbufs, rD)
    def mixer_mla(self, grp, l):
        k, ar, dr = self.k, self.arena, self.dr
        lat = grp == "lat"
        ar.reset_to(self.mark_all)
        win, win_t = self.wins["mla"]
        scr_q = self.tok_scratch(rope=lat)
        scr_kv = self.tok_scratch(rope=lat)
        wuq, wuq_t = ar.alloc([2, 384], BF16, "wuq")
        wukv, wukv_t = ar.alloc([512], BF16, "wukv")
        k.dma("pool", wuq[:, 0, :], dr["w_uq"][l, 0:128, :], writes=[wuq_t])
        k.dma("pool", wuq[0:64, 1, :], dr["w_uq"][l, 128:192, :], writes=[wuq_t])
        k.dma("pool", wukv, dr["w_ukv"][l], writes=[wukv_t])
        gq, gq_t = ar.alloc([4, 96], F32, "g_mq")
        gk, gk_t = ar.alloc([4, 96], F32, "g_mk")
        scale = 96 ** -0.5
        k.ts("dve", gq, self.gains[:, G_OFF["mq"]:G_OFF["mq"] + 96].unsqueeze(1).broadcast_to([128, 4, 96]),
             scale, None, ALU.mult, ALU.bypass, [self.gains_t], [gq_t])
        k.copy("dve", gk, self.gains[:, G_OFF["mk"]:G_OFF["mk"] + 96].unsqueeze(1).broadcast_to([128, 4, 96]), [self.gains_t], [gk_t])
        g_qa = self.gains[:, G_OFF["qa"]:G_OFF["qa"] + 192]
        g_kva = self.gains[:, G_OFF["kva"]:G_OFF["kva"] + 128]
        nkt = 8 + (4 if lat else 0)
        QT, QT_t = ar.alloc([4, NT], BF16, "QT_mla")
        KT, KT_t = ar.alloc([4, nkt * 128], BF16, "KT_mla")
        Vm, Vm_t = ar.alloc([nkt, 256], BF16, "V_mla")
        cq16, cq16_t = ar.alloc([192], BF16, "cq16")
        cqT, cqT_t = ar.alloc([2, 128], BF16, "cqT")
        ckv32 = [ar.alloc([128], F32, "ckv32_%d" % i) for i in range(2)]
        ckv16, ckv16_t = ar.alloc([128], BF16, "ckv16")
        ckvT, ckvT_t = ar.alloc([128], BF16, "ckvT")
        kr32 = [ar.alloc([32], F32, "kr32_%d" % i) for i in range(2)]
        kcat, kcat_t = ar.alloc([4, 96], F32, "kcat")
        nq, nq_t = ar.alloc([4, 96], F32, "mla_nq")
        nk, nk_t = ar.alloc([4, 96], F32, "mla_nk")
        q16, q16_t = ar.alloc([4, 96], BF16, "mla_q16")
        k16, k16_t = ar.alloc([4, 96], BF16, "mla_k16")
        Pbufs = [ar.alloc([512], BF16, "mla_P%d" % i) for i in range(4)]
        rD = ar.alloc([512], F32, "mla_rD")

        def kv_path(kt, tt_rope, ckvn16_src_ts, kr_ap, kr_ts, zq, zkv, trb):
            pT = self.ps[trb][:].bitcast(BF16)
            k.tr(pT[:, 0:128], ckv16, self.ident16[:], [ckv16_t, self.ident16_t], [self.pt[trb]])
            k.copy("dve", ckvT, pT[:, 0:128], [self.pt[trb]], [ckvT_t])
            k.mm(self.ps[zkv][:, :], ckvT, wukv, True, True, [ckvT_t, wukv_t], [self.pt[zkv]])
            kv4 = self.ps[zkv][:, :].rearrange("p (h x) -> p h x", h=4)
            k.copy("act", kcat[:, :, 0:64], kv4[:, :, 0:64], [self.pt[zkv]], [kcat_t])
            k.copy("pool", kcat[:, :, 64:96], kr_ap.unsqueeze(1).broadcast_to([128, 4, 32]), kr_ts, [kcat_t])
            k.copy("act", Vm[:, kt, :].rearrange("p (h d) -> p h d", h=4), kv4[:, :, 64:128], [self.pt[zkv]], [Vm_t])
            self.rms_heads(kcat, [kcat_t], 4, 96, gk, gk_t, nk, [nk_t], scr_kv, mul_eng="dve")
            if tt_rope is not None:
                self.rope(nk[:, :, 64:96], [nk_t], 4, 32, self.rope_m, self.rope_m_t, tt_rope, nk[:, :, 64:96], [nk_t], scr_kv)
            k.copy("act", k16, nk, [nk_t], [k16_t])
            for h in range(4):
                k.tr(pT[0:96, 128 + h * 128:256 + h * 128], k16[:, h, :], self.ident16[:], [k16_t, self.ident16_t], [self.pt[trb]])
            k.copy("dve", KT[0:96, :, kt * 128:(kt + 1) * 128], pT[0:96, 128:640].rearrange("p (h t) -> p h t", h=4), [self.pt[trb]], [KT_t])

        for tt in range(8):
            ts_ = slice(tt * 128, (tt + 1) * 128)
            b = tt // 4
            za, zq, zkv = (0, 1, 2)
            trb = 4 + tt % 2
            hts = [self.hT_t[c][b] for c in range(8)]
            for c in range(8):
                k.mm(self.ps[za][:, 0:352], self.hT[:, c, ts_], win[:, c, :], c == 0, c == 7, [hts[c], win_t], [self.pt[za]])
            z = self.ps[za]
            lane_q, lane_kv = [], []
            k.lane = lane_q
            self.rms_heads(z[:, 0:192].unsqueeze(1), [self.pt[za]], 1, 192, g_qa.unsqueeze(1), self.gains_t,
                           cq16.unsqueeze(1), [cq16_t], scr_q, mul_eng="dve")
            pT = self.ps[trb][:].bitcast(BF16)
            k.tr(pT[:, 0:128], cq16[:, 0:128], self.ident16[:], [cq16_t, self.ident16_t], [self.pt[trb]])
            k.tr(pT[0:64, 128:256], cq16[:, 128:192], self.ident16[:], [cq16_t, self.ident16_t], [self.pt[trb]])
            k.copy("dve", cqT[:, 0, :], pT[:, 0:128], [self.pt[trb]], [cqT_t])
            k.copy("dve", cqT[0:64, 1, :], pT[0:64, 128:256], [self.pt[trb]], [cqT_t])
            k.mm(self.ps[zq][:, 0:384], cqT[:, 0, :], wuq[:, 0, :], True, False, [cqT_t, wuq_t], [self.pt[zq]])
            k.mm(self.ps[zq][:, 0:384], cqT[0:64, 1, :], wuq[0:64, 1, :], False, True, [cqT_t, wuq_t], [self.pt[zq]])
            self.rms_heads(self.ps[zq][:, 0:384].rearrange("p (h d) -> p h d", h=4), [self.pt[zq]], 4, 96, gq, gq_t, nq, [nq_t], scr_q, mul_eng="dve")
            if lat:
                self.rope(nq[:, :, 64:96], [nq_t], 4, 32, self.rope_m, self.rope_m_t, tt, nq[:, :, 64:96], [nq_t], scr_q)
            k.copy("act", q16, nq, [nq_t], [q16_t])
            trq = 6 + tt % 2
            pQ = self.ps[trq][:].bitcast(BF16)
            for h in range(4):
                k.tr(pQ[0:96, h * 128:(h + 1) * 128], q16[:, h, :], self.ident16[:], [q16_t, self.ident16_t], [self.pt[trq]])
            k.copy("dve", QT[0:96, :, ts_], pQ[0:96, 0:512].rearrange("p (h t) -> p h t", h=4), [self.pt[trq]], [QT_t])
            k.lane = lane_kv
            c32, c32_t = ckv32[tt % 2]
            kr, kr_t = kr32[tt % 2]
            self.rms_heads(z[:, 192:320].unsqueeze(1), [self.pt[za]], 1, 128, g_kva.unsqueeze(1), self.gains_t,
                           c32.unsqueeze(1), [c32_t], scr_kv, mul_eng="dve")
            k.copy("act", kr, z[:, 320:352], [self.pt[za]], [kr_t])
            k.copy("pool", ckv16, c32, [c32_t], [ckv16_t])
            if not lat:
                s, t0 = tt // 2, (tt % 2) * 128
                k.dma("sp", self.do["o_ckv"][s, l, t0:t0 + 128, :], c32, reads=[c32_t], final=True)
                k.dma("sp", self.do["o_krope"][s, l, t0:t0 + 128, :], kr, reads=[kr_t], final=True)
            kv_path(tt, tt if lat else None, None, kr, [kr_t], zq, zkv, 3)
            k.run_lanes([lane_q, lane_kv])
        if lat:
            for j in range(4):
                c32, c32_t = ckv32[j % 2]
                kr, kr_t = kr32[j % 2]
                k.dma("sp", c32, dr["c_ckv"][l, j * 128:(j + 1) * 128, :], writes=[c32_t])
                k.dma("sp", kr, dr["c_krope"][l, j * 128:(j + 1) * 128, :], writes=[kr_t])
                k.copy("pool", ckv16, c32, [c32_t], [ckv16_t])
                kv_path(8 + j, None, None, kr, [kr_t], 1, 2, 4 + j % 2)
        units = []
        if not lat:
            for s in range(4):
                for h in range(4):
                    qs = slice(s * 256, (s + 1) * 256)
                    chunks = []
                    for j in range(2):
                        ks = slice(s * 256 + j * 128, s * 256 + (j + 1) * 128)
                        chunks.append((KT[0:96, h, ks], Vm[:, s * 2 + j, h * 64:(h + 1) * 64], [KT_t, Vm_t], None))
                    r0 = (h % 2) * 64
                    units.append(dict(QT=QT[0:96, h, qs], q_ts=[QT_t], nq=256, chunks=chunks, rows=r0,
                                      out=self.mixT[r0:r0 + 64, 6 + h // 2, qs], out_ts=[self.mix_t[6 + h // 2][s // 2]]))
        else:
            for h in range(4):
                for qb in range(2):
                    qs = slice(qb * 512, (qb + 1) * 512)
                    chunks = []
                    for j in range(12):
                        chunks.append((KT[0:96, h, j * 128:(j + 1) * 128], Vm[:, j, h * 64:(h + 1) * 64], [KT_t, Vm_t], None))
                    r0 = (h % 2) * 64
                    units.append(dict(QT=QT[0:96, h, qs], q_ts=[QT_t], nq=512, chunks=chunks, rows=r0,
                                      out=self.mixT[r0:r0 + 64, 6 + h // 2, qs], out_ts=[self.mix_t[6 + h // 2][qb]]))
        self.attention(units, Pbufs, rD)

    def out_proj(self, l, cvi):
        k = self.k
        for b in range(2):
            bs = slice(b * 512, (b + 1) * 512)
            for m in range(8):
                bank = (b * 8 + m) % 4
                for c in range(8):
                    k.mm(self.ps[bank][:, :], self.wout[:, c, m * 128:(m + 1) * 128], self.mixT[:, c, bs], c == 0, c == 7,
                         [self.wout_t, self.mix_t[c][b]], [self.pt[bank]])
                k.stt(self.xT[:, m, bs], self.ps[bank][:, :], self.mods[:, l, cvi, 2, m:m + 1], self.xT[:, m, bs], ALU.mult, ALU.add,
                      [self.pt[bank], self.mods_t, self.xT_t[m][b]], [self.xT_t[m][b]])

    def ffn(self, grp, l, cvi):
        k, ar, dr = self.k, self.arena, self.dr
        S, L = (4, 256) if grp == "ctx" else (1, 1024)
        actT, actT_t = ar.alloc([NFT, NT], BF16, "actT")
        act_ts = [[T() for _ in range(2)] for _ in range(NFT)]
        for i in range(NFT):
            for b in range(2):
                act_ts[i][b].readers = list(actT_t.readers)
                ar.cur.append(act_ts[i][b])
        wup = [ar.alloc([8, 2, 128], BF16, "wup%d" % i) for i in range(2)]
        wdn = [ar.alloc([NFT, 128], BF16, "wdn%d" % i) for i in range(2)]
        ubf_ = [ar.alloc([2 * S * (L + 2)], BF16, "ubuf%d" % i) for i in range(2)]
        ub = [(a.rearrange("p (g s t) -> p g s t", g=2, s=S), t) for (a, t) in ubf_]
        dg = [ar.alloc([6, 128], BF16, "diag%d" % i) for i in range(2)]
        sg = [ar.alloc([512], BF16, "sgate%d" % i) for i in range(2)]
        for i in range(2):
            k.memset("pool", ubf_[i][0], 0.0, [ub[i][1]])
        upsrc = dr["w_up"][l].rearrange("(c p) n -> p c n", p=128)
        dnsrc = dr["w_down"][l].rearrange("(i p) n -> p i n", p=128)

        def load_wdn(m):
            wd, wd_t = wdn[m % 2]
            for i0 in range(0, NFT, 11):
                k.dma("pool", wd[:, i0:i0 + 11, :], dnsrc[:, i0:i0 + 11, m * 128:(m + 1) * 128], writes=[wd_t])

        for i in range(NFT):
            if i == 2:
                load_wdn(0)
                load_wdn(1)
            w, w_t = wup[i % 2]
            u, u_t = ub[i % 2]
            d6, d6_t = dg[i % 2]
            k.dma("pool", w[:, :, 0, :], upsrc[:, :, i * 128:(i + 1) * 128], writes=[w_t])
            k.dma("pool", w[:, :, 1, :], upsrc[:, :, D_FF + i * 128:D_FF + (i + 1) * 128], writes=[w_t])
            for gu in range(2):
                tile_idx = gu * NFT + i
                for tap in range(3):
                    k.ts("pool", d6[:, gu * 3 + tap, :], self.ident[:], self.convw[:, l, tile_idx, tap:tap + 1], 1.0, ALU.mult, ALU.mult,
                         [self.ident_t, self.convw_t], [d6_t])
            for b in range(2):
                bs = slice(b * 512, (b + 1) * 512)
                for gu in range(2):
                    bank = b * 2 + gu
                    for c in range(8):
                        k.mm(self.ps[bank][:, :], w[:, c, gu, :], self.hT[:, c, bs], c == 0, c == 7, [w_t, self.hT_t[c][b]], [self.pt[bank]])
                    if S == 4:
                        dst = u[:, gu, 2 * b:2 * b + 2, 1:L + 1]
                        srcp = self.ps[bank][:, :].rearrange("p (s t) -> p s t", s=2)
                    else:
                        dst = u[:, gu, 0, 1 + b * 512:1 + (b + 1) * 512]
                        srcp = self.ps[bank][:, :]
                    k.copy("act" if gu == 0 else "dve", dst, srcp, [self.pt[bank]], [u_t])
            for b in range(2):
                bs = slice(b * 512, (b + 1) * 512)
                for gu in range(2):
                    bank = 4 + b * 2 + gu
                    for tap in range(3):
                        if S == 4:
                            rhs = u[:, gu, 2 * b:2 * b + 2, tap:tap + L]
                        else:
                            rhs = u[:, gu, 0, b * 512 + tap:b * 512 + tap + 512]
                        k.mm(self.ps[bank][:, :], d6[:, gu * 3 + tap, :], rhs, tap == 0, tap == 2, [d6_t, u_t], [self.pt[bank]])
                sgb, sgb_t = sg[b]
                k.act(sgb, self.ps[4 + b * 2][:, :], AF.Silu, [self.pt[4 + b * 2], self.convb_t], [sgb_t],
                      bias=self.convb[:, l, i:i + 1], scale=1.0)
                k.stt(actT[:, i, bs], self.ps[5 + b * 2][:, :], self.convb[:, l, NFT + i:NFT + i + 1], sgb, ALU.add, ALU.mult,
                      [self.pt[5 + b * 2], self.convb_t, sgb_t], [act_ts[i][b]])
        for m in range(8):
            w, w_t = wdn[m % 2]
            if m >= 2:
                load_wdn(m)
            for b in range(2):
                bs = slice(b * 512, (b + 1) * 512)
                bank = (m * 2 + b) % 4
                for i in range(NFT):
                    k.mm(self.ps[bank][:, :], w[:, i, :], actT[:, i, bs], i == 0, i == NFT - 1, [w_t, act_ts[i][b]], [self.pt[bank]])
                k.stt(self.xT[:, m, bs], self.ps[bank][:, :], self.mods[:, l, cvi, 5, m:m + 1], self.xT[:, m, bs], ALU.mult, ALU.add,
                      [self.pt[bank], self.mods_t, self.xT_t[m][b]], [self.xT_t[m][b]])
    def na_latent(self, l, QKT, QKT_t, Vn, Vn_t, Pbufs, rD):
        k, ar, dr = self.k, self.arena, self.dr
        NEG = -30000.0
        KTc, KTc_t = ar.alloc([2, P_LEN], BF16, "na_KTc")
        Vc, Vc_t = ar.alloc([4, 256], BF16, "na_Vc")
        stg, stg_t = ar.alloc([4, 19, 64], F32, "na_bias32")
        BT, BT_t = ar.alloc([4, 19, 64], BF16, "na_bias16")
        for h in range(4):
            k.dma("pool", KTc[(h % 2) * 64:(h % 2) * 64 + 64, h // 2, :], dr["c_na_kT"][l, h], writes=[KTc_t])
        for j in range(4):
            k.dma("pool", Vc[:, j, :].rearrange("p (h d) -> p h d", h=4),
                  dr["c_na_v"][l, :, j * 128:(j + 1) * 128, :].rearrange("h t d -> t h d"), writes=[Vc_t])
        rp = dr["rpb"]
        for h in range(4):
            base = rp[l, h].offset
            for half in range(2):
                src = bass.AP(rp.tensor, base + 16, [[1, 64], [159, 15], [1, 64]])
                k.dma("sp", stg[half * 64:(half + 1) * 64, h, 0:15, :], src, writes=[stg_t])
        for h in range(4):
            body = stg[:, h, 0:15, :]
            k.tt("dve", body, body, self.vmask[:, 0, :].unsqueeze(1).broadcast_to([128, 15, 64]), ALU.mult, [stg_t, self.vmask_t], [stg_t])
            k.tt("dve", body, body, self.vmask[:, 1, :].unsqueeze(1).broadcast_to([128, 15, 64]), ALU.add, [stg_t, self.vmask_t], [stg_t])
        k.memset("dve", stg[:, :, 15, :], NEG, [stg_t])
        k.memset("dve", stg[:, :, 18, :], NEG, [stg_t])
        k.copy("dve", stg[:, :, 16, :], stg[:, :, 3, :], [stg_t], [stg_t])
        k.copy("dve", stg[:, :, 17, :], stg[:, :, 10, :], [stg_t], [stg_t])
        k.copy("dve", BT.rearrange("p h r c -> p (h r c)"), stg.rearrange("p h r c -> p (h r c)"), [stg_t], [BT_t])
        units = []
        for qr in range(16):
            r0 = min(max(qr - 4, 0), 8)
            for h in range(4):
                rb0 = (h % 2) * 64
                rb = slice(rb0, rb0 + 64)
                idn = self.anti16[rb, rb0:rb0 + 64]
                chunks = []
                if r0 % 2 == 0:
                    for jj in range(4):
                        kr = r0 + 2 * jj
                        slot = kr - qr + 7
                        chunks.append((QKT[rb, 2 + h // 2, kr * 64:kr * 64 + 128], Vn[:, kr // 2, h * 64:(h + 1) * 64], [QKT_t, Vn_t],
                                       (BT[rb, h, slot:slot + 2, :].rearrange("p r c -> p (r c)"), idn, [BT_t, self.anti16_t])))
                else:
                    for jj in range(5):
                        kr = r0 - 1 + 2 * jj
                        slot = 15 if jj == 0 else (17 if jj == 4 else kr - qr + 7)
                        chunks.append((QKT[rb, 2 + h // 2, kr * 64:kr * 64 + 128], Vn[:, kr // 2, h * 64:(h + 1) * 64], [QKT_t, Vn_t],
                                       (BT[rb, h, slot:slot + 2, :].rearrange("p r c -> p (r c)"), idn, [BT_t, self.anti16_t])))
                for j in range(4):
                    chunks.append((KTc[rb, h // 2, j * 128:(j + 1) * 128], Vc[:, j, h * 64:(h + 1) * 64], [KTc_t, Vc_t], None))
                qs = slice(qr * 64, (qr + 1) * 64)
                units.append(dict(QT=QKT[rb, h // 2, qs], q_ts=[QKT_t], nq=64, chunks=chunks, rows=rb0,
                                  out=self.mixT[rb, h // 2, qs], out_ts=[self.mix_t[h // 2][qr // 8]]))
        self.attention(units, Pbufs, rD)

    def frac(self, out, x, shape_t, scr_t, eng="dve"):
        k = self.k
        M = 12582912.0
        t, t_t = scr_t
        k.ts(eng, t, x, M, None, ALU.add, ALU.bypass, shape_t, [t_t])
        k.ts(eng, t, t, -M, None, ALU.add, ALU.bypass, [t_t], [t_t])
        k.tt(eng, out, x, t, ALU.subtract, shape_t + [t_t], shape_t)

    def cmul(self, ore, oim, are, aim, bre, bim, t1, t2, reads, o_t, neg_im=False):
        k = self.k
        (t1a, t1_t), (t2a, t2_t) = t1, t2
        k.tt("dve", t1a, are, bre, ALU.mult, reads, [t1_t])
        k.tt("dve", t2a, aim, bim, ALU.mult, reads, [t2_t])
        k.tt("dve", ore, t1a, t2a, ALU.subtract, [t1_t, t2_t], [o_t])
        k.tt("dve", t1a, are, bim, ALU.mult, reads, [t1_t])
        k.tt("dve", t2a, aim, bre, ALU.mult, reads, [t2_t])
        if neg_im:
            k.tt("dve", t1a, t1a, t2a, ALU.add, [t1_t, t2_t], [t1_t])
            k.ts("dve", oim, t1a, -1.0, None, ALU.mult, ALU.bypass, [t1_t], [o_t])
        else:
            k.tt("dve", oim, t1a, t2a, ALU.add, [t1_t, t2_t], [o_t])

    def mixer_s5(self, grp, l):
        k, ar, dr = self.k, self.arena, self.dr
        lat = grp == "lat"
        S, L = (1, 1024) if lat else (4, 256)
        NKs = L // 8
        seg = NKs + 1
        ar.reset_to(self.mark_s5)
        win, win_t = self.wins["s5"]
        W1, W1_t = ar.alloc([16, 2, 128], BF16, "s5_W1")
        CA16, CA16_t = ar.alloc([16, 2, 128], BF16, "s5_CA16")
        Kt16, Kt16_t = ar.alloc([16, 128], BF16, "s5_Ktot16")
        Ec, Ec_t = ar.alloc([16, 129], F32, "s5_Ecos")
        Es, Es_t = ar.alloc([16, 129], F32, "s5_Esin")
        r8, r8_t = ar.alloc([16], F32, "s5_r8")
        h0, h0_t = ar.alloc([2, 2, 8], F32, "s5_h0")
        wglu, wglu_t = ar.alloc([2, 256], BF16, "s5_wglu")
        k.dma("pool", wglu, dr["w_glu"][l].rearrange("(c p) n -> p c n", p=128), writes=[wglu_t])
        if lat:
            k.dma("sp", h0, dr["s5_h0"][l], writes=[h0_t])
        mark = (ar.off, len(ar.cur))
        lam, lam_t = ar.alloc([2, 16], F32, "lam")
        ldt, ldt_t = ar.alloc([16], F32, "ldt")
        Bm, Bm_t = ar.alloc([2, 16, 16], F32, "Bm")
        Cm, Cm_t = ar.alloc([2, 16, 16], F32, "Cm")
        dsk, dsk_t = ar.alloc([16], F32, "dsk")
        k.dma("sp", lam, dr["s5_lam"][l], writes=[lam_t])
        k.dma("sp", ldt, dr["s5_logdt"][l], writes=[ldt_t])
        k.dma("sp", Bm, dr["s5_B"][l], writes=[Bm_t])
        k.dma("sp", Cm, dr["s5_C"][l], writes=[Cm_t])
        k.dma("sp", dsk, dr["s5_dsk"][l], writes=[dsk_t])
        sm = {n: ar.alloc([16], F32, "s5_" + n) for n in ("dt", "lrdt", "y1", "y8", "den", "fre", "fim", "ta", "tb", "tc")}
        big = {n: ar.alloc([16, 17], F32, "s5_" + n) for n in ("pm", "ang", "t", "a2", "Pre", "Pim")}
        dt, dt_t = sm["dt"]
        k.act(dt, ldt, AF.Exp, [ldt_t], [dt_t])
        lrdt, lrdt_t = sm["lrdt"]
        k.tt("dve", lrdt, lam[:, 0, :], dt, ALU.mult, [lam_t, dt_t], [lrdt_t])
        y1, y1_t = sm["y1"]
        k.tt("dve", y1, lam[:, 1, :], dt, ALU.mult, [lam_t, dt_t], [y1_t])
        k.ts("dve", y1, y1, 1.0 / TWO_PI, None, ALU.mult, ALU.bypass, [y1_t], [y1_t])
        self.frac(y1, y1, [y1_t], sm["ta"])
        bc17 = lambda a: a.unsqueeze(2).broadcast_to([128, 16, 17])
        pvb = self.pvals[:].unsqueeze(1).broadcast_to([128, 16, 17])
        pm, pm_t = big["pm"]
        ang, ang_t = big["ang"]
        a2, a2_t = big["a2"]
        Pre, Pre_t = big["Pre"]
        Pim, Pim_t = big["Pim"]
        k.tt("dve", pm, bc17(lrdt), pvb, ALU.mult, [lrdt_t, self.pvals_t], [pm_t])
        k.act(pm, pm, AF.Exp, [pm_t], [pm_t])
        k.tt("dve", ang, bc17(y1), pvb, ALU.mult, [y1_t, self.pvals_t], [ang_t])
        k.ts("dve", a2, ang, 0.25, None, ALU.add, ALU.bypass, [ang_t], [a2_t])
        self.frac(ang, ang, [ang_t], big["t"])
        self.frac(a2, a2, [a2_t], big["t"])
        k.act(ang, ang, AF.Sin, [ang_t], [ang_t], scale=TWO_PI)
        k.act(a2, a2, AF.Sin, [a2_t], [a2_t], scale=TWO_PI)
        k.tt("dve", Pre, pm, a2, ALU.mult, [pm_t, a2_t], [Pre_t])
        k.tt("dve", Pim, pm, ang, ALU.mult, [pm_t, ang_t], [Pim_t])
        import os
        pst_ = int(os.environ.get('KS5P', '9'))
        if pst_ < 1:
            return
        den, den_t = sm["den"]
        ta, ta_t = sm["ta"]
        tb, tb_t = sm["tb"]
        tc, tc_t = sm["tc"]
        fre, fre_t = sm["fre"]
        fim, fim_t = sm["fim"]
        lr, li = lam[:, 0, :], lam[:, 1, :]
        k.tt("dve", den, lr, lr, ALU.mult, [lam_t], [den_t])
        k.tt("dve", ta, li, li, ALU.mult, [lam_t], [ta_t])
        k.tt("dve", den, den, ta, ALU.add, [den_t, ta_t], [den_t])
        k.op("dve", lambda h: h.reciprocal(out=den, in_=den), [den_t], [den_t])
        k.ts("dve", tc, Pre[:, :, 9], -1.0, None, ALU.add, ALU.bypass, [Pre_t], [tc_t])
        k.tt("dve", ta, tc, lr, ALU.mult, [tc_t, lam_t], [ta_t])
        k.tt("dve", tb, Pim[:, :, 9], li, ALU.mult, [Pim_t, lam_t], [tb_t])
        k.tt("dve", ta, ta, tb, ALU.add, [ta_t, tb_t], [ta_t])
        k.tt("dve", fre, ta, den, ALU.mult, [ta_t, den_t], [fre_t])
        k.tt("dve", ta, Pim[:, :, 9], lr, ALU.mult, [Pim_t, lam_t], [ta_t])
        k.tt("dve", tb, tc, li, ALU.mult, [tc_t, lam_t], [tb_t])
        k.tt("dve", ta, ta, tb, ALU.subtract, [ta_t, tb_t], [ta_t])
        k.tt("dve", fim, ta, den, ALU.mult, [ta_t, den_t], [fim_t])
        Bbr, Bbr_t = ar.alloc([16, 16], F32, "Bbr")
        Bbi, Bbi_t = ar.alloc([16, 16], F32, "Bbi")
        c1 = ar.alloc([8, 8, 16], F32, "s5_c1")
        c2 = ar.alloc([8, 8, 16], F32, "s5_c2")
        bc16 = lambda a: a.unsqueeze(2).broadcast_to([128, 16, 16])
        c1v = (c1[0].rearrange("p a b c -> p (a b c)")[:, 0:256].rearrange("p (a b) -> p a b", a=16), c1[1])
        c2v = (c2[0].rearrange("p a b c -> p (a b c)")[:, 0:256].rearrange("p (a b) -> p a b", a=16), c2[1])
        self.cmul(Bbr, Bbi, bc16(fre), bc16(fim), Bm[:, 0], Bm[:, 1], c1v, c2v, [fre_t, fim_t, Bm_t], Bbr_t)
        Bbi_t.writer = Bbr_t.writer
        y8, y8_t = sm["y8"]
        k.ts("dve", y8, y1, 8.0, None, ALU.mult, ALU.bypass, [y1_t], [y8_t])
        self.frac(y8, y8, [y8_t], sm["ta"])
        k.ts("dve", r8, lrdt, 8.0, None, ALU.mult, ALU.bypass, [lrdt_t], [r8_t])
        k.act(r8, r8, AF.Exp, [r8_t], [r8_t])
        et = ar.alloc([16, 129], F32, "s5_et")
        posb = self.posv[:].unsqueeze(1).broadcast_to([128, 16, 129])
        k.tt("dve", Es, y8.unsqueeze(2).broadcast_to([128, 16, 129]), posb, ALU.mult, [y8_t, self.posv_t], [Es_t])
        k.ts("dve", Ec, Es, 0.25, None, ALU.add, ALU.bypass, [Es_t], [Ec_t])
        self.frac(Es, Es, [Es_t], et)
        self.frac(Ec, Ec, [Ec_t], et)
        k.act(Es, Es, AF.Sin, [Es_t], [Es_t], scale=TWO_PI)
        k.act(Ec, Ec, AF.Sin, [Ec_t], [Ec_t], scale=TWO_PI)
        if pst_ < 2:
            return
        Kacc, Kacc_t = ar.alloc([16, 128], F32, "s5_Kacc")
        k.tt("dve", Kacc, self.ident[:].unsqueeze(1).broadcast_to([128, 16, 128]), dsk.unsqueeze(2).broadcast_to([128, 16, 128]),
             ALU.mult, [self.ident_t, dsk_t], [Kacc_t])
        XAr, XAr_t = ar.alloc([8, 8, 16], F32, "s5_XAr")
        XAi, XAi_t = ar.alloc([8, 8, 16], F32, "s5_XAi")
        tmpK, tmpK_t = ar.alloc([4, 128], F32, "s5_tmpK")
        xb16 = [ar.alloc([8, 128], BF16, "s5_xb16_%d" % i) for i in range(3)]
        fl = lambda a: a.rearrange("p t s c -> p t (s c)")
        bank_i = 0
        for d in range(2):
            tl = slice(d * 8, d * 8 + 8)
            if d == 0:
                p1, p2, pc = slice(15, 7, -1), slice(7, None, -1), slice(9, 17)
            else:
                p1, p2, pc = slice(8, 16), slice(0, 8), slice(16, 8, -1)
            pw = lambda P_, s_: P_[:, tl, s_].unsqueeze(3).broadcast_to([128, 8, 8, 16])
            bb = lambda a: a[:, tl, :].unsqueeze(2).broadcast_to([128, 8, 8, 16])
            rd_ = [Pre_t, Pim_t, Bbr_t, Cm_t]
            self.cmul(XAr, XAi, pw(Pre, pc), pw(Pim, pc), bb(Cm[:, 0]), bb(Cm[:, 1]), c1, c2, rd_, XAr_t, neg_im=True)
            XAi_t.writer = XAr_t.writer
            k.copy("dve", CA16[:, tl, 0, :], fl(XAr), [XAr_t], [CA16_t])
            k.copy("dve", CA16[:, tl, 1, :], fl(XAi), [XAi_t, XAr_t], [CA16_t])
            if pst_ < 3:
                continue
            x1_16, x1_16_t = xb16[0]
            self.cmul(XAr, XAi, pw(Pre, p1), pw(Pim, p1), bb(Bbr), bb(Bbi), c1, c2, rd_ + [CA16_t], XAr_t)
            XAi_t.writer = XAr_t.writer
            q_ = int(os.environ.get('KS5Q', '9'))
            for ri, (xa, xa_t) in enumerate(((XAr, XAr_t), (XAi, XAi_t))):
                if q_ < 1:
                    continue
                k.copy("dve", x1_16, fl(xa), [xa_t, XAr_t], [x1_16_t])
                if q_ < 2:
                    continue
                for t4 in range(2):
                    bank = bank_i % 4
                    bank_i += 1
                    pTw = self.ps[bank][:].bitcast(BF16)
                    for i in range(4):
                        ti = t4 * 4 + i
                        k.tr(pTw[:, i * 128:(i + 1) * 128], x1_16[:, ti, :], self.ident16[:], [x1_16_t, self.ident16_t], [self.pt[bank]])
                    if q_ < 3:
                        continue
                    k.copy("dve", W1[:, d * 8 + t4 * 4:d * 8 + t4 * 4 + 4, ri, :], pTw[:, 0:512].rearrange("p (i n) -> p i n", i=4),
                           [self.pt[bank]], [W1_t])
            if pst_ < 4:
                continue
            kk_ = int(os.environ.get('KS5K', '9'))
            if kk_ < 1:
                continue
            self.cmul(XAr, XAi, pw(Pre, p2), pw(Pim, p2), bb(Bbr), bb(Bbi), c1, c2, rd_ + [x1_16_t], XAr_t)
            XAi_t.writer = XAr_t.writer
            x2r16, x2r_t = xb16[1]
            x2i16, x2i_t = xb16[2]
            k.copy("dve", x2r16, fl(XAr), [XAr_t], [x2r_t])
            k.copy("dve", x2i16, fl(XAi), [XAi_t, XAr_t], [x2i_t])
            Kacc4 = Kacc.rearrange("p (gp g2) n -> p gp g2 n", g2=2)
            for g2 in range(2):
                rr = slice(g2 * 64, g2 * 64 + 64)
                for gp4 in range(2):
                    bank = bank_i % 4
                    bank_i += 1
                    for i in range(4):
                        gp = gp4 * 4 + i
                        k.mm(self.ps[bank][:, i * 128:(i + 1) * 128], x2r16[rr, gp, :], CA16[rr, d * 8 + gp, 0, :], True, False,
                             [x2r_t, CA16_t], [self.pt[bank]])
                        k.mm(self.ps[bank][:, i * 128:(i + 1) * 128], x2i16[rr, gp, :], CA16[rr, d * 8 + gp, 1, :], False, True,
                             [x2i_t, CA16_t], [self.pt[bank]])
                    k.tt("dve", tmpK, self.ps[bank][:, :].rearrange("p (g n) -> p g n", g=4),
                         self.masks[:, d, :].unsqueeze(1).broadcast_to([128, 4, 128]), ALU.mult, [self.pt[bank], self.masks_t], [tmpK_t])
                    kv_ = Kacc4[:, gp4 * 4:gp4 * 4 + 4, g2, :]
                    k.tt("dve", kv_, kv_, tmpK, ALU.add, [tmpK_t, Kacc_t], [Kacc_t])
        k.copy("act", Kt16, Kacc, [Kacc_t], [Kt16_t])
        import os
        stg_ = int(os.environ.get('KS5', '9'))
        evs = []
        for t in ar.cur[mark[1]:]:
            if t.writer is not None:
                evs.append(t.writer)
            evs.extend(t.readers)
        ar.pending = list(ar.pending) + evs
        ar.cur = ar.cur[:mark[1]]
        ar.off = mark[0]
        ubf, ubf_t = ar.alloc([2, NT], BF16, "s5_ubf")
        U8, U8_t = ar.alloc([16, 128], BF16, "s5_U8")
        Hp, Hp_t = ar.alloc([16, 2, 128], BF16, "s5_Hprev")
        y8b, y8b_t = ar.alloc([16, 128], BF16, "s5_y8")
        qre, qre_t = ar.alloc([4, S, seg], F32, "s5_qre")
        qim, qim_t = ar.alloc([4, S, seg], F32, "s5_qim")
        sre, sre_t = ar.alloc([4, S, seg], F32, "s5_sre")
        sim, sim_t = ar.alloc([4, S, seg], F32, "s5_sim")
        Rc, Rc_t = ar.alloc([4, S, seg], F32, "s5_R")
        hre, hre_t = ar.alloc([4, S, seg], F32, "s5_hre")
        him, him_t = ar.alloc([4, S, seg], F32, "s5_him")
        w1, w1_t = ar.alloc([4, S, seg], F32, "s5_w1")
        w2, w2_t = ar.alloc([4, S, seg], F32, "s5_w2")
        Fin, Fin_t = ar.alloc([4, 2, 2, 8], F32, "s5_fin")
        FinT, FinT_t = ar.alloc([128], F32, "s5_finT")
        y32, y32_t = ar.alloc([2, NT], F32, "s5_y32")
        g1, g1_t = ar.alloc([2, NT], F32, "s5_g1")
        yg, yg_t = ar.alloc([2, NT], BF16, "s5_yg")
        sgl, sgl_t = ar.alloc([512], BF16, "s5_sgl")
        if stg_ < 1:
            return
        for ft in range(2):
            for b in range(2):
                bs = slice(b * 512, (b + 1) * 512)
                bank = ft * 2 + b
                for c in range(8):
                    k.mm(self.ps[bank][:, :], win[:, c, ft * 128:(ft + 1) * 128], self.hT[:, c, bs], c == 0, c == 7,
                         [win_t, self.hT_t[c][b]], [self.pt[bank]])
                k.copy("act", ubf[:, ft, bs], self.ps[bank][:, :], [self.pt[bank]], [ubf_t])
        U8v = U8.rearrange("p (ft q par) n -> p ft q par n", ft=2, q=4)
        for q4 in range(4):
            bank = 4 + q4 % 2
            rr = slice(32 * q4, 32 * q4 + 32)
            for ft in range(2):
                for par in range(2):
                    gi = ft * 2 + par
                    for j in range(8):
                        k.mm(self.ps[bank][:, gi * 128:(gi + 1) * 128], self.sel[rr, par, j, :], ubf[rr, ft, j:NT:8], j == 0, j == 7,
                             [self.sel_t, ubf_t], [self.pt[bank]], tile_position=(32 * q4, 0))
            k.copy("dve", U8v[:, :, q4, :, :], self.ps[bank][:, :].rearrange("p (ft par n) -> p ft par n", ft=2, par=2), [self.pt[bank]], [U8_t])
        if stg_ < 2:
            return
        flat = lambda a: a.rearrange("p i s k -> p (i s k)")
        for rnd in range(4):
            d, half = rnd // 2, rnd % 2
            t0 = d * 8 + half * 4
            tl = slice(t0, t0 + 4)
            bre, bim = 0, 1
            for i in range(4):
                gp = half * 4 + i
                for g2 in range(2):
                    g = 2 * gp + g2
                    orow = slice(g2 * 64, g2 * 64 + 64)
                    k.mm(self.ps[bre][orow, i * 128:(i + 1) * 128], W1[:, t0 + i, 0, g2 * 64:g2 * 64 + 64], U8[:, g, :], True, True,
                         [W1_t, U8_t], [self.pt[bre]])
                    k.mm(self.ps[bim][orow, i * 128:(i + 1) * 128], W1[:, t0 + i, 1, g2 * 64:g2 * 64 + 64], U8[:, g, :], True, True,
                         [W1_t, U8_t], [self.pt[bim]])
            Sre = self.ps[bre][:, :].rearrange("p (i s k) -> p i s k", i=4, s=S)
            Sim = self.ps[bim][:, :].rearrange("p (i s k) -> p i s k", i=4, s=S)
            if d == 1:
                Sre = Sre[:, :, :, ::-1]
                Sim = Sim[:, :, :, ::-1]
            cosv = Ec[:, tl, 1:seg].unsqueeze(2).broadcast_to([128, 4, S, NKs])
            sinv = Es[:, tl, 1:seg].unsqueeze(2).broadcast_to([128, 4, S, NKs])
            body = lambda a: a[:, :, :, 1:seg]
            if lat:
                k.copy("dve", qre[:, :, 0, 0], h0[:, d, 0, half * 4:half * 4 + 4], [h0_t], [qre_t])
                k.copy("dve", qim[:, :, 0, 0], h0[:, d, 1, half * 4:half * 4 + 4], [h0_t], [qim_t])
            else:
                k.memset("dve", qre[:, :, :, 0], 0.0, [qre_t])
                k.memset("dve", qim[:, :, :, 0], 0.0, [qim_t])
            k.copy("dve", flat(Rc).rearrange("p (i x) -> p i x", i=4), r8[:, tl].unsqueeze(2).broadcast_to([128, 4, S * seg]), [r8_t], [Rc_t])
            k.memset("dve", Rc[:, :, :, 0], 0.0, [Rc_t])
            k.tt("dve", body(w1), Sre, cosv, ALU.mult, [self.pt[bre], Ec_t], [w1_t])
            k.tt("dve", body(w2), Sim, sinv, ALU.mult, [self.pt[bim], Es_t], [w2_t])
            k.tt("dve", body(qre), body(w1), body(w2), ALU.add, [w1_t, w2_t], [qre_t])
            k.tt("dve", body(w1), Sim, cosv, ALU.mult, [self.pt[bim], Ec_t], [w1_t])
            k.tt("dve", body(w2), Sre, sinv, ALU.mult, [self.pt[bre], Es_t], [w2_t])
            k.tt("dve", body(qim), body(w1), body(w2), ALU.subtract, [w1_t, w2_t], [qim_t])
            k.op("dve", lambda h: h.tensor_tensor_scan(out=flat(sre), data0=flat(Rc), data1=flat(qre), initial=0.0, op0=ALU.mult, op1=ALU.add),
                 [Rc_t, qre_t], [sre_t])
            k.op("dve", lambda h: h.tensor_tensor_scan(out=flat(sim), data0=flat(Rc), data1=flat(qim), initial=0.0, op0=ALU.mult, op1=ALU.add),
                 [Rc_t, qim_t], [sim_t])
            cosa = Ec[:, tl, 0:seg].unsqueeze(2).broadcast_to([128, 4, S, seg])
            sina = Es[:, tl, 0:seg].unsqueeze(2).broadcast_to([128, 4, S, seg])
            k.tt("dve", w1, sre, cosa, ALU.mult, [sre_t, Ec_t], [w1_t])
            k.tt("dve", w2, sim, sina, ALU.mult, [sim_t, Es_t], [w2_t])
            k.tt("dve", hre, w1, w2, ALU.subtract, [w1_t, w2_t], [hre_t])
            k.tt("dve", w1, sre, sina, ALU.mult, [sre_t, Es_t], [w1_t])
            k.tt("dve", w2, sim, cosa, ALU.mult, [sim_t, Ec_t], [w2_t])
            k.tt("dve", him, w1, w2, ALU.add, [w1_t, w2_t], [him_t])
            for ri, (hh, hh_t) in enumerate(((hre, hre_t), (him, him_t))):
                srcv = hh[:, :, :, 0:NKs] if d == 0 else hh[:, :, :, NKs - 1::-1]
                k.copy("act", Hp[:, tl, ri, :].rearrange("p i (s k) -> p i s k", s=S), srcv, [hh_t], [Hp_t])
                if not lat:
                    k.copy("dve", Fin[:, :, d, ri, half * 4:half * 4 + 4].rearrange("p s i -> p i s"), hh[:, :, :, NKs], [hh_t], [Fin_t])
        if stg_ < 3:
            return
        y8v = y8b.rearrange("p (gp g2) n -> p gp g2 n", g2=2)
        bi_ = 0
        for g2 in range(2):
            rr = slice(g2 * 64, g2 * 64 + 64)
            for gp4 in range(2):
                bank = 2 + bi_ % 2
                bi_ += 1
                for i in range(4):
                    gp = gp4 * 4 + i
                    g = 2 * gp + g2
                    osl = self.ps[bank][:, i * 128:(i + 1) * 128]
                    k.mm(osl, Kt16[:, g, :], U8[:, g, :], True, False, [Kt16_t, U8_t], [self.pt[bank]])
                    for d in range(2):
                        ti = d * 8 + gp
                        k.mm(osl, CA16[rr, ti, 0, :], Hp[rr, ti, 0, :], False, False, [CA16_t, Hp_t], [self.pt[bank]])
                        k.mm(osl, CA16[rr, ti, 1, :], Hp[rr, ti, 1, :], False, d == 1, [CA16_t, Hp_t], [self.pt[bank]])
                k.copy("act", y8v[:, gp4 * 4:gp4 * 4 + 4, g2, :], self.ps[bank][:, :].rearrange("p (g n) -> p g n", g=4), [self.pt[bank]], [y8b_t])
        if stg_ < 4:
            return
        for ft in range(2):
            for jh in range(2):
                bank = 4 + (ft * 2 + jh) % 2
                for jj in range(4):
                    j = jh * 4 + jj
                    for q4 in range(4):
                        for par in range(2):
                            g = ft * 8 + q4 * 2 + par
                            k.mm(self.ps[bank][32 * q4:32 * q4 + 32, jj * 128:(jj + 1) * 128], self.selT[:, par, j, :], y8b[:, g, :],
                                 par == 0, par == 1, [self.selT_t, y8b_t], [self.pt[bank]], tile_position=(0, 32 * q4))
                k.copy("act", y32[:, ft, :].rearrange("p (k j) -> p j k", j=8)[:, jh * 4:jh * 4 + 4, :],
                       self.ps[bank][:, :].rearrange("p (j k) -> p j k", j=4), [self.pt[bank]], [y32_t])
        if stg_ < 5:
            return
        k.tt("dve", g1, y32, y32, ALU.mult, [y32_t], [g1_t])
        k.ts("dve", g1, g1, 0.044715, 1.0, ALU.mult, ALU.add, [g1_t], [g1_t])
        k.tt("dve", g1, g1, y32, ALU.mult, [g1_t, y32_t], [g1_t])
        k.act(g1, g1, AF.Sigmoid, [g1_t], [g1_t], scale=1.5957691216057308)
        k.tt("dve", yg, g1, y32, ALU.mult, [g1_t, y32_t], [yg_t])
        for mt in range(2):
            for b in range(2):
                bs = slice(b * 512, (b + 1) * 512)
                bank = mt * 2 + b
                for kc in range(2):
                    k.mm(self.ps[bank][:, :], wglu[:, kc, mt * 128:(mt + 1) * 128], yg[:, kc, bs], kc == 0, kc == 1, [wglu_t, yg_t], [self.pt[bank]])
                k.act(sgl, self.ps[bank][:, :], AF.Sigmoid, [self.pt[bank], self.bglu_t], [sgl_t], bias=self.bglu[:, l, mt:mt + 1], scale=1.0)
                k.tt("dve", self.mixT[:, 2 + mt, bs], yg[:, mt, bs], sgl, ALU.mult, [yg_t, sgl_t], [self.mix_t[2 + mt][b]])
        if stg_ < 6:
            return
        if not lat:
            for s_ in range(4):
                k.dma("sp", self.do["o_s5"][s_, l].rearrange("d r (gp g2) n -> (g2 n) d r gp", g2=2), Fin[:, s_, :, :, :],
                      reads=[Fin_t], final=True, allow_slow_non_contiguous=True)

    def layer(self, grp, l):
        cvi = 0 if grp == "ctx" else 1
        self.gains_bc(l)
        self.arena.reset()
        self.wins = {}
        self.wins["s5"] = self.load_win(l, 768, 1024, "win_s5")
        self.mark_s5 = self.arena.mark()
        self.wins["na"] = self.load_win(l, 0, 768, "win_na")
        self.wins["gq"] = self.load_win(l, 1024, 1536, "win_gq")
        self.wins["mla"] = self.load_win(l, 1536, 1888, "win_mla")
        self.mark_all = self.arena.mark()
        wsrc = self.dr["w_out"][l].rearrange("(c p) n -> p c n", p=128)
        for c in range(0, 8, 2):
            self.k.dma("pool", self.wout[:, c:c + 2, :], wsrc[:, c:c + 2, :], writes=[self.wout_t])
        import os
        ph = os.environ.get("KPH", "norm,na,gq,mla,s5,out,ffn").split(",")
        if "norm" in ph:
            self.norm_mod(l, cvi, 0)
        if "na" in ph:
            self.mixer_na(grp, l)
        if "gq" in ph:
            self.mixer_gq(grp, l)
        if "mla" in ph:
            self.mixer_mla(grp, l)
        if "s5" in ph:
            self.mixer_s5(grp, l)
        if "out" in ph:
            self.out_proj(l, cvi)
        self.arena.reset()
        if "ffn" in ph:
            self.norm_mod(l, cvi, 1)
            self.ffn(grp, l, cvi)


_PROG_CACHE = {}


def _get_prog(shapes, groups, nlayers):
    key = (tuple(sorted((n, tuple(s)) for n, s in shapes.items())), tuple(groups), nlayers)
    if key not in _PROG_CACHE:
        _PROG_CACHE[key] = Prog(shapes, groups=groups, nlayers=nlayers)
    return _PROG_CACHE[key]


def _run(inputs, groups=("ctx", "lat"), nlayers=2):
    common = _common_inputs(inputs)
    in_maps = []
    for core in range(NCORES):
        m = dict(common)
        m.update(_core_inputs(inputs, core))
        in_maps.append(m)
    shapes = {n: a.shape for n, a in in_maps[0].items()}
    prog = _get_prog(shapes, groups, nlayers)
    res = run_bass_kernel_spmd(prog.nc, in_maps, core_ids=list(range(NCORES)))
    return res.results


def kernel(**inputs):
    r = _run(inputs)

    def unT(a):
        return np.ascontiguousarray(np.asarray(a).transpose(2, 1, 0).reshape(NT, D))
    y_prompt = np.concatenate([unT(r[c]["yT_p"]).reshape(4, 256, D) for c in range(NCORES)], axis=0).astype(np.float32)
    y_sample = np.stack([unT(r[2 * b]["yT_s"]) for b in range(4)], axis=0).astype(np.float32)
    cat = lambda n: np.concatenate([np.asarray(r[c][n]) for c in range(NCORES)], axis=0).astype(np.float32)
    return (y_prompt, y_sample, cat("o_na_k"), cat("o_na_v"), cat("o_s5"), cat("o_gq_k"), cat("o_gq_v"), cat("o_ckv"), cat("o_krope"))
```

```python
import math
from contextlib import ExitStack
import numpy as np
import ml_dtypes
import concourse.bass as bass
import concourse.mybir as mybir
from concourse.bass_utils import run_bass_kernel_spmd

F32 = mybir.dt.float32
BF16 = mybir.dt.bfloat16
AF = mybir.ActivationFunctionType
ALU = mybir.AluOpType
AX = mybir.AxisListType

SAME_ENGINE_SYNC = True


class T:
    __slots__ = ("ap", "name", "writer", "readers")

    def __init__(self, ap=None, name=""):
        self.ap = ap
        self.name = name
        self.writer = None
        self.readers = []


class Ev:
    __slots__ = ("sem", "val", "clock", "eng")

    def __init__(self, sem, val, clock, eng):
        self.sem = sem
        self.val = val
        self.clock = clock
        self.eng = eng


class EngState:
    def __init__(self, name, handle, sem):
        self.name = name
        self.h = handle
        self.sem = sem
        self.count = 0
        self.seen = {}
        self.ops = []


class K:
    def __init__(self, nc, n_dma_sems=64):
        self.nc = nc
        self.st = ExitStack()
        self.E = {}
        for name, h in (("pe", nc.tensor), ("act", nc.scalar), ("dve", nc.vector),
                        ("pool", nc.gpsimd), ("sp", nc.sync)):
            sem = self.st.enter_context(nc.semaphore("sem_" + name))
            self.E[name] = EngState(name, h, sem)
        self.dma_sems = []
        for i in range(n_dma_sems):
            s = self.st.enter_context(nc.semaphore("dsem%d" % i))
            self.dma_sems.append([s, 0, None])
        self.dma_rr = 0
        self.n_ops = 0
        self.final_events = []

    def sbuf(self, name, shape, dtype):
        return self.st.enter_context(self.nc.sbuf_tensor("sb_" + name, list(shape), dtype))

    def psum(self, name, shape, dtype=F32):
        return self.st.enter_context(self.nc.psum_tensor(name, list(shape), dtype))

    def _collect(self, es, reads, writes, skip_same):
        need = []
        for t in reads:
            if t.writer is not None:
                need.append(t.writer)
        for t in writes:
            if t.writer is not None:
                need.append(t.writer)
            need.extend(t.readers)
        waits = {}
        for ev in need:
            if skip_same and ev.eng is es:
                continue
            kk = id(ev.sem)
            if es.seen.get(kk, 0) >= ev.val:
                continue
            if kk not in waits or waits[kk][1] < ev.val:
                waits[kk] = (ev.sem, ev.val)
        for ev in need:
            for kk, v in ev.clock.items():
                if es.seen.get(kk, 0) < v:
                    es.seen[kk] = v
        return list(waits.values())

    def run_lanes(self, lanes):
        self.lane = None
        idx = [0] * len(lanes)
        left = sum(len(l_) for l_ in lanes)
        while left:
            for li, l_ in enumerate(lanes):
                if idx[li] < len(l_):
                    kind, a, kw = l_[idx[li]]
                    idx[li] += 1
                    left -= 1
                    if kind == "op":
                        self.op(*a)
                    else:
                        self.dma(*a, **kw)

    def op(self, eng, fn, reads=(), writes=()):
        if getattr(self, "lane", None) is not None:
            self.lane.append(("op", (eng, fn, list(reads), list(writes)), {}))
            return None
        es = self.E[eng]
        waits = self._collect(es, reads, writes, (eng == "pe") or (not SAME_ENGINE_SYNC))
        es.count += 1
        clock = dict(es.seen)
        clock[id(es.sem)] = es.count
        ev = Ev(es.sem, es.count, clock, es)
        es.ops.append((waits, fn, (es.sem, 1)))
        for t in reads:
            t.readers.append(ev)
        for t in writes:
            t.writer = ev
            t.readers = []
        self.n_ops += 1
        return ev

    def dma(self, eng, out, in_, reads=(), writes=(), final=False, **kw):
        if getattr(self, "lane", None) is not None:
            kw2 = dict(kw)
            kw2["final"] = final
            self.lane.append(("dma", (eng, out, in_, list(reads), list(writes)), kw2))
            return None
        es = self.E[eng]
        slot = self.dma_sems[self.dma_rr]
        self.dma_rr = (self.dma_rr + 1) % len(self.dma_sems)
        sem, cum, last = slot
        waits = self._collect(es, reads, writes, False)
        if last is not None and es.seen.get(id(sem), 0) < last.val:
            waits.append((sem, last.val))
            es.seen[id(sem)] = last.val
        cum += 16
        clock = dict(es.seen)
        clock[id(sem)] = cum
        ev = Ev(sem, cum, clock, None)
        slot[1] = cum
        slot[2] = ev

        def fn(h, out=out, in_=in_, kw=kw):
            return h.dma_start(out=out, in_=in_, **kw)
        es.ops.append((waits, fn, (sem, 16)))
        for t in reads:
            t.readers.append(ev)
        for t in writes:
            t.writer = ev
            t.readers = []
        if final:
            self.final_events.append(ev)
        self.n_ops += 1
        return ev

    def fence(self, old_tiles, new_tiles):
        evs = []
        for t in old_tiles:
            if t.writer is not None:
                evs.append(t.writer)
            evs.extend(t.readers)
        for t in new_tiles:
            t.readers = list(t.readers) + evs

    def mm(self, out, lhsT, rhs, start, stop, reads, writes, **kw):
        return self.op("pe", lambda h: h.matmul(out, lhsT, rhs, start=start, stop=stop, **kw), reads, writes)

    def tr(self, out, in_, ident, reads, writes):
        return self.op("pe", lambda h: h.transpose(out, in_, ident), reads, writes)

    def act(self, out, in_, func, reads, writes, **kw):
        return self.op("act", lambda h: h.activation(out=out, in_=in_, func=func, **kw), reads, writes)

    def tt(self, eng, out, in0, in1, op, reads, writes):
        return self.op(eng, lambda h: h.tensor_tensor(out=out, in0=in0, in1=in1, op=op), reads, writes)

    def ts(self, eng, out, in0, s1, s2, op0, op1, reads, writes):
        return self.op(eng, lambda h: h.tensor_scalar(out=out, in0=in0, scalar1=s1, scalar2=s2, op0=op0, op1=op1), reads, writes)

    def stt(self, out, in0, scalar, in1, op0, op1, reads, writes):
        return self.op("dve", lambda h: h.scalar_tensor_tensor(out=out, in0=in0, scalar=scalar, in1=in1, op0=op0, op1=op1), reads, writes)

    def copy(self, eng, out, in_, reads, writes):
        if eng == "act":
            return self.act(out, in_, AF.Copy, reads, writes)
        return self.op(eng, lambda h: h.tensor_copy(out=out, in_=in_), reads, writes)

    def memset(self, eng, ap, val, writes):
        return self.op(eng, lambda h: h.memset(ap, val), (), writes)

    def emit(self):
        nc = self.nc
        fin = []
        for ev in self.final_events:
            fin.append((ev.sem, ev.val))
        for name, es in self.E.items():
            if name != "sp" and es.count > 0:
                fin.append((es.sem, es.count))
        for slot in self.dma_sems:
            if slot[2] is not None:
                fin.append((slot[0], slot[1]))
        with nc.Block() as block:
            def replay(es, h, extra=()):
                for waits, fn, inc in es.ops:
                    for (s, v) in waits:
                        h.wait_ge(s, v)
                    ins = fn(h)
                    ins.then_inc(inc[0], inc[1])
                for (s, v) in extra:
                    h.wait_ge(s, v)

            @block.tensor
            def _(h):
                replay(self.E["pe"], h)

            @block.scalar
            def _(h):
                replay(self.E["act"], h)

            @block.vector
            def _(h):
                replay(self.E["dve"], h)

            @block.gpsimd
            def _(h):
                replay(self.E["pool"], h)

            @block.sync
            def _(h):
                replay(self.E["sp"], h, extra=fin)

    def close(self):
        self.st.close()
D = 1024
NCORES = 8
DEPTH = 2
NT = 1024
P_LEN = 512
D_IN = 1888
D_FF = 2816
NFT = D_FF // 128
EPS = 1e-6
TWO_PI = 2.0 * math.pi
G_OFF = dict(na_q=0, na_k=64, gq_q=128, gq_k=192, qa=256, kva=448, mq=576, mk=672)
NG = 768


def _pc(v, c):
    return np.ascontiguousarray(np.asarray(v, np.float32).reshape(c, 128).T)


def _rope_tables(t, dim):
    def ang1(pos, d):
        half = d // 2
        inv = (np.float32(10000.0) ** (-np.arange(half, dtype=np.float32) / np.float32(half))).astype(np.float32)
        a = pos.astype(np.float32)[:, None] * inv[None, :]
        return np.concatenate([a, a], axis=-1)
    pos = np.arange(t)
    ang = np.concatenate([ang1(pos // 64, dim // 2), ang1(pos % 64, dim // 2)], axis=-1).astype(np.float32)
    cos = np.cos(ang).astype(np.float32)
    sin = np.sin(ang).astype(np.float32)
    q = dim // 4
    sgn = np.tile(np.concatenate([-np.ones(q), np.ones(q)]), 2).astype(np.float32)
    sinS = sin * sgn[None, :]
    def tm(a):
        return np.ascontiguousarray(a.reshape(8, 128, dim).transpose(1, 0, 2))
    return tm(cos), tm(sinS)


def _constants():
    c = {}
    c["ident"] = np.eye(128, dtype=np.float32)
    sel = np.zeros((128, 2, 8, 128), np.float32)
    for p in range(128):
        for par in range(2):
            r = p % 32 - 16 * par
            if 0 <= r < 16:
                for j in range(8):
                    sel[p, par, j, 16 * j + r] = 1.0
    c["sel32"] = sel
    selT = np.zeros((128, 2, 8, 32), np.float32)
    for jj in range(8):
        for co in range(16):
            for par in range(2):
                selT[jj * 16 + co, par, jj, 16 * par + co] = 1.0
    c["selT32"] = selT
    s_idx = np.arange(128) // 16
    mL = (s_idx[:, None] <= s_idx[None, :]).astype(np.float32)
    mU = (s_idx[:, None] >= s_idx[None, :]).astype(np.float32)
    c["masks"] = np.ascontiguousarray(np.stack([mL, mU], axis=1))
    qc = 63 - (np.arange(128) % 64)
    cs = np.clip(qc - 8, 0, 48)
    kc = np.arange(64)
    valid = ((kc[None, :] >= cs[:, None]) & (kc[None, :] < cs[:, None] + 16)).astype(np.float32)
    c["na_vmask"] = np.ascontiguousarray(np.stack([valid, (1.0 - valid) * np.float32(-30000.0)], axis=1))
    aid = np.zeros((128, 128), np.float32)
    for p_ in range(128):
        aid[p_, (p_ // 64) * 64 + 63 - (p_ % 64)] = 1.0
    c["antiid"] = aid
    c["pvals"] = np.ascontiguousarray(np.broadcast_to(np.arange(-8, 9, dtype=np.float32)[None, :], (128, 17)))
    c["posv"] = np.ascontiguousarray(np.broadcast_to(np.arange(0, 129, dtype=np.float32)[None, :], (128, 129)))
    cg, sg = _rope_tables(1024, 64)
    cm, sm = _rope_tables(1024, 32)
    c["rope_g"] = np.ascontiguousarray(np.stack([cg, sg], axis=1))
    c["rope_m"] = np.ascontiguousarray(np.stack([cm, sm], axis=1))
    return c


def _common_inputs(inp):
    f = lambda a: np.ascontiguousarray(np.asarray(a, np.float32))
    c = _constants()
    c["ada_w"] = f(inp["ada_w"])
    c["ada_b"] = np.stack([_pc(inp["ada_b"][l], 48) for l in range(DEPTH)])
    c["normg"] = np.stack([np.stack([_pc(inp["norm1_g"][l], 8), _pc(inp["norm2_g"][l], 8)], axis=1) for l in range(DEPTH)])
    c["w_in"] = f(inp["w_in"])
    c["w_out"] = f(inp["w_out"])
    c["w_up"] = f(inp["ffn_w_up"])
    c["w_down"] = f(inp["ffn_w_down"])
    cw = np.asarray(inp["ffn_conv_w"], np.float32)
    c["convw"] = np.ascontiguousarray(cw.reshape(DEPTH, 3, 44, 128).transpose(0, 3, 2, 1))
    c["convb"] = np.ascontiguousarray(np.asarray(inp["ffn_conv_b"], np.float32).reshape(DEPTH, 44, 128).transpose(0, 2, 1))
    c["gains"] = np.ascontiguousarray(np.concatenate(
        [np.asarray(inp[n], np.float32) for n in ("na_qn", "na_kn", "gq_qn", "gq_kn", "mla_qa_g", "mla_kva_g", "mla_qn", "mla_kn")],
        axis=1))
    c["w_uq"] = f(inp["mla_w_uq"])
    c["w_ukv"] = f(inp["mla_w_ukv"])
    c["w_glu"] = f(inp["s5_w_glu"])
    c["bglu"] = np.stack([_pc(inp["s5_b_glu"][l], 2) for l in range(DEPTH)])
    rp = np.zeros((DEPTH, 4, 15, 159), np.float32)
    rp[..., 64:95] = np.asarray(inp["na_rpb"], np.float32)
    c["rpb"] = rp

    def gn(a):
        a = np.asarray(a, np.float32).reshape(DEPTH, 2, 8, 2, 64)
        return np.ascontiguousarray(a.transpose(0, 3, 4, 1, 2).reshape(DEPTH, 128, 16))
    c["s5_lam"] = np.ascontiguousarray(np.stack([gn(inp["s5_lam_re"]), gn(inp["s5_lam_im"])], axis=2))
    ldt = np.asarray(inp["s5_log_dt"], np.float32)
    c["s5_logdt"] = gn(np.broadcast_to(ldt[..., None], (DEPTH, 2, 16, 64)))

    def gnc(a):
        a = np.asarray(a, np.float32).reshape(DEPTH, 2, 8, 2, 64, 16)
        return np.ascontiguousarray(a.transpose(0, 3, 4, 1, 2, 5).reshape(DEPTH, 128, 16, 16))
    c["s5_B"] = np.ascontiguousarray(np.stack([gnc(inp["s5_b_re"]), gnc(inp["s5_b_im"])], axis=2))
    ct = lambda a: np.asarray(a, np.float32).transpose(0, 1, 2, 4, 3)
    c["s5_C"] = np.ascontiguousarray(np.stack([gnc(ct(inp["s5_c_re"])), gnc(ct(inp["s5_c_im"]))], axis=2))
    dsk = np.asarray(inp["s5_d"], np.float32).reshape(DEPTH, 16, 16)
    c["s5_dsk"] = np.ascontiguousarray(np.broadcast_to(dsk.transpose(0, 2, 1)[:, None, :, :], (DEPTH, 8, 16, 16)).reshape(DEPTH, 128, 16))
    return c


def _core_inputs(inp, core):
    def xT(x):
        return np.ascontiguousarray(np.asarray(x, np.float32).T.reshape(8, 128, NT).transpose(1, 0, 2))
    b = core // 2
    m = {}
    m["xT_p"] = xT(np.asarray(inp["x_prompt"])[4 * core:4 * core + 4].reshape(NT, D))
    m["xT_s"] = xT(np.asarray(inp["x_sample"])[b])
    m["cvec"] = np.ascontiguousarray(np.stack([_pc(inp["c_ctx"], 8), _pc(np.asarray(inp["c"])[b], 8)], axis=2))
    m["c_na_kT"] = np.ascontiguousarray(np.asarray(inp["cache_na_k"], np.float32)[b].transpose(0, 1, 3, 2))
    m["c_na_v"] = np.ascontiguousarray(np.asarray(inp["cache_na_v"], np.float32)[b])
    m["c_gq_kT"] = np.ascontiguousarray(np.asarray(inp["cache_gqa_k"], np.float32)[b].transpose(0, 1, 3, 2))
    m["c_gq_v"] = np.ascontiguousarray(np.asarray(inp["cache_gqa_v"], np.float32)[b])
    m["c_ckv"] = np.ascontiguousarray(np.asarray(inp["cache_mla_ckv"], np.float32)[b])
    m["c_krope"] = np.ascontiguousarray(np.asarray(inp["cache_mla_krope"], np.float32)[b])
    st = np.asarray(inp["state_s5"], np.float32)[b].reshape(DEPTH, 2, 2, 8, 2, 64)
    m["s5_h0"] = np.ascontiguousarray(st.transpose(0, 4, 5, 1, 2, 3).reshape(DEPTH, 128, 2, 2, 8))
    return m
ARENA_BYTES = 104 * 1024


class Arena:
    def __init__(self, k, nbytes):
        self.k = k
        self.t = k.sbuf("arena", [128, nbytes // 2], BF16)
        self.n = nbytes // 2
        self.off = 0
        self.cur = []
        self.pending = []

    def reset(self):
        evs = []
        for t in self.cur:
            if t.writer is not None:
                evs.append(t.writer)
            evs.extend(t.readers)
        self.pending = self._dedupe(evs)
        self.cur = []
        self.off = 0

    @staticmethod
    def _dedupe(evs):
        best = {}
        for ev in evs:
            kk = id(ev.sem)
            if kk not in best or best[kk].val < ev.val:
                best[kk] = ev
        return list(best.values())

    def mark(self):
        return (self.off, len(self.cur))

    def reset_to(self, mark):
        evs = []
        for t in self.cur[mark[1]:]:
            if t.writer is not None:
                evs.append(t.writer)
            evs.extend(t.readers)
        self.pending = self._dedupe(list(self.pending) + evs)
        self.cur = self.cur[:mark[1]]
        self.off = mark[0]

    def alloc(self, free_shape, dtype, name=""):
        n = 1
        for s in free_shape:
            n *= s
        nb16 = n * (2 if dtype == F32 else 1)
        self.off = (self.off + 15) // 16 * 16
        assert self.off + nb16 <= self.n, ("arena overflow", name, self.off, nb16, self.n)
        ap = self.t[:, self.off:self.off + nb16]
        self.off += nb16
        if dtype == F32:
            ap = ap.bitcast(F32)
        if len(free_shape) == 2:
            ap = ap.rearrange("p (a b) -> p a b", a=free_shape[0])
        elif len(free_shape) == 3:
            ap = ap.rearrange("p (a b c) -> p a b c", a=free_shape[0], b=free_shape[1])
        elif len(free_shape) == 4:
            ap = ap.rearrange("p (a b c d) -> p a b c d", a=free_shape[0], b=free_shape[1], c=free_shape[2])
        t = T(ap, name)
        t.readers = list(self.pending)
        self.cur.append(t)
        return ap, t


class Prog:
    def __init__(self, shapes, groups=("ctx", "lat"), nlayers=2, dbg=False):
        self.groups = groups
        self.nlayers = nlayers
        nc = bass.Bass("TRN2", target_bir_lowering=False)
        self.nc = nc
        self.dr = {n: nc.dram_tensor(n, list(s), F32, kind="ExternalInput").ap() for n, s in shapes.items()}
        oshapes = dict(yT_p=[128, 8, NT], yT_s=[128, 8, NT], o_na_k=[4, 2, 4, 256, 64], o_na_v=[4, 2, 4, 256, 64],
                       o_s5=[4, 2, 2, 2, 16, 64], o_gq_k=[4, 2, 2, 256, 64], o_gq_v=[4, 2, 2, 256, 64],
                       o_ckv=[4, 2, 256, 128], o_krope=[4, 2, 256, 32])
        self.do = {n: nc.dram_tensor(n, s, F32, kind="ExternalOutput").ap() for n, s in oshapes.items()}
        k = self.k = K(nc)
        self.ps = [k.psum("ps%d" % i, [128, 512]) for i in range(8)]
        self.pt = [T(self.ps[i][:], "ps%d" % i) for i in range(8)]
        sb = k.sbuf
        self.xT = sb("xT", [128, 8, NT], F32)
        self.xT_t = [[T() for _ in range(2)] for _ in range(8)]
        self.hT = sb("hT", [128, 8, NT], BF16)
        self.hT_t = [[T() for _ in range(2)] for _ in range(8)]
        self.mixT = sb("mixT", [128, 8, NT], BF16)
        self.mix_t = [[T() for _ in range(2)] for _ in range(8)]
        self.wout = sb("wout", [128, 8, D], BF16)
        self.wout_t = T()
        self.arena = Arena(k, ARENA_BYTES)
        self.load_consts()
        self.ada_phase()
        for grp in groups:
            self.load_x(grp)
            for l in range(nlayers):
                self.layer(grp, l)
            self.store_x(grp)
        k.emit()
        k.close()

    def load_consts(self):
        k, dr = self.k, self.dr
        sb = k.sbuf
        self.ident = sb("ident", [128, 128], F32); self.ident_t = T()
        k.dma("sp", self.ident[:], dr["ident"], writes=[self.ident_t])
        self.ident16 = sb("ident16", [128, 128], BF16); self.ident16_t = T()
        k.copy("dve", self.ident16[:], self.ident[:], [self.ident_t], [self.ident16_t])
        self.ones16 = sb("ones16", [128, 128], BF16); self.ones_t = T()
        k.memset("dve", self.ones16[:], 1.0, [self.ones_t])
        self.sel = sb("sel32", [128, 2, 8, 128], BF16); self.sel_t = T()
        k.dma("pool", self.sel[:], dr["sel32"], writes=[self.sel_t])
        self.selT = sb("selT32", [128, 2, 8, 32], BF16); self.selT_t = T()
        k.dma("pool", self.selT[:], dr["selT32"], writes=[self.selT_t])
        self.masks = sb("masks", [128, 2, 128], F32); self.masks_t = T()
        k.dma("sp", self.masks[:], dr["masks"], writes=[self.masks_t])
        self.vmask = sb("na_vmask", [128, 2, 64], F32); self.vmask_t = T()
        k.dma("sp", self.vmask[:], dr["na_vmask"], writes=[self.vmask_t])
        self.anti16 = sb("anti16", [128, 128], BF16); self.anti16_t = T()
        k.dma("pool", self.anti16[:], dr["antiid"], writes=[self.anti16_t])
        self.pvals = sb("pvals", [128, 17], F32); self.pvals_t = T()
        k.dma("sp", self.pvals[:], dr["pvals"], writes=[self.pvals_t])
        self.posv = sb("posv", [128, 129], F32); self.posv_t = T()
        k.dma("sp", self.posv[:], dr["posv"], writes=[self.posv_t])
        self.rope_g = sb("rope_g", [128, 2, 8, 64], F32); self.rope_g_t = T()
        k.dma("sp", self.rope_g[:], dr["rope_g"], writes=[self.rope_g_t])
        self.rope_m = sb("rope_m", [128, 2, 8, 32], F32); self.rope_m_t = T()
        k.dma("sp", self.rope_m[:], dr["rope_m"], writes=[self.rope_m_t])
        self.normg = sb("normg", [128, DEPTH, 2, 8], F32); self.normg_t = T()
        self.convw = sb("convw", [128, DEPTH, 44, 3], F32); self.convw_t = T()
        self.convb = sb("convb", [128, DEPTH, 44], F32); self.convb_t = T()
        self.bglu = sb("bglu", [128, DEPTH, 2], F32); self.bglu_t = T()
        self.adab = sb("adab", [128, DEPTH, 48], F32); self.adab_t = T()
        for l in range(DEPTH):
            k.dma("sp", self.normg[:, l], dr["normg"][l], writes=[self.normg_t])
            k.dma("sp", self.convw[:, l], dr["convw"][l], writes=[self.convw_t])
            k.dma("sp", self.convb[:, l], dr["convb"][l], writes=[self.convb_t])
            k.dma("sp", self.bglu[:, l], dr["bglu"][l], writes=[self.bglu_t])
            k.dma("sp", self.adab[:, l], dr["ada_b"][l], writes=[self.adab_t])
        self.eps = sb("eps_c", [128, 1], F32); self.eps_t = T()
        k.memset("dve", self.eps[:], EPS, [self.eps_t])
        self.gains = sb("gains", [128, NG], F32); self.gains_t = T()
        self.mods = sb("mods", [128, DEPTH, 2, 6, 8], F32); self.mods_t = T()

    def ada_phase(self):
        k, dr, ar = self.k, self.dr, self.arena
        ar.reset()
        cv, cv_t = ar.alloc([8, 2], F32, "cv")
        sc, sc_t = ar.alloc([8, 2], F32, "silu_c")
        st = [ar.alloc([8, 512], F32, "ada_st%d" % i) for i in range(2)]
        raw, raw_t = ar.alloc([48, 2], F32, "mods_raw")
        tmp, tmp_t = ar.alloc([2, 8], F32, "mods_tmp")
        k.dma("sp", cv, dr["cvec"], writes=[cv_t])
        k.act(sc, cv, AF.Silu, [cv_t], [sc_t])
        pb, pb_t = self.ps[0], self.pt[0]
        for l in range(self.nlayers):
            for jg in range(12):
                sap, s_t = st[jg % 2]
                k.dma("sp", sap, dr["ada_w"][l][:, jg * 512:(jg + 1) * 512].rearrange("(c p) n -> p c n", p=128), writes=[s_t])
                for jj in range(4):
                    j = jg * 4 + jj
                    for c in range(8):
                        k.mm(pb[:, j * 2:j * 2 + 2], sap[:, c, jj * 128:(jj + 1) * 128], sc[:, c, :], c == 0, c == 7,
                             [s_t, sc_t], [pb_t])
            k.tt("dve", raw, pb[:, 0:96].rearrange("p (j v) -> p j v", v=2),
                 self.adab[:, l, :].unsqueeze(2).broadcast_to([128, 48, 2]), ALU.add, [pb_t, self.adab_t], [raw_t])
            md = self.mods
            for cvi in range(2):
                r6 = raw[:, :, cvi].rearrange("p (m c) -> p m c", c=8)
                k.copy("dve", md[:, l, cvi, 0:6:3, :], r6[:, 0:6:3, :], [raw_t], [self.mods_t])
                k.copy("dve", md[:, l, cvi, 2:6:3, :], r6[:, 2:6:3, :], [raw_t], [self.mods_t])
                k.ts("dve", tmp, r6[:, 1:6:3, :], 1.0, None, ALU.add, ALU.bypass, [raw_t], [tmp_t])
                k.tt("dve", md[:, l, cvi, 1:6:3, :], tmp, self.normg[:, l, :, :], ALU.mult, [tmp_t, self.normg_t], [self.mods_t])

    def load_x(self, grp):
        k = self.k
        src = self.dr["xT_p" if grp == "ctx" else "xT_s"]
        for c in range(8):
            k.dma("sp", self.xT[:, c, :], src[:, c, :], writes=[self.xT_t[c][0], self.xT_t[c][1]])

    def store_x(self, grp):
        k = self.k
        dst = self.do["yT_p" if grp == "ctx" else "yT_s"]
        for c in range(8):
            k.dma("sp", dst[:, c, :], self.xT[:, c, :], reads=[self.xT_t[c][0], self.xT_t[c][1]], final=True)

    def norm_mod(self, l, cvi, which):
        k, ar = self.k, self.arena
        sq, sq_t = ar.alloc([8, 512], BF16, "nm_sq")
        rs, rs_t = ar.alloc([512], F32, "nm_rs")
        rstd, rstd_t = ar.alloc([512], F32, "nm_rstd")
        tmps = [ar.alloc([512], F32, "nm_tmp%d" % i) for i in range(2)]
        sh_slot, gm_slot = (0, 1) if which == 0 else (3, 4)
        pb, pb_t = self.ps[7], self.pt[7]
        for b in range(2):
            bs = slice(b * 512, (b + 1) * 512)
            xts = [self.xT_t[c][b] for c in range(8)]
            k.act(sq, self.xT[:, :, bs], AF.Square, xts, [sq_t])
            for c in range(8):
                k.mm(pb[:, :], self.ones16[:], sq[:, c, :], c == 0, c == 7, [sq_t, self.ones_t], [pb_t])
            k.act(rs, pb[:, :], AF.Sqrt, [pb_t, self.eps_t], [rs_t], scale=1.0 / D, bias=self.eps[:])
            k.op("dve", lambda h, o=rstd, i=rs: h.reciprocal(out=o, in_=i), [rs_t], [rstd_t])
            for c in range(8):
                tp, tp_t = tmps[c % 2]
                k.stt(tp, self.xT[:, c, bs], self.mods[:, l, cvi, gm_slot, c:c + 1], rstd, ALU.mult, ALU.mult,
                      [self.xT_t[c][b], self.mods_t, rstd_t], [tp_t])
                k.act(self.hT[:, c, bs], tp, AF.Identity, [tp_t, self.mods_t], [self.hT_t[c][b]],
                      bias=self.mods[:, l, cvi, sh_slot, c:c + 1], scale=1.0)

    def rms_heads(self, src, src_ts, nh, dh, g_bc, g_t, dst, dst_ts, scr, mul_eng="pool", dst2=None):
        k = self.k
        n = nh * dh
        sq, sq_t = scr["sq"]
        ss, ss_t = scr["ss"]
        rstd, rstd_t = scr["rstd"]
        tmp, tmp_t = scr["tmp"]
        sq3 = sq[:, 0:n].rearrange("p (h d) -> p h d", h=nh)
        tmp3 = tmp[:, 0:n].rearrange("p (h d) -> p h d", h=nh)
        k.act(sq3, src, AF.Square, src_ts, [sq_t])
        k.op("dve", lambda h: h.tensor_reduce(out=ss[:, 0:nh], in_=sq3, axis=AX.X, op=ALU.add), [sq_t], [ss_t])
        k.act(ss[:, 0:nh], ss[:, 0:nh], AF.Sqrt, [ss_t, self.eps_t], [ss_t], scale=1.0 / dh, bias=self.eps[:])
        k.op("dve", lambda h: h.reciprocal(out=rstd[:, 0:nh], in_=ss[:, 0:nh]), [ss_t], [rstd_t])
        k.tt("dve", tmp3, src, rstd[:, 0:nh].unsqueeze(2).broadcast_to([128, nh, dh]), ALU.mult, src_ts + [rstd_t], [tmp_t])
        k.tt(mul_eng, dst, tmp3, g_bc, ALU.mult, [tmp_t, g_t], dst_ts)
        if dst2 is not None:
            d2, d2_ts, hs = dst2
            k.tt("dve", d2, tmp3[:, hs, :], g_bc[:, hs, :], ALU.mult, [tmp_t, g_t], d2_ts)

    def rope(self, x, x_ts, nh, dr, tab, tab_t, tt, out, out_ts, scr):
        k = self.k
        q = dr // 4
        t1, t1_t = scr["r1"]
        t2, t2_t = scr["r2"]
        n = nh * dr
        t13 = t1[:, 0:n].rearrange("p (h d) -> p h d", h=nh)
        t23 = t2[:, 0:n].rearrange("p (h d) -> p h d", h=nh)
        cos = tab[:, 0, tt, :]
        sin = tab[:, 1, tt, :]
        k.tt("dve", t13, x, cos.unsqueeze(1).broadcast_to([128, nh, dr]), ALU.mult, x_ts + [tab_t], [t1_t])
        for b in range(2):
            xs = x[:, :, b * 2 * q:(b + 1) * 2 * q].rearrange("p h (f q) -> p h f q", f=2)[:, :, ::-1, :]
            sb_ = sin[:, b * 2 * q:(b + 1) * 2 * q].rearrange("p (f q) -> p f q", f=2).unsqueeze(1).broadcast_to([128, nh, 2, q])
            ob = t23[:, :, b * 2 * q:(b + 1) * 2 * q].rearrange("p h (f q) -> p h f q", f=2)
            k.tt("pool", ob, xs, sb_, ALU.mult, x_ts + [tab_t], [t2_t])
        k.tt("dve", out, t13, t23, ALU.add, [t1_t, t2_t], out_ts)

    def tok_scratch(self, rope=True, n=512):
        ar = self.arena
        s = {}
        s["sq"] = ar.alloc([n], F32, "sq")
        s["ss"] = ar.alloc([8], F32, "ss")
        s["rstd"] = ar.alloc([8], F32, "rstd")
        s["tmp"] = ar.alloc([n], F32, "tmp")
        if rope:
            s["r1"] = ar.alloc([n], F32, "r1")
            s["r2"] = ar.alloc([n], F32, "r2")
        return s

    def load_win(self, l, c0, c1, name):
        k, ar = self.k, self.arena
        w, w_t = ar.alloc([8, c1 - c0], BF16, name)
        src = self.dr["w_in"][l][:, c0:c1].rearrange("(c p) n -> p c n", p=128)
        for c in range(0, 8, 2):
            k.dma("pool", w[:, c:c + 2, :], src[:, c:c + 2, :], writes=[w_t])
        return w, w_t

    def gains_bc(self, l):
        k = self.k
        k.dma("sp", self.gains[:], self.dr["gains"][l:l + 1, :].broadcast_to([128, NG]), writes=[self.gains_t])

    def attention(self, units, Pbufs, rD):
        k = self.k
        sbanks = [0, 1, 2, 3]
        sb_i = 0
        pb_i = 0
        for ui, u in enumerate(units):
            nq = u["nq"]
            per = max(1, 512 // nq)
            chunks = u["chunks"]
            groups = [chunks[i:i + per] for i in range(0, len(chunks), per)]
            ob, db = (4, 5) if ui % 2 == 0 else (6, 7)
            psO, psD = self.ps[ob], self.ps[db]
            r0 = u["rows"]
            rsl = slice(r0, r0 + 64)
            nch = len(chunks)
            done = [0]

            def do_s(grp_chunks):
                nonlocal sb_i, pb_i
                bank = sbanks[sb_i % 4]
                sb_i += 1
                pS, pS_t = self.ps[bank], self.pt[bank]
                for j, (KT, V, k_ts, bias) in enumerate(grp_chunks):
                    k.mm(pS[:, j * nq:(j + 1) * nq], KT, u["QT"], True, bias is None, u["q_ts"] + k_ts, [pS_t])
                    if bias is not None:
                        k.mm(pS[:, j * nq:(j + 1) * nq], bias[0], bias[1], False, True, bias[2], [pS_t])
                Pb, Pb_t = Pbufs[pb_i % len(Pbufs)]
                pb_i += 1
                cols = len(grp_chunks) * nq
                k.act(Pb[:, 0:cols], pS[:, 0:cols], AF.Exp, [pS_t], [Pb_t])
                return Pb, Pb_t

            def do_pv(grp_chunks, Pb, Pb_t):
                for j, (KT, V, k_ts, bias) in enumerate(grp_chunks):
                    first = done[0] == 0
                    last = done[0] == nch - 1
                    k.mm(psO[rsl, 0:nq], V, Pb[:, j * nq:(j + 1) * nq], first, last, [Pb_t] + k_ts, [self.pt[ob]])
                    k.mm(psD[rsl, 0:nq], self.ones16[:, 0:64], Pb[:, j * nq:(j + 1) * nq], first, last, [Pb_t, self.ones_t], [self.pt[db]])
                    done[0] += 1

            pend = []
            for g in groups:
                pend.append((g,) + do_s(g))
                if len(pend) > 2:
                    do_pv(*pend.pop(0))
            while pend:
                do_pv(*pend.pop(0))
            rd, rd_t = rD
            k.op("dve", lambda h, o=rd[rsl, 0:nq], i=psD[rsl, 0:nq]: h.reciprocal(out=o, in_=i), [self.pt[db]], [rd_t])
            k.tt("dve", u["out"], psO[rsl, 0:nq], rd[rsl, 0:nq], ALU.mult, [self.pt[ob], rd_t], u["out_ts"])

    def mixer_na(self, grp, l):
        k, ar, dr = self.k, self.arena, self.dr
        lat = grp == "lat"
        ar.reset_to(self.mark_all)
        win, win_t = self.wins["na"]
        scrs = [self.tok_scratch(rope=False), self.tok_scratch(rope=False)]
        gbc, gbc_t = ar.alloc([8, 64], F32, "g_na")
        QKT, QKT_t = ar.alloc([4, NT], BF16, "QKT_na")
        Vn, Vn_t = ar.alloc([8, 256], BF16, "V_na")
        nrm = [ar.alloc([8, 64], F32, "na_nrm%d" % i) for i in range(2)]
        qk16 = [ar.alloc([512], BF16, "na_qk16%d" % i) for i in range(2)]
        v32 = [ar.alloc([256], F32, "na_v32%d" % i) for i in range(2)]
        Pbufs = [ar.alloc([512], BF16, "na_P%d" % i) for i in range(4)]
        rD = ar.alloc([512], F32, "na_rD")
        scale = 64 ** -0.5
        k.ts("dve", gbc[:, 0:4, :], self.gains[:, G_OFF["na_q"]:G_OFF["na_q"] + 64].unsqueeze(1).broadcast_to([128, 4, 64]),
             scale, None, ALU.mult, ALU.bypass, [self.gains_t], [gbc_t])
        k.copy("dve", gbc[:, 4:8, :], self.gains[:, G_OFF["na_k"]:G_OFF["na_k"] + 64].unsqueeze(1).broadcast_to([128, 4, 64]),
               [self.gains_t], [gbc_t])
        def prep(tt):
            scr = scrs[tt % 2]
            ts_ = slice(tt * 128, (tt + 1) * 128)
            b = tt // 4
            za, zb = (0, 1) if tt % 2 == 0 else (2, 3)
            trb = 4 + tt % 2
            hts = [self.hT_t[c][b] for c in range(8)]
            for c in range(8):
                k.mm(self.ps[za][:, :], self.hT[:, c, ts_], win[:, c, 0:512], c == 0, c == 7, [hts[c], win_t], [self.pt[za]])
            for c in range(8):
                k.mm(self.ps[zb][:, 0:256], self.hT[:, c, ts_], win[:, c, 512:768], c == 0, c == 7, [hts[c], win_t], [self.pt[zb]])
            nr, nr_t = nrm[tt % 2]
            q16, q16_t = qk16[tt % 2]
            vv, vv_t = v32[tt % 2]
            self.rms_heads(self.ps[za][:, :].rearrange("p (h d) -> p h d", h=8), [self.pt[za]], 8, 64, gbc, gbc_t,
                           q16.rearrange("p (h d) -> p h d", h=8), [q16_t], scr, mul_eng="dve",
                           dst2=None if lat else (nr[:, 4:8, :], [nr_t], slice(4, 8)))
            if not lat:
                k.copy("act", vv, self.ps[zb][:, 0:256], [self.pt[zb]], [vv_t])
            k.copy("act", Vn[:, tt, :], self.ps[zb][:, 0:256], [self.pt[zb]], [Vn_t])
            if not lat:
                s, t0 = tt // 2, (tt % 2) * 128
                k.dma("sp", self.do["o_na_k"][s, l, :, t0:t0 + 128, :].rearrange("h t d -> t h d"), nr[:, 4:8, :], reads=[nr_t], final=True)
                k.dma("sp", self.do["o_na_v"][s, l, :, t0:t0 + 128, :].rearrange("h t d -> t h d"),
                      vv.rearrange("p (h d) -> p h d", h=4), reads=[vv_t], final=True)
            pT = self.ps[trb][:].bitcast(BF16)
            for i in range(4):
                k.tr(pT[:, i * 128:(i + 1) * 128], q16[:, i * 128:(i + 1) * 128], self.ident16[:], [q16_t, self.ident16_t], [self.pt[trb]])
            k.copy("dve", QKT[:, :, ts_], pT[:, 0:512].rearrange("p (i t) -> p i t", i=4), [self.pt[trb]], [QKT_t])
        for tp_ in range(4):
            lanes = []
            for tt in (2 * tp_, 2 * tp_ + 1):
                k.lane = []
                lanes.append(k.lane)
                prep(tt)
            k.run_lanes(lanes)
        if not lat:
            units = []
            for s in range(4):
                for h in range(4):
                    r0 = (h % 2) * 64
                    rs_ = slice(r0, r0 + 64)
                    qs = slice(s * 256, (s + 1) * 256)
                    chunks = []
                    for j in range(2):
                        ks = slice(s * 256 + j * 128, s * 256 + (j + 1) * 128)
                        chunks.append((QKT[rs_, 2 + h // 2, ks], Vn[:, s * 2 + j, h * 64:(h + 1) * 64], [QKT_t, Vn_t], None))
                    units.append(dict(QT=QKT[rs_, h // 2, qs], q_ts=[QKT_t], nq=256, chunks=chunks, rows=r0,
                                      out=self.mixT[rs_, h // 2, qs], out_ts=[self.mix_t[h // 2][s // 2]]))
            self.attention(units, Pbufs, rD)
        else:
            self.na_latent(l, QKT, QKT_t, Vn, Vn_t, Pbufs, rD)

    def mixer_gq(self, grp, l):
        k, ar, dr = self.k, self.arena, self.dr
        lat = grp == "lat"
        ar.reset_to(self.mark_all)
        win, win_t = self.wins["gq"]
        scrs = [self.tok_scratch(rope=lat), self.tok_scratch(rope=lat)]
        gbc, gbc_t = ar.alloc([6, 64], F32, "g_gq")
        nkeys = NT + (P_LEN if lat else 0)
        QT, QT_t = ar.alloc([2, NT], BF16, "QT_gq")
        KT, KT_t = ar.alloc([nkeys], BF16, "KT_gq")
        Vg, Vg_t = ar.alloc([nkeys // 128, 128], BF16, "V_gq")
        nrm = [ar.alloc([6, 64], F32, "gq_nrm%d" % i) for i in range(2)]
        rp = [ar.alloc([6, 64], F32, "gq_rp%d" % i) for i in range(2)]
        qk16 = [ar.alloc([384], BF16, "gq_qk16%d" % i) for i in range(2)]
        v32 = [ar.alloc([128], F32, "gq_v32%d" % i) for i in range(2)]
        Pbufs = [ar.alloc([512], BF16, "gq_P%d" % i) for i in range(4)]
        rD = ar.alloc([512], F32, "gq_rD")
        scale = 64 ** -0.5
        k.ts("dve", gbc[:, 0:4, :], self.gains[:, G_OFF["gq_q"]:G_OFF["gq_q"] + 64].unsqueeze(1).broadcast_to([128, 4, 64]),
             scale, None, ALU.mult, ALU.bypass, [self.gains_t], [gbc_t])
        k.copy("dve", gbc[:, 4:6, :], self.gains[:, G_OFF["gq_k"]:G_OFF["gq_k"] + 64].unsqueeze(1).broadcast_to([128, 2, 64]),
               [self.gains_t], [gbc_t])
        if lat:
            st32, st32_t = ar.alloc([512], F32, "gq_stage")
            for kv in range(2):
                k.dma("sp", st32[kv * 64:(kv + 1) * 64, :], dr["c_gq_kT"][l, kv], writes=[st32_t])
            k.copy("pool", KT[:, NT:NT + P_LEN], st32, [st32_t], [KT_t])
            for j in range(4):
                k.dma("pool", Vg[:, 8 + j, :].rearrange("p (h d) -> p h d", h=2),
                      dr["c_gq_v"][l, :, j * 128:(j + 1) * 128, :].rearrange("h t d -> t h d"), writes=[Vg_t])
        def prep(tt):
            scr = scrs[tt % 2]
            ts_ = slice(tt * 128, (tt + 1) * 128)
            b = tt // 4
            za = 0 if tt % 2 == 0 else 2
            trb = 4 + tt % 2
            hts = [self.hT_t[c][b] for c in range(8)]
            for c in range(8):
                k.mm(self.ps[za][:, :], self.hT[:, c, ts_], win[:, c, :], c == 0, c == 7, [hts[c], win_t], [self.pt[za]])
            nr, nr_t = nrm[tt % 2]
            q16, q16_t = qk16[tt % 2]
            vv, vv_t = v32[tt % 2]
            self.rms_heads(self.ps[za][:, 0:384].rearrange("p (h d) -> p h d", h=6), [self.pt[za]], 6, 64, gbc, gbc_t, nr, [nr_t], scr,
                           mul_eng="dve")
            k.copy("act", vv, self.ps[za][:, 384:512], [self.pt[za]], [vv_t])
            k.copy("pool", Vg[:, tt, :], vv, [vv_t], [Vg_t])
            if not lat:
                s, t0 = tt // 2, (tt % 2) * 128
                k.dma("sp", self.do["o_gq_k"][s, l, :, t0:t0 + 128, :].rearrange("h t d -> t h d"), nr[:, 4:6, :], reads=[nr_t], final=True)
                k.dma("sp", self.do["o_gq_v"][s, l, :, t0:t0 + 128, :].rearrange("h t d -> t h d"),
                      vv.rearrange("p (h d) -> p h d", h=2), reads=[vv_t], final=True)
                src, src_t = nr, nr_t
            else:
                src, src_t = rp[tt % 2]
                self.rope(nr, [nr_t], 6, 64, self.rope_g, self.rope_g_t, tt, src, [src_t], scr)
            q16v = q16.rearrange("p (h d) -> p h d", h=6)
            k.copy("pool", q16v[:, 0:4, :].rearrange("p (b a) d -> p a b d", a=2), src[:, 0:4, :].rearrange("p (a b) d -> p a b d", a=2),
                   [src_t], [q16_t])
            k.copy("pool", q16v[:, 4:6, :], src[:, 4:6, :], [src_t], [q16_t])
            pT = self.ps[trb][:].bitcast(BF16)
            for i in range(3):
                k.tr(pT[:, i * 128:(i + 1) * 128], q16[:, i * 128:(i + 1) * 128], self.ident16[:], [q16_t, self.ident16_t], [self.pt[trb]])
            k.copy("dve", QT[:, :, ts_], pT[:, 0:256].rearrange("p (i t) -> p i t", i=2), [self.pt[trb]], [QT_t])
            k.copy("dve", KT[:, ts_], pT[:, 256:384], [self.pt[trb]], [KT_t])
        for tp_ in range(4):
            lanes = []
            for tt in (2 * tp_, 2 * tp_ + 1):
                k.lane = []
                lanes.append(k.lane)
                prep(tt)
            k.run_lanes(lanes)
        units = []
        if not lat:
            for s in range(4):
                for h in range(4):
                    kv = h // 2
                    rb = slice(kv * 64, kv * 64 + 64)
                    qs = slice(s * 256, (s + 1) * 256)
                    chunks = []
                    for j in range(2):
                        ks = slice(s * 256 + j * 128, s * 256 + (j + 1) * 128)
                        chunks.append((KT[rb, ks], Vg[:, s * 2 + j, kv * 64:(kv + 1) * 64], [KT_t, Vg_t], None))
                    r0 = (h % 2) * 64
                    units.append(dict(QT=QT[rb, h % 2, qs], q_ts=[QT_t], nq=256, chunks=chunks, rows=r0,
                                      out=self.mixT[r0:r0 + 64, 4 + h // 2, qs], out_ts=[self.mix_t[4 + h // 2][s // 2]]))
        else:
            for h in range(4):
                kv = h // 2
                rb = slice(kv * 64, kv * 64 + 64)
                for qb in range(2):
                    qs = slice(qb * 512, (qb + 1) * 512)
                    chunks = []
                    for j in range(12):
                        chunks.append((KT[rb, j * 128:(j + 1) * 128], Vg[:, j, kv * 64:(kv + 1) * 64], [KT_t, Vg_t], None))
                    r0 = (h % 2) * 64
                    units.append(dict(QT=QT[rb, h % 2, qs], q_ts=[QT_t], nq=512, chunks=chunks, rows=r0,
                                      out=self.mixT[r0:r0 + 64, 4 + h // 2, qs], out_ts=[self.mix_t[4 + h // 2][qb]]))
        self.attention(units, Pbufs, rD)
    def mixer_mla(self, grp, l):
        k, ar, dr = self.k, self.arena, self.dr
        lat = grp == "lat"
        ar.reset_to(self.mark_all)
        win, win_t = self.wins["mla"]
        scr_q = self.tok_scratch(rope=lat)
        scr_kv = self.tok_scratch(rope=lat)
        wuq, wuq_t = ar.alloc([2, 384], BF16, "wuq")
        wukv, wukv_t = ar.alloc([512], BF16, "wukv")
        k.dma("pool", wuq[:, 0, :], dr["w_uq"][l, 0:128, :], writes=[wuq_t])
        k.dma("pool", wuq[0:64, 1, :], dr["w_uq"][l, 128:192, :], writes=[wuq_t])
        k.dma("pool", wukv, dr["w_ukv"][l], writes=[wukv_t])
        gq, gq_t = ar.alloc([4, 96], F32, "g_mq")
        gk, gk_t = ar.alloc([4, 96], F32, "g_mk")
        scale = 96 ** -0.5
        k.ts("dve", gq, self.gains[:, G_OFF["mq"]:G_OFF["mq"] + 96].unsqueeze(1).broadcast_to([128, 4, 96]),
             scale, None, ALU.mult, ALU.bypass, [self.gains_t], [gq_t])
        k.copy("dve", gk, self.gains[:, G_OFF["mk"]:G_OFF["mk"] + 96].unsqueeze(1).broadcast_to([128, 4, 96]), [self.gains_t], [gk_t])
        g_qa = self.gains[:, G_OFF["qa"]:G_OFF["qa"] + 192]
        g_kva = self.gains[:, G_OFF["kva"]:G_OFF["kva"] + 128]
        nkt = 8 + (4 if lat else 0)
        QT, QT_t = ar.alloc([4, NT], BF16, "QT_mla")
        KT, KT_t = ar.alloc([4, nkt * 128], BF16, "KT_mla")
        Vm, Vm_t = ar.alloc([nkt, 256], BF16, "V_mla")
        cq16, cq16_t = ar.alloc([192], BF16, "cq16")
        cqT, cqT_t = ar.alloc([2, 128], BF16, "cqT")
        ckv32 = [ar.alloc([128], F32, "ckv32_%d" % i) for i in range(2)]
        ckv16, ckv16_t = ar.alloc([128], BF16, "ckv16")
        ckvT, ckvT_t = ar.alloc([128], BF16, "ckvT")
        kr32 = [ar.alloc([32], F32, "kr32_%d" % i) for i in range(2)]
        kcat, kcat_t = ar.alloc([4, 96], F32, "kcat")
        nq, nq_t = ar.alloc([4, 96], F32, "mla_nq")
        nk, nk_t = ar.alloc([4, 96], F32, "mla_nk")
        q16, q16_t = ar.alloc([4, 96], BF16, "mla_q16")
        k16, k16_t = ar.alloc([4, 96], BF16, "mla_k16")
        Pbufs = [ar.alloc([512], BF16, "mla_P%d" % i) for i in range(4)]
        rD = ar.alloc([512], F32, "mla_rD")

        def kv_path(kt, tt_rope, ckvn16_src_ts, kr_ap, kr_ts, zq, zkv, trb):
            pT = self.ps[trb][:].bitcast(BF16)
            k.tr(pT[:, 0:128], ckv16, self.ident16[:], [ckv16_t, self.ident16_t], [self.pt[trb]])
            k.copy("dve", ckvT, pT[:, 0:128], [self.pt[trb]], [ckvT_t])
            k.mm(self.ps[zkv][:, :], ckvT, wukv, True, True, [ckvT_t, wukv_t], [self.pt[zkv]])
            kv4 = self.ps[zkv][:, :].rearrange("p (h x) -> p h x", h=4)
            k.copy("act", kcat[:, :, 0:64], kv4[:, :, 0:64], [self.pt[zkv]], [kcat_t])
            k.copy("pool", kcat[:, :, 64:96], kr_ap.unsqueeze(1).broadcast_to([128, 4, 32]), kr_ts, [kcat_t])
            k.copy("act", Vm[:, kt, :].rearrange("p (h d) -> p h d", h=4), kv4[:, :, 64:128], [self.pt[zkv]], [Vm_t])
            self.rms_heads(kcat, [kcat_t], 4, 96, gk, gk_t, nk, [nk_t], scr_kv, mul_eng="dve")
            if tt_rope is not None:
                self.rope(nk[:, :, 64:96], [nk_t], 4, 32, self.rope_m, self.rope_m_t, tt_rope, nk[:, :, 64:96], [nk_t], scr_kv)
            k.copy("act", k16, nk, [nk_t], [k16_t])
            for h in range(4):
                k.tr(pT[0:96, 128 + h * 128:256 + h * 128], k16[:, h, :], self.ident16[:], [k16_t, self.ident16_t], [self.pt[trb]])
            k.copy("dve", KT[0:96, :, kt * 128:(kt + 1) * 128], pT[0:96, 128:640].rearrange("p (h t) -> p h t", h=4), [self.pt[trb]], [KT_t])

        for tt in range(8):
            ts_ = slice(tt * 128, (tt + 1) * 128)
            b = tt // 4
            za, zq, zkv = (0, 1, 2)
            trb = 4 + tt % 2
            hts = [self.hT_t[c][b] for c in range(8)]
            for c in range(8):
                k.mm(self.ps[za][:, 0:352], self.hT[:, c, ts_], win[:, c, :], c == 0, c == 7, [hts[c], win_t], [self.pt[za]])
            z = self.ps[za]
            lane_q, lane_kv = [], []
            k.lane = lane_q
            self.rms_heads(z[:, 0:192].unsqueeze(1), [self.pt[za]], 1, 192, g_qa.unsqueeze(1), self.gains_t,
                           cq16.unsqueeze(1), [cq16_t], scr_q, mul_eng="dve")
            pT = self.ps[trb][:].bitcast(BF16)
            k.tr(pT[:, 0:128], cq16[:, 0:128], self.ident16[:], [cq16_t, self.ident16_t], [self.pt[trb]])
            k.tr(pT[0:64, 128:256], cq16[:, 128:192], self.ident16[:], [cq16_t, self.ident16_t], [self.pt[trb]])
            k.copy("dve", cqT[:, 0, :], pT[:, 0:128], [self.pt[trb]], [cqT_t])
            k.copy("dve", cqT[0:64, 1, :], pT[0:64, 128:256], [self.pt[trb]], [cqT_t])
            k.mm(self.ps[zq][:, 0:384], cqT[:, 0, :], wuq[:, 0, :], True, False, [cqT_t, wuq_t], [self.pt[zq]])
            k.mm(self.ps[zq][:, 0:384], cqT[0:64, 1, :], wuq[0:64, 1, :], False, True, [cqT_t, wuq_t], [self.pt[zq]])
            self.rms_heads(self.ps[zq][:, 0:384].rearrange("p (h d) -> p h d", h=4), [self.pt[zq]], 4, 96, gq, gq_t, nq, [nq_t], scr_q, mul_eng="dve")
            if lat:
                self.rope(nq[:, :, 64:96], [nq_t], 4, 32, self.rope_m, self.rope_m_t, tt, nq[:, :, 64:96], [nq_t], scr_q)
            k.copy("act", q16, nq, [nq_t], [q16_t])
            trq = 6 + tt % 2
            pQ = self.ps[trq][:].bitcast(BF16)
            for h in range(4):
                k.tr(pQ[0:96, h * 128:(h + 1) * 128], q16[:, h, :], self.ident16[:], [q16_t, self.ident16_t], [self.pt[trq]])
            k.copy("dve", QT[0:96, :, ts_], pQ[0:96, 0:512].rearrange("p (h t) -> p h t", h=4), [self.pt[trq]], [QT_t])
            k.lane = lane_kv
            c32, c32_t = ckv32[tt % 2]
            kr, kr_t = kr32[tt % 2]
            self.rms_heads(z[:, 192:320].unsqueeze(1), [self.pt[za]], 1, 128, g_kva.unsqueeze(1), self.gains_t,
                           c32.unsqueeze(1), [c32_t], scr_kv, mul_eng="dve")
            k.copy("act", kr, z[:, 320:352], [self.pt[za]], [kr_t])
            k.copy("pool", ckv16, c32, [c32_t], [ckv16_t])
            if not lat:
                s, t0 = tt // 2, (tt % 2) * 128
                k.dma("sp", self.do["o_ckv"][s, l, t0:t0 + 128, :], c32, reads=[c32_t], final=True)
                k.dma("sp", self.do["o_krope"][s, l, t0:t0 + 128, :], kr, reads=[kr_t], final=True)
            kv_path(tt, tt if lat else None, None, kr, [kr_t], zq, zkv, 3)
            k.run_lanes([lane_q, lane_kv])
        if lat:
            for j in range(4):
                c32, c32_t = ckv32[j % 2]
                kr, kr_t = kr32[j % 2]
                k.dma("sp", c32, dr["c_ckv"][l, j * 128:(j + 1) * 128, :], writes=[c32_t])
                k.dma("sp", kr, dr["c_krope"][l, j * 128:(j + 1) * 128, :], writes=[kr_t])
                k.copy("pool", ckv16, c32, [c32_t], [ckv16_t])
                kv_path(8 + j, None, None, kr, [kr_t], 1, 2, 4 + j % 2)
        units = []
        if not lat:
            for s in range(4):
                for h in range(4):
                    qs = slice(s * 256, (s + 1) * 256)
                    chunks = []
                    for j in range(2):
                        ks = slice(s * 256 + j * 128, s * 256 + (j + 1) * 128)
                        chunks.append((KT[0:96, h, ks], Vm[:, s * 2 + j, h * 64:(h + 1) * 64], [KT_t, Vm_t], None))
                    r0 = (h % 2) * 64
                    units.append(dict(QT=QT[0:96, h, qs], q_ts=[QT_t], nq=256, chunks=chunks, rows=r0,
                                      out=self.mixT[r0:r0 + 64, 6 + h // 2, qs], out_ts=[self.mix_t[6 + h // 2][s // 2]]))
        else:
            for h in range(4):
                for qb in range(2):
                    qs = slice(qb * 512, (qb + 1) * 512)
                    chunks = []
                    for j in range(12):
                        chunks.append((KT[0:96, h, j * 128:(j + 1) * 128], Vm[:, j, h * 64:(h + 1) * 64], [KT_t, Vm_t], None))
                    r0 = (h % 2) * 64
                    units.append(dict(QT=QT[0:96, h, qs], q_ts=[QT_t], nq=512, chunks=chunks, rows=r0,
                                      out=self.mixT[r0:r0 + 64, 6 + h // 2, qs], out_ts=[self.mix_t[6 + h // 2][qb]]))
        self.attention(units, Pbufs, rD)

    def out_proj(self, l, cvi):
        k = self.k
        for b in range(2):
            bs = slice(b * 512, (b + 1) * 512)
            for m in range(8):
                bank = (b * 8 + m) % 4
                for c in range(8):
                    k.mm(self.ps[bank][:, :], self.wout[:, c, m * 128:(m + 1) * 128], self.mixT[:, c, bs], c == 0, c == 7,
                         [self.wout_t, self.mix_t[c][b]], [self.pt[bank]])
                k.stt(self.xT[:, m, bs], self.ps[bank][:, :], self.mods[:, l, cvi, 2, m:m + 1], self.xT[:, m, bs], ALU.mult, ALU.add,
                      [self.pt[bank], self.mods_t, self.xT_t[m][b]], [self.xT_t[m][b]])

    def ffn(self, grp, l, cvi):
        k, ar, dr = self.k, self.arena, self.dr
        S, L = (4, 256) if grp == "ctx" else (1, 1024)
        actT, actT_t = ar.alloc([NFT, NT], BF16, "actT")
        act_ts = [[T() for _ in range(2)] for _ in range(NFT)]
        for i in range(NFT):
            for b in range(2):
                act_ts[i][b].readers = list(actT_t.readers)
                ar.cur.append(act_ts[i][b])
        wup = [ar.alloc([8, 2, 128], BF16, "wup%d" % i) for i in range(2)]
        wdn = [ar.alloc([NFT, 128], BF16, "wdn%d" % i) for i in range(2)]
        ubf_ = [ar.alloc([2 * S * (L + 2)], BF16, "ubuf%d" % i) for i in range(2)]
        ub = [(a.rearrange("p (g s t) -> p g s t", g=2, s=S), t) for (a, t) in ubf_]
        dg = [ar.alloc([6, 128], BF16, "diag%d" % i) for i in range(2)]
        sg = [ar.alloc([512], BF16, "sgate%d" % i) for i in range(2)]
        for i in range(2):
            k.memset("pool", ubf_[i][0], 0.0, [ub[i][1]])
        upsrc = dr["w_up"][l].rearrange("(c p) n -> p c n", p=128)
        for i in range(NFT):
            w, w_t = wup[i % 2]
            u, u_t = ub[i % 2]
            d6, d6_t = dg[i % 2]
            k.dma("pool", w[:, :, 0, :], upsrc[:, :, i * 128:(i + 1) * 128], writes=[w_t])
            k.dma("pool", w[:, :, 1, :], upsrc[:, :, D_FF + i * 128:D_FF + (i + 1) * 128], writes=[w_t])
            for gu in range(2):
                tile_idx = gu * NFT + i
                for tap in range(3):
                    k.ts("pool", d6[:, gu * 3 + tap, :], self.ident[:], self.convw[:, l, tile_idx, tap:tap + 1], 1.0, ALU.mult, ALU.mult,
                         [self.ident_t, self.convw_t], [d6_t])
            for b in range(2):
                bs = slice(b * 512, (b + 1) * 512)
                for gu in range(2):
                    bank = b * 2 + gu
                    for c in range(8):
                        k.mm(self.ps[bank][:, :], w[:, c, gu, :], self.hT[:, c, bs], c == 0, c == 7, [w_t, self.hT_t[c][b]], [self.pt[bank]])
                    if S == 4:
                        dst = u[:, gu, 2 * b:2 * b + 2, 1:L + 1]
                        srcp = self.ps[bank][:, :].rearrange("p (s t) -> p s t", s=2)
                    else:
                        dst = u[:, gu, 0, 1 + b * 512:1 + (b + 1) * 512]
                        srcp = self.ps[bank][:, :]
                    k.copy("act" if gu == 0 else "dve", dst, srcp, [self.pt[bank]], [u_t])
            for b in range(2):
                bs = slice(b * 512, (b + 1) * 512)
                for gu in range(2):
                    bank = 4 + b * 2 + gu
                    for tap in range(3):
                        if S == 4:
                            rhs = u[:, gu, 2 * b:2 * b + 2, tap:tap + L]
                        else:
                            rhs = u[:, gu, 0, b * 512 + tap:b * 512 + tap + 512]
                        k.mm(self.ps[bank][:, :], d6[:, gu * 3 + tap, :], rhs, tap == 0, tap == 2, [d6_t, u_t], [self.pt[bank]])
                sgb, sgb_t = sg[b]
                k.act(sgb, self.ps[4 + b * 2][:, :], AF.Silu, [self.pt[4 + b * 2], self.convb_t], [sgb_t],
                      bias=self.convb[:, l, i:i + 1], scale=1.0)
                k.stt(actT[:, i, bs], self.ps[5 + b * 2][:, :], self.convb[:, l, NFT + i:NFT + i + 1], sgb, ALU.add, ALU.mult,
                      [self.pt[5 + b * 2], self.convb_t, sgb_t], [act_ts[i][b]])
        dnsrc = dr["w_down"][l].rearrange("(i p) n -> p i n", p=128)
        for m in range(8):
            w, w_t = wdn[m % 2]
            for i0 in range(0, NFT, 11):
                k.dma("pool", w[:, i0:i0 + 11, :], dnsrc[:, i0:i0 + 11, m * 128:(m + 1) * 128], writes=[w_t])
            for b in range(2):
                bs = slice(b * 512, (b + 1) * 512)
                bank = (m * 2 + b) % 4
                for i in range(NFT):
                    k.mm(self.ps[bank][:, :], w[:, i, :], actT[:, i, bs], i == 0, i == NFT - 1, [w_t, act_ts[i][b]], [self.pt[bank]])
                k.stt(self.xT[:, m, bs], self.ps[bank][:, :], self.mods[:, l, cvi, 5, m:m + 1], self.xT[:, m, bs], ALU.mult, ALU.add,
                      [self.pt[bank], self.mods_t, self.xT_t[m][b]], [self.xT_t[m][b]])
    def na_latent(self, l, QKT, QKT_t, Vn, Vn_t, Pbufs, rD):
        k, ar, dr = self.k, self.arena, self.dr
        NEG = -30000.0
        KTc, KTc_t = ar.alloc([2, P_LEN], BF16, "na_KTc")
        Vc, Vc_t = ar.alloc([4, 256], BF16, "na_Vc")
        stg, stg_t = ar.alloc([4, 19, 64], F32, "na_bias32")
        BT, BT_t = ar.alloc([4, 19, 64], BF16, "na_bias16")
        for h in range(4):
            k.dma("pool", KTc[(h % 2) * 64:(h % 2) * 64 + 64, h // 2, :], dr["c_na_kT"][l, h], writes=[KTc_t])
        for j in range(4):
            k.dma("pool", Vc[:, j, :].rearrange("p (h d) -> p h d", h=4),
                  dr["c_na_v"][l, :, j * 128:(j + 1) * 128, :].rearrange("h t d -> t h d"), writes=[Vc_t])
        rp = dr["rpb"]
        for h in range(4):
            base = rp[l, h].offset
            for half in range(2):
                src = bass.AP(rp.tensor, base + 16, [[1, 64], [159, 15], [1, 64]])
                k.dma("sp", stg[half * 64:(half + 1) * 64, h, 0:15, :], src, writes=[stg_t])
        for h in range(4):
            body = stg[:, h, 0:15, :]
            k.tt("dve", body, body, self.vmask[:, 0, :].unsqueeze(1).broadcast_to([128, 15, 64]), ALU.mult, [stg_t, self.vmask_t], [stg_t])
            k.tt("dve", body, body, self.vmask[:, 1, :].unsqueeze(1).broadcast_to([128, 15, 64]), ALU.add, [stg_t, self.vmask_t], [stg_t])
        k.memset("dve", stg[:, :, 15, :], NEG, [stg_t])
        k.memset("dve", stg[:, :, 18, :], NEG, [stg_t])
        k.copy("dve", stg[:, :, 16, :], stg[:, :, 3, :], [stg_t], [stg_t])
        k.copy("dve", stg[:, :, 17, :], stg[:, :, 10, :], [stg_t], [stg_t])
        k.copy("dve", BT.rearrange("p h r c -> p (h r c)"), stg.rearrange("p h r c -> p (h r c)"), [stg_t], [BT_t])
        units = []
        for qr in range(16):
            r0 = min(max(qr - 4, 0), 8)
            for h in range(4):
                rb0 = (h % 2) * 64
                rb = slice(rb0, rb0 + 64)
                idn = self.anti16[rb, rb0:rb0 + 64]
                chunks = []
                if r0 % 2 == 0:
                    for jj in range(4):
                        kr = r0 + 2 * jj
                        slot = kr - qr + 7
                        chunks.append((QKT[rb, 2 + h // 2, kr * 64:kr * 64 + 128], Vn[:, kr // 2, h * 64:(h + 1) * 64], [QKT_t, Vn_t],
                                       (BT[rb, h, slot:slot + 2, :].rearrange("p r c -> p (r c)"), idn, [BT_t, self.anti16_t])))
                else:
                    for jj in range(5):
                        kr = r0 - 1 + 2 * jj
                        slot = 15 if jj == 0 else (17 if jj == 4 else kr - qr + 7)
                        chunks.append((QKT[rb, 2 + h // 2, kr * 64:kr * 64 + 128], Vn[:, kr // 2, h * 64:(h + 1) * 64], [QKT_t, Vn_t],
                                       (BT[rb, h, slot:slot + 2, :].rearrange("p r c -> p (r c)"), idn, [BT_t, self.anti16_t])))
                for j in range(4):
                    chunks.append((KTc[rb, h // 2, j * 128:(j + 1) * 128], Vc[:, j, h * 64:(h + 1) * 64], [KTc_t, Vc_t], None))
                qs = slice(qr * 64, (qr + 1) * 64)
                units.append(dict(QT=QKT[rb, h // 2, qs], q_ts=[QKT_t], nq=64, chunks=chunks, rows=rb0,
                                  out=self.mixT[rb, h // 2, qs], out_ts=[self.mix_t[h // 2][qr // 8]]))
        self.attention(units, Pbufs, rD)

    def frac(self, out, x, shape_t, scr_t, eng="dve"):
        k = self.k
        M = 12582912.0
        t, t_t = scr_t
        k.ts(eng, t, x, M, None, ALU.add, ALU.bypass, shape_t, [t_t])
        k.ts(eng, t, t, -M, None, ALU.add, ALU.bypass, [t_t], [t_t])
        k.tt(eng, out, x, t, ALU.subtract, shape_t + [t_t], shape_t)

    def cmul(self, ore, oim, are, aim, bre, bim, t1, t2, reads, o_t, neg_im=False):
        k = self.k
        (t1a, t1_t), (t2a, t2_t) = t1, t2
        k.tt("dve", t1a, are, bre, ALU.mult, reads, [t1_t])
        k.tt("dve", t2a, aim, bim, ALU.mult, reads, [t2_t])
        k.tt("dve", ore, t1a, t2a, ALU.subtract, [t1_t, t2_t], [o_t])
        k.tt("dve", t1a, are, bim, ALU.mult, reads, [t1_t])
        k.tt("dve", t2a, aim, bre, ALU.mult, reads, [t2_t])
        if neg_im:
            k.tt("dve", t1a, t1a, t2a, ALU.add, [t1_t, t2_t], [t1_t])
            k.ts("dve", oim, t1a, -1.0, None, ALU.mult, ALU.bypass, [t1_t], [o_t])
        else:
            k.tt("dve", oim, t1a, t2a, ALU.add, [t1_t, t2_t], [o_t])

    def mixer_s5(self, grp, l):
        k, ar, dr = self.k, self.arena, self.dr
        lat = grp == "lat"
        S, L = (1, 1024) if lat else (4, 256)
        NKs = L // 8
        seg = NKs + 1
        ar.reset_to(self.mark_s5)
        win, win_t = self.wins["s5"]
        W1, W1_t = ar.alloc([16, 2, 128], BF16, "s5_W1")
        CA16, CA16_t = ar.alloc([16, 2, 128], BF16, "s5_CA16")
        Kt16, Kt16_t = ar.alloc([16, 128], BF16, "s5_Ktot16")
        Ec, Ec_t = ar.alloc([16, 129], F32, "s5_Ecos")
        Es, Es_t = ar.alloc([16, 129], F32, "s5_Esin")
        r8, r8_t = ar.alloc([16], F32, "s5_r8")
        h0, h0_t = ar.alloc([2, 2, 8], F32, "s5_h0")
        wglu, wglu_t = ar.alloc([2, 256], BF16, "s5_wglu")
        k.dma("pool", wglu, dr["w_glu"][l].rearrange("(c p) n -> p c n", p=128), writes=[wglu_t])
        if lat:
            k.dma("sp", h0, dr["s5_h0"][l], writes=[h0_t])
        mark = (ar.off, len(ar.cur))
        lam, lam_t = ar.alloc([2, 16], F32, "lam")
        ldt, ldt_t = ar.alloc([16], F32, "ldt")
        Bm, Bm_t = ar.alloc([2, 16, 16], F32, "Bm")
        Cm, Cm_t = ar.alloc([2, 16, 16], F32, "Cm")
        dsk, dsk_t = ar.alloc([16], F32, "dsk")
        k.dma("sp", lam, dr["s5_lam"][l], writes=[lam_t])
        k.dma("sp", ldt, dr["s5_logdt"][l], writes=[ldt_t])
        k.dma("sp", Bm, dr["s5_B"][l], writes=[Bm_t])
        k.dma("sp", Cm, dr["s5_C"][l], writes=[Cm_t])
        k.dma("sp", dsk, dr["s5_dsk"][l], writes=[dsk_t])
        sm = {n: ar.alloc([16], F32, "s5_" + n) for n in ("dt", "lrdt", "y1", "y8", "den", "fre", "fim", "ta", "tb", "tc")}
        big = {n: ar.alloc([16, 17], F32, "s5_" + n) for n in ("pm", "ang", "t", "a2", "Pre", "Pim")}
        dt, dt_t = sm["dt"]
        k.act(dt, ldt, AF.Exp, [ldt_t], [dt_t])
        lrdt, lrdt_t = sm["lrdt"]
        k.tt("dve", lrdt, lam[:, 0, :], dt, ALU.mult, [lam_t, dt_t], [lrdt_t])
        y1, y1_t = sm["y1"]
        k.tt("dve", y1, lam[:, 1, :], dt, ALU.mult, [lam_t, dt_t], [y1_t])
        k.ts("dve", y1, y1, 1.0 / TWO_PI, None, ALU.mult, ALU.bypass, [y1_t], [y1_t])
        self.frac(y1, y1, [y1_t], sm["ta"])
        bc17 = lambda a: a.unsqueeze(2).broadcast_to([128, 16, 17])
        pvb = self.pvals[:].unsqueeze(1).broadcast_to([128, 16, 17])
        pm, pm_t = big["pm"]
        ang, ang_t = big["ang"]
        a2, a2_t = big["a2"]
        Pre, Pre_t = big["Pre"]
        Pim, Pim_t = big["Pim"]
        k.tt("dve", pm, bc17(lrdt), pvb, ALU.mult, [lrdt_t, self.pvals_t], [pm_t])
        k.act(pm, pm, AF.Exp, [pm_t], [pm_t])
        k.tt("dve", ang, bc17(y1), pvb, ALU.mult, [y1_t, self.pvals_t], [ang_t])
        k.ts("dve", a2, ang, 0.25, None, ALU.add, ALU.bypass, [ang_t], [a2_t])
        self.frac(ang, ang, [ang_t], big["t"])
        self.frac(a2, a2, [a2_t], big["t"])
        k.act(ang, ang, AF.Sin, [ang_t], [ang_t], scale=TWO_PI)
        k.act(a2, a2, AF.Sin, [a2_t], [a2_t], scale=TWO_PI)
        k.tt("dve", Pre, pm, a2, ALU.mult, [pm_t, a2_t], [Pre_t])
        k.tt("dve", Pim, pm, ang, ALU.mult, [pm_t, ang_t], [Pim_t])
        import os
        pst_ = int(os.environ.get('KS5P', '9'))
        if pst_ < 1:
            return
        den, den_t = sm["den"]
        ta, ta_t = sm["ta"]
        tb, tb_t = sm["tb"]
        tc, tc_t = sm["tc"]
        fre, fre_t = sm["fre"]
        fim, fim_t = sm["fim"]
        lr, li = lam[:, 0, :], lam[:, 1, :]
        k.tt("dve", den, lr, lr, ALU.mult, [lam_t], [den_t])
        k.tt("dve", ta, li, li, ALU.mult, [lam_t], [ta_t])
        k.tt("dve", den, den, ta, ALU.add, [den_t, ta_t], [den_t])
        k.op("dve", lambda h: h.reciprocal(out=den, in_=den), [den_t], [den_t])
        k.ts("dve", tc, Pre[:, :, 9], -1.0, None, ALU.add, ALU.bypass, [Pre_t], [tc_t])
        k.tt("dve", ta, tc, lr, ALU.mult, [tc_t, lam_t], [ta_t])
        k.tt("dve", tb, Pim[:, :, 9], li, ALU.mult, [Pim_t, lam_t], [tb_t])
        k.tt("dve", ta, ta, tb, ALU.add, [ta_t, tb_t], [ta_t])
        k.tt("dve", fre, ta, den, ALU.mult, [ta_t, den_t], [fre_t])
        k.tt("dve", ta, Pim[:, :, 9], lr, ALU.mult, [Pim_t, lam_t], [ta_t])
        k.tt("dve", tb, tc, li, ALU.mult, [tc_t, lam_t], [tb_t])
        k.tt("dve", ta, ta, tb, ALU.subtract, [ta_t, tb_t], [ta_t])
        k.tt("dve", fim, ta, den, ALU.mult, [ta_t, den_t], [fim_t])
        Bbr, Bbr_t = ar.alloc([16, 16], F32, "Bbr")
        Bbi, Bbi_t = ar.alloc([16, 16], F32, "Bbi")
        c1 = ar.alloc([8, 8, 16], F32, "s5_c1")
        c2 = ar.alloc([8, 8, 16], F32, "s5_c2")
        bc16 = lambda a: a.unsqueeze(2).broadcast_to([128, 16, 16])
        c1v = (c1[0].rearrange("p a b c -> p (a b c)")[:, 0:256].rearrange("p (a b) -> p a b", a=16), c1[1])
        c2v = (c2[0].rearrange("p a b c -> p (a b c)")[:, 0:256].rearrange("p (a b) -> p a b", a=16), c2[1])
        self.cmul(Bbr, Bbi, bc16(fre), bc16(fim), Bm[:, 0], Bm[:, 1], c1v, c2v, [fre_t, fim_t, Bm_t], Bbr_t)
        Bbi_t.writer = Bbr_t.writer
        y8, y8_t = sm["y8"]
        k.ts("dve", y8, y1, 8.0, None, ALU.mult, ALU.bypass, [y1_t], [y8_t])
        self.frac(y8, y8, [y8_t], sm["ta"])
        k.ts("dve", r8, lrdt, 8.0, None, ALU.mult, ALU.bypass, [lrdt_t], [r8_t])
        k.act(r8, r8, AF.Exp, [r8_t], [r8_t])
        et = ar.alloc([16, 129], F32, "s5_et")
        posb = self.posv[:].unsqueeze(1).broadcast_to([128, 16, 129])
        k.tt("dve", Es, y8.unsqueeze(2).broadcast_to([128, 16, 129]), posb, ALU.mult, [y8_t, self.posv_t], [Es_t])
        k.ts("dve", Ec, Es, 0.25, None, ALU.add, ALU.bypass, [Es_t], [Ec_t])
        self.frac(Es, Es, [Es_t], et)
        self.frac(Ec, Ec, [Ec_t], et)
        k.act(Es, Es, AF.Sin, [Es_t], [Es_t], scale=TWO_PI)
        k.act(Ec, Ec, AF.Sin, [Ec_t], [Ec_t], scale=TWO_PI)
        if pst_ < 2:
            return
        Kacc, Kacc_t = ar.alloc([16, 128], F32, "s5_Kacc")
        k.tt("dve", Kacc, self.ident[:].unsqueeze(1).broadcast_to([128, 16, 128]), dsk.unsqueeze(2).broadcast_to([128, 16, 128]),
             ALU.mult, [self.ident_t, dsk_t], [Kacc_t])
        XAr, XAr_t = ar.alloc([8, 8, 16], F32, "s5_XAr")
        XAi, XAi_t = ar.alloc([8, 8, 16], F32, "s5_XAi")
        tmpK, tmpK_t = ar.alloc([4, 128], F32, "s5_tmpK")
        xb16 = [ar.alloc([8, 128], BF16, "s5_xb16_%d" % i) for i in range(3)]
        fl = lambda a: a.rearrange("p t s c -> p t (s c)")
        bank_i = 0
        for d in range(2):
            tl = slice(d * 8, d * 8 + 8)
            if d == 0:
                p1, p2, pc = slice(15, 7, -1), slice(7, None, -1), slice(9, 17)
            else:
                p1, p2, pc = slice(8, 16), slice(0, 8), slice(16, 8, -1)
            pw = lambda P_, s_: P_[:, tl, s_].unsqueeze(3).broadcast_to([128, 8, 8, 16])
            bb = lambda a: a[:, tl, :].unsqueeze(2).broadcast_to([128, 8, 8, 16])
            rd_ = [Pre_t, Pim_t, Bbr_t, Cm_t]
            self.cmul(XAr, XAi, pw(Pre, pc), pw(Pim, pc), bb(Cm[:, 0]), bb(Cm[:, 1]), c1, c2, rd_, XAr_t, neg_im=True)
            XAi_t.writer = XAr_t.writer
            k.copy("dve", CA16[:, tl, 0, :], fl(XAr), [XAr_t], [CA16_t])
            k.copy("dve", CA16[:, tl, 1, :], fl(XAi), [XAi_t, XAr_t], [CA16_t])
            if pst_ < 3:
                continue
            x1_16, x1_16_t = xb16[0]
            self.cmul(XAr, XAi, pw(Pre, p1), pw(Pim, p1), bb(Bbr), bb(Bbi), c1, c2, rd_ + [CA16_t], XAr_t)
            XAi_t.writer = XAr_t.writer
            q_ = int(os.environ.get('KS5Q', '9'))
            for ri, (xa, xa_t) in enumerate(((XAr, XAr_t), (XAi, XAi_t))):
                if q_ < 1:
                    continue
                k.copy("dve", x1_16, fl(xa), [xa_t, XAr_t], [x1_16_t])
                if q_ < 2:
                    continue
                for t4 in range(2):
                    bank = bank_i % 4
                    bank_i += 1
                    pTw = self.ps[bank][:].bitcast(BF16)
                    for i in range(4):
                        ti = t4 * 4 + i
                        k.tr(pTw[:, i * 128:(i + 1) * 128], x1_16[:, ti, :], self.ident16[:], [x1_16_t, self.ident16_t], [self.pt[bank]])
                    if q_ < 3:
                        continue
                    k.copy("dve", W1[:, d * 8 + t4 * 4:d * 8 + t4 * 4 + 4, ri, :], pTw[:, 0:512].rearrange("p (i n) -> p i n", i=4),
                           [self.pt[bank]], [W1_t])
            if pst_ < 4:
                continue
            kk_ = int(os.environ.get('KS5K', '9'))
            if kk_ < 1:
                continue
            self.cmul(XAr, XAi, pw(Pre, p2), pw(Pim, p2), bb(Bbr), bb(Bbi), c1, c2, rd_ + [x1_16_t], XAr_t)
            XAi_t.writer = XAr_t.writer
            x2r16, x2r_t = xb16[1]
            x2i16, x2i_t = xb16[2]
            k.copy("dve", x2r16, fl(XAr), [XAr_t], [x2r_t])
            k.copy("dve", x2i16, fl(XAi), [XAi_t, XAr_t], [x2i_t])
            Kacc4 = Kacc.rearrange("p (gp g2) n -> p gp g2 n", g2=2)
            for g2 in range(2):
                rr = slice(g2 * 64, g2 * 64 + 64)
                for gp4 in range(2):
                    bank = bank_i % 4
                    bank_i += 1
                    for i in range(4):
                        gp = gp4 * 4 + i
                        k.mm(self.ps[bank][:, i * 128:(i + 1) * 128], x2r16[rr, gp, :], CA16[rr, d * 8 + gp, 0, :], True, False,
                             [x2r_t, CA16_t], [self.pt[bank]])
                        k.mm(self.ps[bank][:, i * 128:(i + 1) * 128], x2i16[rr, gp, :], CA16[rr, d * 8 + gp, 1, :], False, True,
                             [x2i_t, CA16_t], [self.pt[bank]])
                    k.tt("dve", tmpK, self.ps[bank][:, :].rearrange("p (g n) -> p g n", g=4),
                         self.masks[:, d, :].unsqueeze(1).broadcast_to([128, 4, 128]), ALU.mult, [self.pt[bank], self.masks_t], [tmpK_t])
                    kv_ = Kacc4[:, gp4 * 4:gp4 * 4 + 4, g2, :]
                    k.tt("dve", kv_, kv_, tmpK, ALU.add, [tmpK_t, Kacc_t], [Kacc_t])
        k.copy("act", Kt16, Kacc, [Kacc_t], [Kt16_t])
        import os
        stg_ = int(os.environ.get('KS5', '9'))
        evs = []
        for t in ar.cur[mark[1]:]:
            if t.writer is not None:
                evs.append(t.writer)
            evs.extend(t.readers)
        ar.pending = list(ar.pending) + evs
        ar.cur = ar.cur[:mark[1]]
        ar.off = mark[0]
        ubf, ubf_t = ar.alloc([2, NT], BF16, "s5_ubf")
        U8, U8_t = ar.alloc([16, 128], BF16, "s5_U8")
        Hp, Hp_t = ar.alloc([16, 2, 128], BF16, "s5_Hprev")
        y8b, y8b_t = ar.alloc([16, 128], BF16, "s5_y8")
        qre, qre_t = ar.alloc([4, S, seg], F32, "s5_qre")
        qim, qim_t = ar.alloc([4, S, seg], F32, "s5_qim")
        sre, sre_t = ar.alloc([4, S, seg], F32, "s5_sre")
        sim, sim_t = ar.alloc([4, S, seg], F32, "s5_sim")
        Rc, Rc_t = ar.alloc([4, S, seg], F32, "s5_R")
        hre, hre_t = ar.alloc([4, S, seg], F32, "s5_hre")
        him, him_t = ar.alloc([4, S, seg], F32, "s5_him")
        w1, w1_t = ar.alloc([4, S, seg], F32, "s5_w1")
        w2, w2_t = ar.alloc([4, S, seg], F32, "s5_w2")
        Fin, Fin_t = ar.alloc([4, 2, 2, 8], F32, "s5_fin")
        FinT, FinT_t = ar.alloc([128], F32, "s5_finT")
        y32, y32_t = ar.alloc([2, NT], F32, "s5_y32")
        g1, g1_t = ar.alloc([2, NT], F32, "s5_g1")
        yg, yg_t = ar.alloc([2, NT], BF16, "s5_yg")
        sgl, sgl_t = ar.alloc([512], BF16, "s5_sgl")
        if stg_ < 1:
            return
        for ft in range(2):
            for b in range(2):
                bs = slice(b * 512, (b + 1) * 512)
                bank = ft * 2 + b
                for c in range(8):
                    k.mm(self.ps[bank][:, :], win[:, c, ft * 128:(ft + 1) * 128], self.hT[:, c, bs], c == 0, c == 7,
                         [win_t, self.hT_t[c][b]], [self.pt[bank]])
                k.copy("act", ubf[:, ft, bs], self.ps[bank][:, :], [self.pt[bank]], [ubf_t])
        U8v = U8.rearrange("p (ft q par) n -> p ft q par n", ft=2, q=4)
        sp_banks = (4, 5, 6, 7)
        for ft in range(2):
            for par in range(2):
                gi = ft * 2 + par
                for j in range(8):
                    for q4 in range(4):
                        bank = sp_banks[q4]
                        rr = slice(32 * q4, 32 * q4 + 32)
                        k.mm(self.ps[bank][:, gi * 128:(gi + 1) * 128], self.sel[rr, par, j, :], ubf[rr, ft, j:NT:8], j == 0, j == 7,
                             [self.sel_t, ubf_t], [self.pt[bank]], tile_position=(32 * q4, 0))
        for q4 in range(4):
            bank = sp_banks[q4]
            k.copy("dve", U8v[:, :, q4, :, :], self.ps[bank][:, :].rearrange("p (ft par n) -> p ft par n", ft=2, par=2), [self.pt[bank]], [U8_t])
        if stg_ < 2:
            return
        flat = lambda a: a.rearrange("p i s k -> p (i s k)")
        for rnd in range(4):
            d, half = rnd // 2, rnd % 2
            t0 = d * 8 + half * 4
            tl = slice(t0, t0 + 4)
            bre, bim = 0, 1
            for i in range(4):
                gp = half * 4 + i
                for g2 in range(2):
                    g = 2 * gp + g2
                    orow = slice(g2 * 64, g2 * 64 + 64)
                    k.mm(self.ps[bre][orow, i * 128:(i + 1) * 128], W1[:, t0 + i, 0, g2 * 64:g2 * 64 + 64], U8[:, g, :], True, True,
                         [W1_t, U8_t], [self.pt[bre]])
                    k.mm(self.ps[bim][orow, i * 128:(i + 1) * 128], W1[:, t0 + i, 1, g2 * 64:g2 * 64 + 64], U8[:, g, :], True, True,
                         [W1_t, U8_t], [self.pt[bim]])
            Sre = self.ps[bre][:, :].rearrange("p (i s k) -> p i s k", i=4, s=S)
            Sim = self.ps[bim][:, :].rearrange("p (i s k) -> p i s k", i=4, s=S)
            if d == 1:
                Sre = Sre[:, :, :, ::-1]
                Sim = Sim[:, :, :, ::-1]
            cosv = Ec[:, tl, 1:seg].unsqueeze(2).broadcast_to([128, 4, S, NKs])
            sinv = Es[:, tl, 1:seg].unsqueeze(2).broadcast_to([128, 4, S, NKs])
            body = lambda a: a[:, :, :, 1:seg]
            if lat:
                k.copy("dve", qre[:, :, 0, 0], h0[:, d, 0, half * 4:half * 4 + 4], [h0_t], [qre_t])
                k.copy("dve", qim[:, :, 0, 0], h0[:, d, 1, half * 4:half * 4 + 4], [h0_t], [qim_t])
            else:
                k.memset("dve", qre[:, :, :, 0], 0.0, [qre_t])
                k.memset("dve", qim[:, :, :, 0], 0.0, [qim_t])
            k.copy("dve", flat(Rc).rearrange("p (i x) -> p i x", i=4), r8[:, tl].unsqueeze(2).broadcast_to([128, 4, S * seg]), [r8_t], [Rc_t])
            k.memset("dve", Rc[:, :, :, 0], 0.0, [Rc_t])
            k.tt("dve", body(w1), Sre, cosv, ALU.mult, [self.pt[bre], Ec_t], [w1_t])
            k.tt("dve", body(w2), Sim, sinv, ALU.mult, [self.pt[bim], Es_t], [w2_t])
            k.tt("dve", body(qre), body(w1), body(w2), ALU.add, [w1_t, w2_t], [qre_t])
            k.tt("dve", body(w1), Sim, cosv, ALU.mult, [self.pt[bim], Ec_t], [w1_t])
            k.tt("dve", body(w2), Sre, sinv, ALU.mult, [self.pt[bre], Es_t], [w2_t])
            k.tt("dve", body(qim), body(w1), body(w2), ALU.subtract, [w1_t, w2_t], [qim_t])
            k.op("dve", lambda h: h.tensor_tensor_scan(out=flat(sre), data0=flat(Rc), data1=flat(qre), initial=0.0, op0=ALU.mult, op1=ALU.add),
                 [Rc_t, qre_t], [sre_t])
            k.op("dve", lambda h: h.tensor_tensor_scan(out=flat(sim), data0=flat(Rc), data1=flat(qim), initial=0.0, op0=ALU.mult, op1=ALU.add),
                 [Rc_t, qim_t], [sim_t])
            cosa = Ec[:, tl, 0:seg].unsqueeze(2).broadcast_to([128, 4, S, seg])
            sina = Es[:, tl, 0:seg].unsqueeze(2).broadcast_to([128, 4, S, seg])
            k.tt("dve", w1, sre, cosa, ALU.mult, [sre_t, Ec_t], [w1_t])
            k.tt("dve", w2, sim, sina, ALU.mult, [sim_t, Es_t], [w2_t])
            k.tt("dve", hre, w1, w2, ALU.subtract, [w1_t, w2_t], [hre_t])
            k.tt("dve", w1, sre, sina, ALU.mult, [sre_t, Es_t], [w1_t])
            k.tt("dve", w2, sim, cosa, ALU.mult, [sim_t, Ec_t], [w2_t])
            k.tt("dve", him, w1, w2, ALU.add, [w1_t, w2_t], [him_t])
            for ri, (hh, hh_t) in enumerate(((hre, hre_t), (him, him_t))):
                srcv = hh[:, :, :, 0:NKs] if d == 0 else hh[:, :, :, NKs - 1::-1]
                k.copy("act", Hp[:, tl, ri, :].rearrange("p i (s k) -> p i s k", s=S), srcv, [hh_t], [Hp_t])
                if not lat:
                    k.copy("dve", Fin[:, :, d, ri, half * 4:half * 4 + 4].rearrange("p s i -> p i s"), hh[:, :, :, NKs], [hh_t], [Fin_t])
        if stg_ < 3:
            return
        y8v = y8b.rearrange("p (gp g2) n -> p gp g2 n", g2=2)
        bi_ = 0
        for g2 in range(2):
            rr = slice(g2 * 64, g2 * 64 + 64)
            for gp4 in range(2):
                bank = 2 + bi_ % 2
                bi_ += 1
                for i in range(4):
                    gp = gp4 * 4 + i
                    g = 2 * gp + g2
                    osl = self.ps[bank][:, i * 128:(i + 1) * 128]
                    k.mm(osl, Kt16[:, g, :], U8[:, g, :], True, False, [Kt16_t, U8_t], [self.pt[bank]])
                    for d in range(2):
                        ti = d * 8 + gp
                        k.mm(osl, CA16[rr, ti, 0, :], Hp[rr, ti, 0, :], False, False, [CA16_t, Hp_t], [self.pt[bank]])
                        k.mm(osl, CA16[rr, ti, 1, :], Hp[rr, ti, 1, :], False, d == 1, [CA16_t, Hp_t], [self.pt[bank]])
                k.copy("act", y8v[:, gp4 * 4:gp4 * 4 + 4, g2, :], self.ps[bank][:, :].rearrange("p (g n) -> p g n", g=4), [self.pt[bank]], [y8b_t])
        if stg_ < 4:
            return
        for ft in range(2):
            for jh in range(2):
                bank = 4 + (ft * 2 + jh) % 2
                for jj in range(4):
                    j = jh * 4 + jj
                    for q4 in range(4):
                        for par in range(2):
                            g = ft * 8 + q4 * 2 + par
                            k.mm(self.ps[bank][32 * q4:32 * q4 + 32, jj * 128:(jj + 1) * 128], self.selT[:, par, j, :], y8b[:, g, :],
                                 par == 0, par == 1, [self.selT_t, y8b_t], [self.pt[bank]], tile_position=(0, 32 * q4))
                k.copy("act", y32[:, ft, :].rearrange("p (k j) -> p j k", j=8)[:, jh * 4:jh * 4 + 4, :],
                       self.ps[bank][:, :].rearrange("p (j k) -> p j k", j=4), [self.pt[bank]], [y32_t])
        if stg_ < 5:
            return
        k.tt("dve", g1, y32, y32, ALU.mult, [y32_t], [g1_t])
        k.ts("dve", g1, g1, 0.044715, 1.0, ALU.mult, ALU.add, [g1_t], [g1_t])
        k.tt("dve", g1, g1, y32, ALU.mult, [g1_t, y32_t], [g1_t])
        k.act(g1, g1, AF.Sigmoid, [g1_t], [g1_t], scale=1.5957691216057308)
        k.tt("dve", yg, g1, y32, ALU.mult, [g1_t, y32_t], [yg_t])
        for mt in range(2):
            for b in range(2):
                bs = slice(b * 512, (b + 1) * 512)
                bank = mt * 2 + b
                for kc in range(2):
                    k.mm(self.ps[bank][:, :], wglu[:, kc, mt * 128:(mt + 1) * 128], yg[:, kc, bs], kc == 0, kc == 1, [wglu_t, yg_t], [self.pt[bank]])
                k.act(sgl, self.ps[bank][:, :], AF.Sigmoid, [self.pt[bank], self.bglu_t], [sgl_t], bias=self.bglu[:, l, mt:mt + 1], scale=1.0)
                k.tt("dve", self.mixT[:, 2 + mt, bs], yg[:, mt, bs], sgl, ALU.mult, [yg_t, sgl_t], [self.mix_t[2 + mt][b]])
        if stg_ < 6:
            return
        if not lat:
            for s_ in range(4):
                k.dma("sp", self.do["o_s5"][s_, l].rearrange("d r (gp g2) n -> (g2 n) d r gp", g2=2), Fin[:, s_, :, :, :],
                      reads=[Fin_t], final=True, allow_slow_non_contiguous=True)

    def layer(self, grp, l):
        cvi = 0 if grp == "ctx" else 1
        self.gains_bc(l)
        self.arena.reset()
        self.wins = {}
        self.wins["s5"] = self.load_win(l, 768, 1024, "win_s5")
        self.mark_s5 = self.arena.mark()
        self.wins["na"] = self.load_win(l, 0, 768, "win_na")
        self.wins["gq"] = self.load_win(l, 1024, 1536, "win_gq")
        self.wins["mla"] = self.load_win(l, 1536, 1888, "win_mla")
        self.mark_all = self.arena.mark()
        wsrc = self.dr["w_out"][l].rearrange("(c p) n -> p c n", p=128)
        for c in range(0, 8, 2):
            self.k.dma("pool", self.wout[:, c:c + 2, :], wsrc[:, c:c + 2, :], writes=[self.wout_t])
        import os
        ph = os.environ.get("KPH", "norm,na,gq,mla,s5,out,ffn").split(",")
        if "norm" in ph:
            self.norm_mod(l, cvi, 0)
        if "na" in ph:
            self.mixer_na(grp, l)
        if "gq" in ph:
            self.mixer_gq(grp, l)
        if "mla" in ph:
            self.mixer_mla(grp, l)
        if "s5" in ph:
            self.mixer_s5(grp, l)
        if "out" in ph:
            self.out_proj(l, cvi)
        self.arena.reset()
        if "ffn" in ph:
            self.norm_mod(l, cvi, 1)
            self.ffn(grp, l, cvi)


_PROG_CACHE = {}


def _get_prog(shapes, groups, nlayers):
    key = (tuple(sorted((n, tuple(s)) for n, s in shapes.items())), tuple(groups), nlayers)
    if key not in _PROG_CACHE:
        _PROG_CACHE[key] = Prog(shapes, groups=groups, nlayers=nlayers)
    return _PROG_CACHE[key]


def _run(inputs, groups=("ctx", "lat"), nlayers=2):
    common = _common_inputs(inputs)
    in_maps = []
    for core in range(NCORES):
        m = dict(common)
        m.update(_core_inputs(inputs, core))
        in_maps.append(m)
    shapes = {n: a.shape for n, a in in_maps[0].items()}
    prog = _get_prog(shapes, groups, nlayers)
    res = run_bass_kernel_spmd(prog.nc, in_maps, core_ids=list(range(NCORES)))
    return res.results


def kernel(**inputs):
    r = _run(inputs)

    def unT(a):
        return np.ascontiguousarray(np.asarray(a).transpose(2, 1, 0).reshape(NT, D))
    y_prompt = np.concatenate([unT(r[c]["yT_p"]).reshape(4, 256, D) for c in range(NCORES)], axis=0).astype(np.float32)
    y_sample = np.stack([unT(r[2 * b]["yT_s"]) for b in range(4)], axis=0).astype(np.float32)
    cat = lambda n: np.concatenate([np.asarray(r[c][n]) for c in range(NCORES)], axis=0).astype(np.float32)
    return (y_prompt, y_sample, cat("o_na_k"), cat("o_na_v"), cat("o_s5"), cat("o_gq_k"), cat("o_gq_v"), cat("o_ckv"), cat("o_krope"))
```
